# Optimizing a Trainium2 kernel written in Bass

```python
import math
import jax, jax.numpy as jnp
from jax import lax
import numpy as np

D_MODEL = 2048
BATCH = 4
SEQ = 2048
DEPTH = 4
DEC_BATCH = 8
DEC_SEQ = 1
PAST_LEN = 16384
PAGE_SIZE = 128

HEAD_DIM = 64
D_MIX = D_MODEL
D_ATTN = D_MIX // 2
D_CONV = D_MIX - D_ATTN
N_HEADS = D_ATTN // HEAD_DIM
CONV_GROUPS = D_CONV // HEAD_DIM
CONV_WIDTH = 3
D_FF = 4 * D_MODEL
DILATED_CONFIGS = ((128, 1), (512, 4), (2048, 16))
ATTN_WINDOW = max(w for w, _ in DILATED_CONFIGS)
N_BUCKETS = 32
MAX_DISTANCE = 2048
Q_BLOCK = 128
EPS = 1e-6
ATTN_SCALE = HEAD_DIM ** -0.5
D_IN = 3 * D_ATTN + 3 * D_CONV

kernel_name = "hymba_dilated_swa_shortconv_decoder_step"


def _rmsnorm(x, g):
    xf = x.astype(jnp.float32)
    y = xf * lax.rsqrt(jnp.mean(xf * xf, axis=-1, keepdims=True) + EPS)
    return (y * g.astype(jnp.float32)).astype(x.dtype)


def _t5_bucket(dist):
    n_exact = N_BUCKETS // 2
    large = n_exact + (np.log(np.maximum(dist, 1) / n_exact) / np.log(MAX_DISTANCE / n_exact)
                       * (N_BUCKETS - n_exact)).astype(np.int32)
    large = np.minimum(large, N_BUCKETS - 1)
    return np.where(dist < n_exact, dist, large).astype(np.int32)


def _branch_biases(rel_bias):
    return [rel_bias[_t5_bucket(np.arange(w // d + 1) * d)] for w, d in DILATED_CONFIGS]


def _dilated_branch_prompt(q, k, v, bias_k, dil):
    b, s, h, e = q.shape
    nw = bias_k.shape[0] - 1
    L = s // dil
    bq = math.gcd(L, Q_BLOCK)
    nblk = L // bq
    pad = ((0, 0), (nw, 0), (0, 0), (0, 0), (0, 0))
    qb = q.reshape(b, nblk, bq, dil, h, e)
    kp = jnp.pad(k.reshape(b, L, dil, h, e), pad)
    vp = jnp.pad(v.reshape(b, L, dil, h, e), pad)
    idx = np.arange(nblk)[:, None] * bq + np.arange(bq + nw)[None, :]
    kb = kp[:, idx]
    vb = vp[:, idx]
    logits = jnp.einsum('bnqrhe,bnkrhe->bnrhqk', qb, kb).astype(jnp.float32) * ATTN_SCALE
    rel = np.arange(bq)[:, None] - np.arange(bq + nw)[None, :] + nw
    valid = ((rel >= 0) & (rel <= nw))[None] & ((idx - nw) >= 0)[:, None, :]
    bias = bias_k[np.clip(rel, 0, nw)].transpose(2, 0, 1).astype(jnp.float32)
    logits = jnp.where(valid[None, :, None, None], logits + bias[None, None, None], -jnp.inf)
    m = jnp.max(logits, axis=-1)
    p = jnp.exp(logits - m[..., None])
    den = jnp.sum(p, axis=-1)
    o = jnp.einsum('bnrhqk,bnkrhe->bnqrhe', p, vb.astype(jnp.float32)).reshape(b, s, h, e)
    m = m.transpose(0, 1, 4, 2, 3).reshape(b, s, h)
    den = den.transpose(0, 1, 4, 2, 3).reshape(b, s, h)
    return o, m, den


def _dilated_branch_sample(q, kcat, vcat, bias_k, dil):
    b, t, h, e = q.shape
    wb = kcat.shape[1] - t
    nk = bias_k.shape[0]
    j = wb + np.arange(t)[:, None] - np.arange(nk)[None, :] * dil
    valid = j >= 0
    jc = np.maximum(j, 0)
    kg = kcat[:, jc]
    vg = vcat[:, jc]
    logits = jnp.einsum('bthe,btkhe->bhtk', q, kg).astype(jnp.float32) * ATTN_SCALE
    logits = logits + bias_k.T.astype(jnp.float32)[None, :, None, :]
    logits = jnp.where(valid[None, None], logits, -jnp.inf)
    m = jnp.max(logits, axis=-1)
    p = jnp.exp(logits - m[..., None])
    den = jnp.sum(p, axis=-1)
    o = jnp.einsum('bhtk,btkhe->bthe', p, vg.astype(jnp.float32))
    return o, m.transpose(0, 2, 1), den.transpose(0, 2, 1)


def _combine_branches(branches):
    ms = jnp.stack([br[1] for br in branches], axis=0)
    mmax = jnp.max(ms, axis=0)
    num = 0.0
    den = 0.0
    for o_c, m_c, d_c in branches:
        w = jnp.exp(m_c - mmax)
        num = num + w[..., None] * o_c
        den = den + w * d_c
    return num / den[..., None]


def _mixer_inputs(x, g_norm, w_in, q_g, k_g):
    b, s, _ = x.shape
    h = _rmsnorm(x, g_norm)
    proj = jnp.einsum('bsd,dc->bsc', h, w_in)
    cuts = [D_ATTN, 2 * D_ATTN, 3 * D_ATTN, 3 * D_ATTN + D_CONV, 3 * D_ATTN + 2 * D_CONV]
    q, k, v, vc, gate_b, gate_c = jnp.split(proj, cuts, axis=-1)
    q = _rmsnorm(q.reshape(b, s, N_HEADS, HEAD_DIM), q_g)
    k = _rmsnorm(k.reshape(b, s, N_HEADS, HEAD_DIM), k_g)
    v = v.reshape(b, s, N_HEADS, HEAD_DIM)
    u = gate_c * vc
    return q, k, v, u, gate_b


def _depthwise_conv(ucat, w, t):
    y = w[0] * ucat[:, 0:t]
    for i in range(1, CONV_WIDTH):
        y = y + w[i] * ucat[:, i:i + t]
    return y


def _mixer_output(o_attn, y_conv, gate_b, attn_g, conv_g, w_out):
    b, s = o_attn.shape[:2]
    oa = _rmsnorm(o_attn.reshape(b, s, D_ATTN).astype(y_conv.dtype), attn_g)
    oc = _rmsnorm(gate_b * y_conv, conv_g)
    return jnp.einsum('bsc,cd->bsd', jnp.concatenate([oa, oc], axis=-1), w_out)


def _mlp(x, g, w_up, w_down):
    h = _rmsnorm(x, g)
    a = jnp.square(jax.nn.relu(jnp.einsum('bsd,df->bsf', h, w_up)))
    return jnp.einsum('bsf,fd->bsd', a, w_down)


def setup_inputs(seed: int = 0) -> dict:
    key = jax.random.key(seed)
    ks = jax.random.split(key, 17)
    wb = min(ATTN_WINDOW, PAST_LEN)
    f32 = jnp.float32

    def nrm(k, shape, scale=1.0):
        return jax.random.normal(k, shape, f32) * scale

    return {
        "x_prompt": nrm(ks[0], (BATCH, SEQ, D_MODEL)),
        "x_sample": nrm(ks[1], (DEC_BATCH, DEC_SEQ, D_MODEL)),
        "state_attn_k": nrm(ks[2], (DEPTH, DEC_BATCH, wb, N_HEADS, HEAD_DIM)),
        "state_attn_v": nrm(ks[3], (DEPTH, DEC_BATCH, wb, N_HEADS, HEAD_DIM)),
        "state_conv": nrm(ks[4], (DEPTH, DEC_BATCH, CONV_WIDTH - 1, D_CONV)),
        "rel_bias": nrm(ks[5], (N_BUCKETS, N_HEADS), 0.5),
        "norm_mix": 1.0 + nrm(ks[6], (DEPTH, D_MODEL), 0.05),
        "w_in": nrm(ks[7], (DEPTH, D_MODEL, D_IN), D_MODEL ** -0.5),
        "q_norm": 1.0 + nrm(ks[8], (DEPTH, HEAD_DIM), 0.05),
        "k_norm": 1.0 + nrm(ks[9], (DEPTH, HEAD_DIM), 0.05),
        "conv_w": nrm(ks[10], (DEPTH, CONV_WIDTH, D_CONV), CONV_WIDTH ** -0.5),
        "attn_out_norm": 1.0 + nrm(ks[11], (DEPTH, D_ATTN), 0.05),
        "conv_out_norm": 1.0 + nrm(ks[12], (DEPTH, D_CONV), 0.05),
        "w_out": nrm(ks[13], (DEPTH, D_MIX, D_MODEL), D_MIX ** -0.5),
        "norm_mlp": 1.0 + nrm(ks[14], (DEPTH, D_MODEL), 0.05),
        "w_up": nrm(ks[15], (DEPTH, D_MODEL, D_FF), D_MODEL ** -0.5),
        "w_down": nrm(ks[16], (DEPTH, D_FF, D_MODEL), 0.8 * D_FF ** -0.5),
    }


def reference(x_prompt, x_sample, state_attn_k, state_attn_v, state_conv, rel_bias,
              norm_mix, w_in, q_norm, k_norm, conv_w, attn_out_norm, conv_out_norm,
              w_out, norm_mlp, w_up, w_down):
    biases = _branch_biases(rel_bias)
    s_p = x_prompt.shape[1]
    t_s = x_sample.shape[1]
    pw = min(ATTN_WINDOW, s_p)
    wb = state_attn_k.shape[2]
    xp, xs = x_prompt, x_sample
    kp_new, vp_new, cp_new, ks_new, vs_new, cs_new = [], [], [], [], [], []
    for l in range(DEPTH):
        q, k, v, u, gb = _mixer_inputs(xp, norm_mix[l], w_in[l], q_norm[l], k_norm[l])
        o = _combine_branches([_dilated_branch_prompt(q, k, v, bias_c, d)
                               for bias_c, (_, d) in zip(biases, DILATED_CONFIGS)])
        ucat = jnp.pad(u, ((0, 0), (CONV_WIDTH - 1, 0), (0, 0)))
        yc = _depthwise_conv(ucat, conv_w[l], s_p)
        xp = xp + _mixer_output(o, yc, gb, attn_out_norm[l], conv_out_norm[l], w_out[l])
        xp = xp + _mlp(xp, norm_mlp[l], w_up[l], w_down[l])
        kp_new.append(k[:, s_p - pw:])
        vp_new.append(v[:, s_p - pw:])
        cp_new.append(u[:, s_p - (CONV_WIDTH - 1):])

        q, k, v, u, gb = _mixer_inputs(xs, norm_mix[l], w_in[l], q_norm[l], k_norm[l])
        kcat = jnp.concatenate([state_attn_k[l].astype(k.dtype), k], axis=1)
        vcat = jnp.concatenate([state_attn_v[l].astype(v.dtype), v], axis=1)
        o = _combine_branches([_dilated_branch_sample(q, kcat, vcat, bias_c, d)
                               for bias_c, (_, d) in zip(biases, DILATED_CONFIGS)])
        ucat = jnp.concatenate([state_conv[l].astype(u.dtype), u], axis=1)
        yc = _depthwise_conv(ucat, conv_w[l], t_s)
        xs = xs + _mixer_output(o, yc, gb, attn_out_norm[l], conv_out_norm[l], w_out[l])
        xs = xs + _mlp(xs, norm_mlp[l], w_up[l], w_down[l])
        ks_new.append(kcat[:, t_s:t_s + wb])
        vs_new.append(vcat[:, t_s:t_s + wb])
        cs_new.append(ucat[:, t_s:])

    return (xp, xs, jnp.stack(kp_new), jnp.stack(vp_new), jnp.stack(cp_new),
            jnp.stack(ks_new), jnp.stack(vs_new), jnp.stack(cs_new))
```

```python
import numpy as np
import concourse.bass as bass
import concourse.mybir as mybir
from concourse.bass_utils import run_bass_kernel_spmd

F32 = mybir.dt.float32
BF16 = mybir.dt.bfloat16
ALU = mybir.AluOpType
AF = mybir.ActivationFunctionType
AX = mybir.AxisListType

L = 4
T = 1024
TC = 1025
NSLOT = 5
PPL = 192
NPARL = 74
EPS = 1e-6
TW = 2176
TILES = ((0, 512), (512, 1024), (1024, 1025))


def _bucket(dist):
    n_exact = 16
    large = n_exact + (np.log(np.maximum(dist, 1) / n_exact) / np.log(2048 / n_exact) * (32 - n_exact)).astype(np.int32)
    large = np.minimum(large, 31)
    return np.where(dist < n_exact, dist, large).astype(np.int32)


class Buf:
    __slots__ = ("w", "r", "name")

    def __init__(self, name=""):
        self.w = None
        self.r = []
        self.name = name


class Plan:
    ENG = ("pe", "act", "dve", "pool", "sp")

    def __init__(self, nc, semfn):
        self.nc = nc
        self.semfn = semfn
        self.q = {e: [] for e in self.ENG}
        self.cnt = {e: 0 for e in self.ENG}
        self.sem = {e: semfn("c_" + e) for e in ("pe", "act", "dve", "pool")}
        self.waited = {e: {} for e in self.ENG}
        nds = 12
        self.dsems = {e: [semfn("d_%s%d" % (e, i)) for i in range(nds)] for e in ("pool", "sp")}
        self.dval = {e: [0] * nds for e in ("pool", "sp")}
        self.dslot = {e: 0 for e in ("pool", "sp")}

    def new_layer_sems(self, l):
        for e in ("pe", "act", "dve", "pool"):
            self.sem[e] = self.semfn("c_%s_%d" % (e, l))
            self.cnt[e] = 0

    def _waits(self, eng, reads, writes, extra):
        need = {}
        lst = list(extra)
        for b in reads:
            if b.w is not None:
                lst.append(b.w)
        for b in writes:
            if b.w is not None:
                lst.append(b.w)
            lst.extend(b.r)
        for t in lst:
            if t is None:
                continue
            k = id(t[0])
            if k not in need or need[k][1] < t[1]:
                need[k] = t
        final = []
        for k, (sem, val) in need.items():
            if self.waited[eng].get(k, 0) >= val:
                continue
            self.waited[eng][k] = val
            final.append((sem, val))
        return final

    def op(self, eng, fn, reads=(), writes=(), extra=()):
        final = self._waits(eng, reads, writes, extra)
        self.cnt[eng] += 1
        tk = (self.sem[eng], self.cnt[eng])
        self.q[eng].append((final, fn, tk, 1))
        for b in reads:
            b.r.append(tk)
        for b in writes:
            b.w = tk
            b.r = []
        return tk

    def dma(self, eng, fn, reads=(), writes=(), extra=()):
        slot = self.dslot[eng]
        self.dslot[eng] = (slot + 1) % len(self.dsems[eng])
        sem = self.dsems[eng][slot]
        prev = self.dval[eng][slot]
        ex = list(extra)
        if prev > 0:
            ex.append((sem, prev))
        final = self._waits(eng, reads, writes, ex)
        self.dval[eng][slot] = prev + 16
        tk = (sem, prev + 16)
        self.q[eng].append((final, fn, tk, 16))
        for b in reads:
            b.r.append(tk)
        for b in writes:
            b.w = tk
            b.r = []
        return tk

    def coll(self, fn, reads=(), writes=()):
        sem = self.semfn("cc%d" % len(self.q["pool"]))
        final = self._waits("pool", reads, writes, ())
        tk = (sem, 1)
        self.q["pool"].append((final, fn, tk, 0))
        for b in reads:
            b.r.append(tk)
        for b in writes:
            b.w = tk
            b.r = []
        return tk

    def emit(self, eng, e):
        for final, fn, tk, inc in self.q[eng]:
            for sem, val in final:
                e.wait_ge(sem, val)
            ins = fn(e)
            if inc == 0:
                ins.then_inc(tk[0])
            else:
                ins.then_inc(tk[0], inc)
        if eng in ("pool", "sp"):
            for i, sem in enumerate(self.dsems[eng]):
                if self.dval[eng][i] > 0:
                    e.wait_ge(sem, self.dval[eng][i])


class _Stop(Exception):
    pass


def build(nl=L, stop=None, dumps=(), npieces=None):
    nc = bass.Bass("TRN2", target_bir_lowering=False)

    def mark(name):
        if stop is not None and name == stop:
            raise _Stop()
    dt = nc.dram_tensor
    xin = dt("xin", [128, 16, TC], F32, kind="ExternalInput").ap()
    wst = dt("wst", [npieces or nl * PPL, 128, 2048], F32, kind="ExternalInput").ap()
    par = dt("par", [128, 304], F32, kind="ExternalInput").ap()
    sconv = dt("sconv", [128, L, 8, 2], F32, kind="ExternalInput").ap()
    btoe = dt("btoe", [16, 128, TW], F32, kind="ExternalInput").ap()
    mtoe = dt("mtoe", [128, TW], F32, kind="ExternalInput").ap()
    sbias = dt("sbias", [128, 48], F32, kind="ExternalInput").ap()
    sb0 = dt("sb0", [1, 16], F32, kind="ExternalInput").ap()
    cst = dt("cst", [128, 4, 128], F32, kind="ExternalInput").ap()
    kst = dt("kst", [nl, 2048, 1024], F32, kind="ExternalInput").ap()
    vst = dt("vst", [nl, 2048, 1024], F32, kind="ExternalInput").ap()
    yout = dt("yout", [128, 16, TC], F32, kind="ExternalOutput").ap()
    kTo = dt("kTo", [L, 128, 8, T], F32, kind="ExternalOutput").ap()
    vTo = dt("vTo", [L, 128, 8, T], F32, kind="ExternalOutput").ap()
    cpo = dt("cpo", [L, 128, 8, 2], F32, kind="ExternalOutput").ap()
    kso = dt("kso", [nl, 2048, 1024], F32, kind="ExternalOutput").ap()
    vso = dt("vso", [nl, 2048, 1024], F32, kind="ExternalOutput").ap()
    cso = dt("cso", [L, 128, 8, 2], F32, kind="ExternalOutput").ap()
    xsp = dt("xsp", [128, 16, TC], F32).ap()
    bins = [[dt("bin%d_%d" % (l, c), [256, T], BF16) for c in range(8)] for l in range(L)]
    bouts = [[dt("bout%d_%d" % (l, c), [512, T], BF16) for c in range(8)] for l in range(L)]
    bin2s = [dt("binb%d" % l, [128, 16], F32) for l in range(L)]
    bout2s = [dt("boutb%d" % l, [256, 16], F32) for l in range(L)]

    off = [16512]

    def alloc(name, shape, dtp, at=None):
        esz = 2 if dtp == BF16 else 4
        nb = int(np.prod(shape[1:])) * esz
        nb = (nb + 31) // 32 * 32
        if at is None:
            o = off[0]
            off[0] += nb
        else:
            o = at
        return nc.alloc_sbuf_tensor_at(name, list(shape), dtp, offset=o), o, nb

    ring = []
    for i in range(NSLOT):
        t_, _, _ = alloc("ring%d" % i, [128, 2048], BF16)
        ring.append(t_)
    identb, _, _ = alloc("identb", [128, 128], BF16)
    onesb, _, _ = alloc("onesb", [128, 128], BF16)
    blkb, _, _ = alloc("blkb", [128, 128], BF16)
    cstf, _, _ = alloc("cstf", [128, 2, 128], F32)
    parS, _, _ = alloc("parS", [128, 304], F32)
    scS, _, _ = alloc("scS", [128, L, 8, 2], F32)
    sbS, _, _ = alloc("sbS", [128, 48], F32)
    sb0S, _, _ = alloc("sb0S", [1, 16], F32)
    mtS, _, _ = alloc("mtS", [128, TW], BF16)
    tail, _, _ = alloc("tail", [128, 8, 2], F32)
    y02, _, _ = alloc("y02", [128, 8, 2], F32)
    gb02, _, _ = alloc("gb02", [128, 8, 2], F32)
    csst, _, _ = alloc("csst", [128, 8, 2], F32)
    ptl, _, _ = alloc("ptl", [128, 8, 2], F32)
    sm1, _, _ = alloc("sm1", [128, 8, 8], F32)
    epsS, _, _ = alloc("epsS", [128, 1], F32)
    xT, XO, _ = alloc("xT", [128, 16, TC], F32)
    hT, HO, _ = alloc("hT", [128, 16, TC], BF16)
    off[0] += 32
    qT, QO, _ = alloc("qT", [128, 8, TC], BF16)
    kT, KO, _ = alloc("kT", [128, 8, TC], BF16)
    vT, _, _ = alloc("vT", [128, 8, TC], BF16)
    zT, _, _ = alloc("zT", [128, 8, TC], BF16)
    tA, _, _ = alloc("tA", [128, 1032], F32)
    tB, _, _ = alloc("tB", [128, 1032], F32)
    tCc, _, _ = alloc("tC", [128, 1032], F32)
    tU, _, _ = alloc("tU", [128, 1032], F32)
    tD, _, _ = alloc("tD", [128, 1032], BF16)
    assert off[0] <= 229376, off[0]
    oT, _, _ = alloc("oT", [128, 8, TC], BF16, at=HO)
    sqs, _, _ = alloc("sqs", [128, 8, TC], BF16, at=HO + 16416)
    hid = [alloc("hid0", [128, 8, TC], BF16, at=QO)[0], alloc("hid1", [128, 8, TC], BF16, at=KO)[0]]
    xo = [XO]

    def xalloc(name, shape, dtp):
        t_, o, nb = alloc(name, shape, dtp, at=xo[0])
        xo[0] += nb
        return t_

    Vown = [xalloc("Vown%d" % i, [128, 8, 192], BF16) for i in range(2)]
    Vprv = [xalloc("Vprv%d" % i, [128, 8, 192], BF16) for i in range(2)]
    kTp = [xalloc("kTp%d" % i, [128, T], BF16) for i in range(2)]
    vTp = [xalloc("vTp%d" % i, [128, T], BF16) for i in range(2)]
    Th = [xalloc("Th%d" % i, [128, TW], BF16) for i in range(4)]
    Eb = [xalloc("Eb%d" % i, [128, T], BF16) for i in range(2)]
    Pb = [xalloc("Pb%d" % i, [128, T], BF16) for i in range(2)]
    bst = xalloc("bst", [128, TW], F32)
    rden = xalloc("rden", [128, T], F32)
    vrow = xalloc("vrow", [1, T], F32)
    assert xo[0] <= XO + 65600, xo[0] - XO
    ho = [HO + 16416]

    def halloc(name, shape, dtp):
        t_, o, nb = alloc(name, shape, dtp, at=ho[0])
        ho[0] += nb
        return t_

    qb = halloc("qb", [128, T], F32)
    Kt = halloc("Kt", [128, T], F32)
    Vt2 = halloc("Vt2", [128, T], F32)
    assert ho[0] <= HO + 32832

    sems = []

    def semfn(name):
        s = nc.semaphore(name)
        h = s.__enter__()
        sems.append(s)
        return h

    ps_cm = nc.psum_tensor("ps", [128, 8, 512], F32)
    ps = ps_cm.__enter__()
    psf = ps[:].rearrange("p b n -> p (b n)")
    P = Plan(nc, semfn)

    B = {}

    def bf(name):
        if name not in B:
            B[name] = Buf(name)
        return B[name]

    def bl(prefix, n):
        return [bf("%s%d" % (prefix, i)) for i in range(n)]

    bX = bl("x", 16)
    bH = bl("h", 16)
    bQ = bl("q", 8)
    bK = bl("k", 8)
    bV = bl("v", 8)
    bZ = bl("z", 8)
    bRing = bl("ring", NSLOT)
    bBank = bl("bank", 8)
    bTA, bTB, bTC, bTU, bTD = bf("tA"), bf("tB"), bf("tC"), bf("tU"), bf("tD")
    bHid = [bl("hid0_", 8), bl("hid1_", 8)]
    accs = [(psf[:, 0:TC], [bBank[0], bBank[1], bBank[2]]), (psf[:, 1536:1536 + TC], [bBank[3], bBank[4], bBank[5]])]
    STAT = (psf[:, 3072:4096], [bBank[6], bBank[7]])

    def pv(i):
        return parS[:, i:i + 1]

    bC = bf("consts")
    P.dma("sp", lambda e: e.dma_start(out=xT[:], in_=xin), writes=bX)
    P.dma("sp", lambda e: e.dma_start(out=parS[:], in_=par), writes=[bC])
    P.dma("sp", lambda e: e.dma_start(out=scS[:], in_=sconv), writes=[bC])
    P.dma("sp", lambda e: e.dma_start(out=sbS[:], in_=sbias), writes=[bC])
    P.dma("sp", lambda e: e.dma_start(out=sb0S[:], in_=sb0), writes=[bC])
    P.dma("sp", lambda e: e.dma_start(out=cstf[:, 0, :], in_=cst[:, 0, :]), writes=[bC])
    P.dma("sp", lambda e: e.dma_start(out=cstf[:, 1, :], in_=cst[:, 1, :]), writes=[bC])
    P.dma("pool", lambda e: e.dma_start(out=identb[:], in_=cst[:, 0, :]), writes=[bC])
    P.dma("pool", lambda e: e.dma_start(out=onesb[:], in_=cst[:, 1, :]), writes=[bC])
    P.dma("pool", lambda e: e.dma_start(out=blkb[:], in_=cst[:, 2, :]), writes=[bC])
    P.dma("pool", lambda e: e.dma_start(out=mtS[:], in_=mtoe), writes=[bC])
    identf = cstf[:, 0, :]
    onesf = cstf[:, 1, :]
    P.op("dve", lambda e: e.memset(epsS[:], EPS), writes=[bC])
    P.op("dve", lambda e: e.memset(tU[:, 0:2], 0.0), writes=[bTU])
    FLAG = 296
    bVo = [bl("vo0_", 1), bl("vo1_", 1)]
    bVp = [bl("vp0_", 1), bl("vp1_", 1)]
    bKp = [bl("kTp0_", 1), bl("kTp1_", 1)]
    bVTp = [bl("vTp0_", 1), bl("vTp1_", 1)]
    bTh = [bf("Th%d" % i) for i in range(4)]
    bBst = bf("bst")
    bE = [bf("E0"), bf("E1")]
    bP = [bf("P0"), bf("P1")]
    bRd = bf("rden")
    bVr = bf("vrow")
    bSm = bf("sm1")
    XAL0 = [bVo[0][0], bVo[1][0], bVp[0][0], bVp[1][0], bKp[0][0], bKp[1][0], bVTp[0][0], bVTp[1][0]] + bTh + bE + bP + [bBst, bRd, bVr]
    XAL = XAL0 + [bf("E4_%d" % i) for i in range(4)] + [bf("P4_%d" % i) for i in range(4)]
    QROW = tCc[0:1, 0:T]
    KROW = tU[0:1, 2:2 + T]
    VROW = vrow[0:1, :]
    ROWS = [QROW, KROW, VROW]

    ws = {"next": 0, "cons": 0}

    def fetch_upto(n):
        while ws["next"] < min(n, npieces or nl * PPL):
            i = ws["next"]
            s = i % NSLOT
            P.dma("pool", (lambda i, s: lambda e: e.dma_start(out=ring[s][:], in_=wst[i]))(i, s), writes=[bRing[s]])
            ws["next"] += 1

    def next_piece():
        i = ws["cons"]
        ws["cons"] += 1
        fetch_upto(i + NSLOT)
        return i % NSLOT

    def mm_piece(slot, srcs, sbufs, acc, nk=16, wcol=lambda kc: (kc * 128, kc * 128 + 128)):
        accap, accb = acc

        def fn(e):
            ins = None
            for kc in range(nk):
                a, b = wcol(kc)
                for lo, hi in TILES:
                    ins = e.matmul(accap[:, lo:hi], lhsT=ring[slot][:, a:b], rhs=srcs[kc][:, lo:hi],
                                   start=(kc == 0), stop=(kc == nk - 1))
            return ins
        return P.op("pe", fn, reads=[bRing[slot]] + list(sbufs), writes=accb)

    def stat_mm(lhs, srcs, sbufs, acc):
        accap, accb = acc
        n = len(srcs)

        def fn(e):
            ins = None
            for kc in range(n):
                for lo, hi in TILES:
                    ins = e.matmul(accap[:, lo:hi], lhsT=lhs[:], rhs=srcs[kc][:, lo:hi], start=(kc == 0), stop=(kc == n - 1))
            return ins
        return P.op("pe", fn, reads=[bC] + list(sbufs), writes=accb)

    def rstd_from(acc, out_t, out_b, scale):
        accap, accb = acc
        P.op("act", lambda e: e.activation(out=out_t[:, 0:TC], in_=accap, func=AF.Sqrt, bias=epsS[:], scale=scale),
             reads=accb + [bC], writes=[out_b])
        P.op("dve", lambda e: e.reciprocal(out=out_t[:, 0:TC], in_=out_t[:, 0:TC]), reads=[out_b], writes=[out_b])

    def rmsnorm_x(gbase):
        for c in range(16):
            P.op("act", (lambda c: lambda e: e.activation(out=hT[:, c, :], in_=xT[:, c, :], func=AF.Square))(c),
                 reads=[bX[c]], writes=[bH[c]])
        stat_mm(onesb, [hT[:, c, :] for c in range(16)], bH, accs[0])
        rstd_from(accs[0], tA, bTA, 1.0 / 2048)
        for c in range(16):
            eng = "dve"
            P.op(eng, (lambda c: lambda e: e.scalar_tensor_tensor(out=hT[:, c, :], in0=xT[:, c, :], scalar=pv(gbase + c),
                                                                  in1=tA[:, 0:TC], op0=ALU.mult, op1=ALU.mult))(c),
                 reads=[bX[c], bTA, bC], writes=[bH[c]])

    hsrc = [hT[:, c, :] for c in range(16)]
    accsel = [0]

    def nextacc():
        accsel[0] ^= 1
        return accs[accsel[0]]

    def layer(l):
        if l > 0:
            P.new_layer_sems(l)
        pb = l * NPARL
        G_MIX, G_MLP, G_Q, G_K, G_CW, G_GA, G_GC = pb, pb + 16, pb + 32, pb + 33, pb + 34, pb + 58, pb + 66
        for (dst_, src_) in ((kso, kst), (vso, vst)):
            P.dma("sp", (lambda l, dst_, src_: lambda e: e.dma_start(
                out=dst_[l, 0:2047, :].rearrange("(a b) c -> a (b c)", b=23),
                in_=src_[l, 1:2048, :].rearrange("(a b) c -> a (b c)", b=23)))(l, dst_, src_))
        rmsnorm_x(G_MIX)
        sp_tk = P.dma("sp", lambda e: e.dma_start(out=xsp, in_=xT[:]), reads=bX)
        for b_ in XAL:
            b_.w = sp_tk
            b_.r = []
        for i in range(2):
            P.op("dve", (lambda i: lambda e: e.memset(Vown[i][:, :, 64:128], 1.0))(i), writes=bVo[i])
            P.op("dve", (lambda i: lambda e: e.memset(Vprv[i][:, :, 64:128], 1.0))(i), writes=bVp[i])
            P.op("dve", (lambda i: lambda e: e.tensor_scalar(out=Vprv[i][:, :, 64:128], in0=Vprv[i][:, :, 64:128],
                                                              scalar1=pv(FLAG), scalar2=None, op0=ALU.mult))(i),
                 reads=[bC], writes=bVp[i])

        mark('p0')
        def qk_piece(dst, dbuf, c, gidx):
            slot = next_piece()
            mm_piece(slot, hsrc, bH, accs[0])
            P.op("dve", lambda e: e.tensor_copy(out=tA[:, 0:TC], in_=accs[0][0]), reads=accs[0][1], writes=[bTA])
            P.op("act", lambda e: e.activation(out=tD[:, 0:TC], in_=tA[:, 0:TC], func=AF.Square), reads=[bTA], writes=[bTD])
            stat_mm(blkb, [tD[:, 0:TC]], [bTD], accs[1])
            rstd_from(accs[1], tB, bTB, 1.0 / 64)
            P.op("dve", lambda e: e.scalar_tensor_tensor(out=dst[:, c, :], in0=tA[:, 0:TC], scalar=pv(gidx), in1=tB[:, 0:TC],
                                                         op0=ALU.mult, op1=ALU.mult),
                 reads=[bTA, bTB, bC], writes=[dbuf[c]])

        for c in range(8):
            qk_piece(kT, bK, c, G_K)
        bBin = [bf("bin%d_%d" % (l, c)) for c in range(8)]
        bBout = [bf("bout%d_%d" % (l, c)) for c in range(8)]
        for c in range(8):
            slot = next_piece()
            acc = nextacc()
            mm_piece(slot, hsrc, bH, acc)
            P.op("act", (lambda c, acc: lambda e: e.activation(out=vT[:, c, :], in_=acc[0], func=AF.Copy))(c, acc),
                 reads=acc[1], writes=[bV[c]])
            P.dma("sp", (lambda l, c: lambda e: e.dma_start(out=bins[l][c].ap()[0:128, :], in_=kT[:, c, 0:T]))(l, c),
                  reads=[bK[c]], writes=[bBin[c]])
            P.dma("sp", (lambda l, c: lambda e: e.dma_start(out=bins[l][c].ap()[128:256, :], in_=vT[:, c, 0:T]))(l, c),
                  reads=[bV[c]], writes=[bBin[c]])
            P.coll((lambda l, c: lambda e: e.collective_compute("AllGather", ALU.bypass,
                                                                replica_groups=[[0, 1], [2, 3], [4, 5], [6, 7]],
                                                                ins=[bins[l][c].ap()], outs=[bouts[l][c].ap()]))(l, c),
                   reads=[bBin[c]], writes=[bBout[c]])
        mark('x2')
        P.dma("pool", (lambda l: lambda e: e.dma_start(out=kTo[l], in_=kT[:, :, 0:T]))(l), reads=bK)
        P.dma("pool", (lambda l: lambda e: e.dma_start(out=vTo[l], in_=vT[:, :, 0:T]))(l), reads=bV)
        mark('xchg')
        bTail, bY02, bG02, bCs = bf("tail"), bf("y02"), bf("gb02"), bf("csst")
        for c in range(8):
            w0, w1, w2 = pv(G_CW + c), pv(G_CW + 8 + c), pv(G_CW + 16 + c)
            slot = next_piece(); acc = nextacc()
            mm_piece(slot, hsrc, bH, acc)
            P.op("act", (lambda acc: lambda e: e.activation(out=tA[:, 0:TC], in_=acc[0], func=AF.Copy))(acc),
                 reads=acc[1], writes=[bTA])
            mark('cv1')
            slot = next_piece(); acc = nextacc()
            mm_piece(slot, hsrc, bH, acc)
            P.op("dve", (lambda acc: lambda e: e.tensor_tensor(out=tU[:, 2:2 + TC], in0=acc[0], in1=tA[:, 0:TC], op=ALU.mult))(acc),
                 reads=acc[1] + [bTA], writes=[bTU])
            mark('cv2')
            P.op("act", (lambda c: lambda e: e.activation(out=tail[:, c, :], in_=tU[:, 1024:1026], func=AF.Copy))(c),
                 reads=[bTU], writes=[bTail])
            P.op("act", (lambda c: lambda e: e.activation(out=csst[:, c, 1:2], in_=tU[:, 1026:1027], func=AF.Copy))(c),
                 reads=[bTU], writes=[bCs])
            P.op("act", (lambda c, l: lambda e: e.activation(out=csst[:, c, 0:1], in_=scS[:, l, c, 1:2], func=AF.Copy))(c, l),
                 reads=[bC], writes=[bCs])
            mark('cv3')
            P.op("dve", (lambda w2: lambda e: e.tensor_scalar(out=tCc[:, 0:T], in0=tU[:, 2:2 + T], scalar1=w2, scalar2=None,
                                                              op0=ALU.mult))(w2), reads=[bTU, bC], writes=[bTC])
            P.op("dve", (lambda w1: lambda e: e.scalar_tensor_tensor(out=tCc[:, 0:T], in0=tU[:, 1:1 + T], scalar=w1, in1=tCc[:, 0:T],
                                                                     op0=ALU.mult, op1=ALU.add))(w1), reads=[bTU, bTC, bC], writes=[bTC])
            P.op("dve", (lambda w0: lambda e: e.scalar_tensor_tensor(out=tCc[:, 0:T], in0=tU[:, 0:T], scalar=w0, in1=tCc[:, 0:T],
                                                                     op0=ALU.mult, op1=ALU.add))(w0), reads=[bTU, bTC, bC], writes=[bTC])
            mark('cv4')
            P.op("act", (lambda c, l, w0: lambda e: e.activation(out=tCc[:, T:TC], in_=scS[:, l, c, 0:1], func=AF.Identity, scale=w0))(c, l, w0),
                 reads=[bC], writes=[bTC])
            P.op("act", (lambda c, l, w1: lambda e: e.activation(out=tCc[:, T:TC], in_=scS[:, l, c, 1:2], func=AF.Identity, scale=w1,
                                                                 bias=tCc[:, T:TC]))(c, l, w1), reads=[bC, bTC], writes=[bTC])
            P.op("act", (lambda w2: lambda e: e.activation(out=tCc[:, T:TC], in_=tU[:, 1026:1027], func=AF.Identity, scale=w2,
                                                           bias=tCc[:, T:TC]))(w2), reads=[bTU, bTC, bC], writes=[bTC])
            mark('cv5')
            P.op("act", (lambda c: lambda e: e.activation(out=y02[:, c, :], in_=tCc[:, 0:2], func=AF.Copy))(c),
                 reads=[bTC], writes=[bY02])
            slot = next_piece(); acc = nextacc()
            mm_piece(slot, hsrc, bH, acc)
            P.op("dve", (lambda c, acc: lambda e: e.tensor_tensor(out=zT[:, c, :], in0=acc[0], in1=tCc[:, 0:TC], op=ALU.mult))(c, acc),
                 reads=acc[1] + [bTC], writes=[bZ[c]])
            P.op("act", (lambda c, acc: lambda e: e.activation(out=gb02[:, c, :], in_=acc[0][:, 0:2], func=AF.Copy))(c, acc),
                 reads=[], writes=[bG02] + acc[1])
        mark('conv')
        bB2i, bB2o = bf("b2i%d" % l), bf("b2o%d" % l)
        P.dma("sp", (lambda l: lambda e: e.dma_start(out=bin2s[l].ap().rearrange("p (c r) -> p c r", r=2), in_=tail[:]))(l),
              reads=[bTail], writes=[bB2i])
        P.coll((lambda l: lambda e: e.collective_compute("AllGather", ALU.bypass,
                                                         replica_groups=[[0, 1], [2, 3], [4, 5], [6, 7]],
                                                         ins=[bin2s[l].ap()], outs=[bout2s[l].ap()]))(l),
               reads=[bB2i], writes=[bB2o])
        P.dma("sp", (lambda l: lambda e: e.dma_start(out=cpo[l], in_=tail[:]))(l), reads=[bTail])
        P.dma("sp", (lambda l: lambda e: e.dma_start(out=cso[l], in_=csst[:]))(l), reads=[bCs])
        for c in range(8):
            qk_piece(qT, bQ, c, G_Q)

        mark('p1')
        bO = bH[0:8]
        bS2 = bH[8:16]
        bank6, bank7 = psf[:, 3072:3584], psf[:, 3584:4096]
        for ri, (src, sb_) in enumerate(((qT, bQ), (kT, bK), (vT, bV))):
            def fn(e, src=src):
                ins = None
                for c in range(8):
                    ins = e.matmul(psf[0:1, 3072 + c * 128:3072 + (c + 1) * 128], lhsT=src[:, c, T:TC], rhs=identb[:],
                                   start=True, stop=True)
                return ins
            P.op("pe", fn, reads=list(sb_) + [bC], writes=[bBank[6], bBank[7]])
            P.op("act", (lambda ri: lambda e: e.activation(out=ROWS[ri], in_=psf[0:1, 3072:4096], func=AF.Copy))(ri),
                 reads=[bBank[6], bBank[7]], writes=bS2 + [bTC, bTU, bVr])
        P.dma("sp", (lambda l: lambda e: e.dma_start(out=kso[l, 2047:2048, :], in_=KROW))(l), reads=[bTU])
        P.dma("sp", (lambda l: lambda e: e.dma_start(out=vso[l, 2047:2048, :], in_=VROW))(l), reads=[bVr])

        def fn(e):
            e.matmul(bank6, lhsT=onesf[0:1, :], rhs=QROW[:, 0:512], start=True, stop=True)
            return e.matmul(bank7, lhsT=onesf[0:1, :], rhs=QROW[:, 512:1024], start=True, stop=True)
        P.op("pe", fn, reads=bS2 + [bC, bTC], writes=[bBank[6], bBank[7]])
        P.op("act", lambda e: e.activation(out=qb[:], in_=psf[:, 3072:4096], func=AF.Copy), reads=[bBank[6], bBank[7]], writes=bS2)
        lg = sm1[:, 0:6, :].rearrange("p a b -> p (a b)")
        pe_ = tB[:, 0:48]
        for br, d in enumerate((1, 4, 16)):
            P.dma("sp", (lambda l, d: lambda e: e.dma_start(out=Kt[:], in_=kst[l, 2048 - 128 * d:2048:d, :]))(l, d), writes=bS2)
            P.op("dve", lambda e: e.tensor_tensor(out=Kt[:], in0=Kt[:], in1=qb[:], op=ALU.mult), reads=bS2, writes=bS2)
            P.op("dve", (lambda br: lambda e: e.tensor_reduce(out=lg[:, br * 16:(br + 1) * 16],
                                                              in_=Kt[:].rearrange("p (h e) -> p h e", e=64), axis=AX.X, op=ALU.add))(br),
                 reads=bS2, writes=[bSm])
        P.op("dve", lambda e: e.scalar_tensor_tensor(out=lg, in0=lg, scalar=0.125, in1=sbS[:], op0=ALU.mult, op1=ALU.add),
             reads=[bC, bSm], writes=[bSm])
        P.op("act", lambda e: e.activation(out=pe_, in_=lg, func=AF.Exp), reads=[bSm], writes=[bTB])
        for br, d in enumerate((1, 4, 16)):
            P.dma("sp", (lambda l, d: lambda e: e.dma_start(out=Vt2[:], in_=vst[l, 2048 - 128 * d:2048:d, :]))(l, d), writes=bS2)
            P.op("dve", (lambda br: lambda e: e.tensor_tensor(
                out=Vt2[:].rearrange("p (h e) -> p h e", e=64), in0=Vt2[:].rearrange("p (h e) -> p h e", e=64),
                in1=pe_[:, br * 16:(br + 1) * 16].unsqueeze(2).to_broadcast([128, 16, 64]), op=ALU.mult))(br),
                reads=bS2 + [bTB], writes=bS2)

            def fn(e, br=br):
                e.matmul(psf[0:1, 3072:3584], lhsT=onesf[:, 0:1], rhs=Vt2[:, 0:512], start=(br == 0), stop=(br == 2))
                e.matmul(psf[0:1, 3584:4096], lhsT=onesf[:, 0:1], rhs=Vt2[:, 512:1024], start=(br == 0), stop=(br == 2))
                return e.matmul(psf[0:1, 2048:2064], lhsT=onesf[:, 0:1], rhs=pe_[:, br * 16:(br + 1) * 16], start=(br == 0), stop=(br == 2))
            P.op("pe", fn, reads=bS2 + [bTB, bC], writes=[bBank[6], bBank[7], bBank[4]])
        t1 = tA[0:1, 0:T]
        l0 = sm1[0:1, 6, 0:8]
        l0 = sm1[0:1, 6:8, :].rearrange("p a b -> p (a b)")
        P.op("dve", lambda e: e.tensor_tensor(out=t1, in0=QROW, in1=KROW, op=ALU.mult), reads=bS2 + [bTC, bTU], writes=[bTA])
        P.op("dve", lambda e: e.tensor_reduce(out=l0, in_=t1.rearrange("p (h e) -> p h e", e=64), axis=AX.X, op=ALU.add),
             reads=[bTA], writes=[bSm])
        P.op("dve", lambda e: e.scalar_tensor_tensor(out=l0, in0=l0, scalar=0.125, in1=sb0S[:], op0=ALU.mult, op1=ALU.add),
             reads=[bC, bSm], writes=[bSm])
        P.op("act", lambda e: e.activation(out=l0, in_=l0, func=AF.Exp), reads=[bSm], writes=[bSm])
        P.op("dve", lambda e: e.tensor_scalar(out=l0, in0=l0, scalar1=3.0, scalar2=None, op0=ALU.mult), reads=[bSm], writes=[bSm])
        P.op("dve", lambda e: e.tensor_tensor(out=t1.rearrange("p (h e) -> p h e", e=64),
                                              in0=VROW.rearrange("p (h e) -> p h e", e=64),
                                              in1=l0.unsqueeze(2).to_broadcast([1, 16, 64]), op=ALU.mult),
             reads=bS2 + [bVr, bSm], writes=[bTA])
        P.op("dve", lambda e: e.tensor_tensor(out=t1, in0=psf[0:1, 3072:4096], in1=t1, op=ALU.add),
             reads=[bBank[6], bBank[7], bTA], writes=[bTA])
        P.op("dve", lambda e: e.tensor_tensor(out=l0, in0=psf[0:1, 2048:2064], in1=l0, op=ALU.add), reads=[bBank[4], bSm], writes=[bSm])
        P.op("dve", lambda e: e.reciprocal(out=l0, in_=l0), reads=[bSm], writes=[bSm])
        P.op("dve", lambda e: e.tensor_tensor(out=QROW.rearrange("p (h e) -> p h e", e=64),
                                              in0=t1.rearrange("p (h e) -> p h e", e=64),
                                              in1=l0.unsqueeze(2).to_broadcast([1, 16, 64]), op=ALU.mult),
             reads=[bTA, bSm], writes=bS2 + [bTC])

        def fn(e):
            ins = None
            for c in range(8):
                ins = e.matmul(psf[:, 3072 + 2 * c:3074 + 2 * c], lhsT=QROW[:, c * 128:(c + 1) * 128], rhs=onesf[0:1, 0:2],
                               start=True, stop=True)
            return ins
        P.op("pe", fn, reads=bS2 + [bC, bTC], writes=[bBank[6], bBank[7]])
        bOs = bf("osample")
        P.op("act", lambda e: e.activation(out=oT[:, :, T:TC], in_=psf[:, 3072:3088].rearrange("p (c two) -> p c two", two=2)[:, :, 0:1], func=AF.Copy),
             reads=[bBank[6], bBank[7]], writes=[bOs])

        mark('samp')
        Sb = [(psf[:, k * 512:(k + 1) * 512], [bBank[k]]) for k in range(4)]
        Oaccs = [(psf[:, 2048:3072], [bBank[4], bBank[5]]), (psf[:, 3072:4096], [bBank[6], bBank[7]])]
        bE4 = [bf("E4_%d" % i) for i in range(4)]
        bP4 = [bf("P4_%d" % i) for i in range(4)]
        for b_ in bE4 + bP4:
            b_.w = sp_tk
            b_.r = []
        EbS = [Eb[k // 2][:, (k % 2) * 512:(k % 2) * 512 + 512] for k in range(4)]
        PbS = [Pb[k // 2][:, (k % 2) * 512:(k % 2) * 512 + 512] for k in range(4)]
        tbank = (psf[:, 0:512], psf[:, 512:1024])

        def build_v(c, pp):
            P.dma("sp", (lambda l, c, pp: lambda e: e.dma_start(out=kTp[pp][:], in_=bouts[l][c].ap()[0:128, :]))(l, c, pp),
                  reads=[bBout[c]], writes=bKp[pp])
            P.dma("sp", (lambda l, c, pp: lambda e: e.dma_start(out=vTp[pp][:], in_=bouts[l][c].ap()[128:256, :]))(l, c, pp),
                  reads=[bBout[c]], writes=bVTp[pp])
            for hh in range(2):
                ti = pp * 2 + hh
                P.dma("sp", (lambda c, hh: lambda e: e.dma_start(out=bst[:], in_=btoe[2 * c + hh]))(c, hh), writes=[bBst])
                P.op("act", (lambda ti: lambda e: e.activation(out=Th[ti][:], in_=bst[:], func=AF.Exp))(ti), reads=[bBst], writes=[bTh[ti]])
                P.op("dve", (lambda ti: lambda e: e.tensor_tensor(out=Th[ti][:], in0=Th[ti][:], in1=mtS[:], op=ALU.mult))(ti),
                     reads=[bTh[ti], bC], writes=[bTh[ti]])
            for (src, srcb, dstl, dstb, prev) in ((vT[:, c, :], [bV[c]], Vown[pp], bVo[pp], False),
                                                  (vTp[pp], bVTp[pp], Vprv[pp], bVp[pp], True)):
                for half in range(2):
                    bank = tbank[half]
                    bb = [bBank[half]]

                    def fn(e, src=src, half=half, bank=bank):
                        ins = None
                        for j in range(4):
                            jj = half * 4 + j
                            ins = e.matmul(bank[:, j * 128:(j + 1) * 128], lhsT=src[:, jj * 128:(jj + 1) * 128], rhs=identb[:],
                                           start=True, stop=True)
                        return ins
                    P.op("pe", fn, reads=list(srcb) + [bC], writes=bb)
                    bv = bank.rearrange("p (j f) -> p j f", f=128)
                    for (dc, sc_) in ((0, 0), (128, 64)):
                        if prev:
                            P.op("dve", (lambda dstl, half, dc, sc_, bv: lambda e: e.tensor_scalar(
                                out=dstl[:, half * 4:half * 4 + 4, dc:dc + 64], in0=bv[:, :, sc_:sc_ + 64], scalar1=pv(FLAG), scalar2=None,
                                op0=ALU.mult))(dstl, half, dc, sc_, bv), reads=bb + [bC], writes=dstb)
                        else:
                            P.op("act", (lambda dstl, half, dc, sc_, bv: lambda e: e.activation(
                                out=dstl[:, half * 4:half * 4 + 4, dc:dc + 64], in_=bv[:, :, sc_:sc_ + 64], func=AF.Copy))(dstl, half, dc, sc_, bv),
                                reads=bb, writes=dstb)

        def attn_head(c, pp, hh):
            po = 64 * hh
            ti = pp * 2 + hh
            Oacc = Oaccs[hh]
            steps = []
            for jp in range(8):
                for h in range(2):
                    steps.append((True, jp, 512 * h, 512 * h + 512, 1024 - 128 * jp + 512 * h))
            for j in range(8):
                for h in range(2):
                    a_, b_ = max(128 * j, 512 * h), 512 * h + 512
                    if a_ < b_:
                        steps.append((False, j, a_, b_, a_ - 128 * j))
            n = len(steps)
            firsts = {}
            lasts = {}
            for si, st in enumerate(steps):
                bk = st[2] // 512
                firsts.setdefault(bk, si)
                lasts[bk] = si

            def S_op(si):
                prev, j, a_, b_, ts = steps[si]
                sap, sbk = Sb[si % 4]
                if prev:
                    ksrc, kb = kTp[pp][po:po + 64, j * 128:(j + 1) * 128], bKp[pp]
                else:
                    ksrc, kb = kT[po:po + 64, c, j * 128:(j + 1) * 128], [bK[c]]
                P.op("pe", lambda e: e.matmul(sap[:, 0:b_ - a_], lhsT=ksrc, rhs=qT[po:po + 64, c, a_:b_], start=True, stop=True),
                     reads=list(kb) + [bQ[c]], writes=sbk)

            def EP_op(si):
                prev, j, a_, b_, ts = steps[si]
                sap, sbk = Sb[si % 4]
                w_ = b_ - a_
                k = si % 4
                P.op("act", lambda e: e.activation(out=EbS[k][:, 0:w_], in_=sap[:, 0:w_], func=AF.Exp, scale=0.125),
                     reads=sbk, writes=[bE4[k]])
                P.op("dve", lambda e: e.tensor_tensor(out=PbS[k][:, 0:w_], in0=EbS[k][:, 0:w_], in1=Th[ti][:, ts:ts + w_], op=ALU.mult),
                     reads=[bE4[k], bTh[ti]], writes=[bP4[k]])

            def PV_op(si):
                prev, j, a_, b_, ts = steps[si]
                vt = (Vprv if prev else Vown)[pp]
                vb = (bVp if prev else bVo)[pp]
                lhs = vt[:, j, 0:128] if hh == 0 else vt[:, j, 64:192]
                bk = a_ // 512
                k = si % 4
                P.op("pe", lambda e: e.matmul(Oacc[0][:, a_:b_], lhsT=lhs, rhs=PbS[k][:, 0:b_ - a_],
                                              start=(firsts[bk] == si), stop=(lasts[bk] == si)),
                     reads=list(vb) + [bP4[k]], writes=Oacc[1])

            for si in range(min(3, n)):
                S_op(si)
            for si in range(n):
                EP_op(si)
                PV_op(si)
                if si + 3 < n:
                    S_op(si + 3)
            dpo = 64 - po
            P.op("act", lambda e: e.activation(out=rden[po:po + 64, :], in_=Oacc[0][dpo:dpo + 64, :], func=AF.Ln), reads=Oacc[1], writes=[bRd])
            P.op("act", lambda e: e.activation(out=rden[po:po + 64, :], in_=rden[po:po + 64, :], func=AF.Exp, scale=-1.0), reads=[bRd], writes=[bRd])
            P.op("dve", lambda e: e.tensor_tensor(out=oT[po:po + 64, c, 0:T], in0=Oacc[0][po:po + 64, :], in1=rden[po:po + 64, :],
                                                  op=ALU.mult), reads=Oacc[1] + [bRd], writes=[bO[c]])

        build_v(0, 0)
        for c in range(8):
            if c + 1 < 8:
                build_v(c + 1, (c + 1) % 2)
            for hh in range(2):
                attn_head(c, c % 2, hh)

        mark('attn')
        bPt = bf("ptl")
        P.dma("sp", (lambda l: lambda e: e.dma_start(out=ptl[:], in_=bout2s[l].ap()[0:128, :].rearrange("p (c r) -> p c r", r=2)))(l),
              reads=[bB2o], writes=[bPt])
        P.op("dve", lambda e: e.tensor_scalar(out=ptl[:], in0=ptl[:], scalar1=pv(FLAG), scalar2=None, op0=ALU.mult),
             reads=[bPt, bC], writes=[bPt])
        W0, W1 = parS[:, G_CW:G_CW + 8], parS[:, G_CW + 8:G_CW + 16]
        f0, f1, f2 = sm1[:, 0, :], sm1[:, 1, :], sm1[:, 2, :]
        bF = bSm
        P.op("dve", lambda e: e.tensor_tensor(out=f0, in0=ptl[:, :, 0], in1=W0, op=ALU.mult), reads=[bPt, bC], writes=[bF])
        P.op("dve", lambda e: e.tensor_tensor(out=f1, in0=ptl[:, :, 1], in1=W1, op=ALU.mult), reads=[bPt, bC, bF], writes=[bF])
        P.op("dve", lambda e: e.tensor_tensor(out=f0, in0=f0, in1=f1, op=ALU.add), reads=[bF], writes=[bF])
        P.op("dve", lambda e: e.tensor_tensor(out=f0, in0=f0, in1=y02[:, :, 0], op=ALU.add), reads=[bF, bY02], writes=[bF])
        P.op("dve", lambda e: e.tensor_tensor(out=zT[:, :, 0], in0=f0, in1=gb02[:, :, 0], op=ALU.mult), reads=[bF, bG02], writes=bZ)
        P.op("dve", lambda e: e.tensor_tensor(out=f2, in0=ptl[:, :, 1], in1=W0, op=ALU.mult), reads=[bPt, bC, bF], writes=[bF])
        P.op("dve", lambda e: e.tensor_tensor(out=f2, in0=f2, in1=y02[:, :, 1], op=ALU.add), reads=[bF, bY02], writes=[bF])
        P.op("dve", lambda e: e.tensor_tensor(out=zT[:, :, 1], in0=f2, in1=gb02[:, :, 1], op=ALU.mult), reads=[bF, bG02], writes=bZ)
        for (src, sbuf_, acc, rt, rb, gbase) in ((oT, bO, accs[0], tA, bTA, G_GA), (zT, bZ, accs[1], tB, bTB, G_GC)):
            for c in range(8):
                P.op("act", (lambda c, src: lambda e: e.activation(out=sqs[:, c, :], in_=src[:, c, :], func=AF.Square))(c, src),
                     reads=[sbuf_[c], bOs], writes=[bS2[c]])
            stat_mm(onesb, [sqs[:, c, :] for c in range(8)], bS2, acc)
            rstd_from(acc, rt, rb, 1.0 / 1024)
            for c in range(8):
                P.op("dve", (lambda c, src, rt, gbase: lambda e: e.scalar_tensor_tensor(
                    out=src[:, c, :], in0=src[:, c, :], scalar=pv(gbase + c), in1=rt[:, 0:TC], op0=ALU.mult, op1=ALU.mult))(c, src, rt, gbase),
                    reads=[sbuf_[c], rb, bC, bOs], writes=[sbuf_[c]])
        allx = XAL
        P.dma("sp", lambda e: e.dma_start(out=xT[:], in_=xsp), reads=[], writes=bX + allx)

        mark('p25')
        osrc = [oT[:, c, :] for c in range(8)] + [zT[:, c, :] for c in range(8)]
        for m in range(16):
            slot = next_piece(); acc = nextacc()
            mm_piece(slot, osrc, list(bO) + list(bZ), acc)
            P.op("dve", (lambda m, acc: lambda e: e.tensor_tensor(out=xT[:, m, :], in0=acc[0], in1=xT[:, m, :], op=ALU.add))(m, acc),
                 reads=acc[1] + [bX[m]], writes=[bX[m]])

        mark('p3')
        rmsnorm_x(G_MLP)

        def up_block(b):
            hb = b % 2
            for j in range(8):
                slot = next_piece(); acc = nextacc()
                mm_piece(slot, hsrc, bH, acc)
                P.op("act", (lambda acc: lambda e: e.activation(out=tCc[:, 0:TC], in_=acc[0], func=AF.Relu))(acc),
                     reads=acc[1], writes=[bTC])
                P.op("dve", (lambda hb, j: lambda e: e.tensor_tensor(out=hid[hb][:, j, :], in0=tCc[:, 0:TC], in1=tCc[:, 0:TC], op=ALU.mult))(hb, j),
                     reads=[bTC], writes=[bHid[hb][j], bQ[j] if hb == 0 else bK[j]])

        def down_block(b):
            hb = b % 2
            hs = [hid[hb][:, j, :] for j in range(8)]
            for g in range(8):
                slot = next_piece()
                for mm_ in range(2):
                    acc = nextacc()
                    m = 2 * g + mm_
                    mm_piece(slot, hs, bHid[hb] + (bQ if hb == 0 else bK), acc, nk=8, wcol=(lambda mm_: lambda kc: (kc * 256 + mm_ * 128, kc * 256 + mm_ * 128 + 128))(mm_))
                    P.op("dve", (lambda m, acc: lambda e: e.tensor_tensor(out=xT[:, m, :], in0=acc[0], in1=xT[:, m, :], op=ALU.add))(m, acc),
                         reads=acc[1] + [bX[m]], writes=[bX[m]])

        up_block(0)
        for b in range(8):
            if b + 1 < 8:
                up_block(b + 1)
            down_block(b)

    try:
        for l in range(nl):
            layer(l)
    except _Stop:
        pass
    names = dict(xT=(xT, bX), hT=(hT, bH), qT=(qT, bQ), kT=(kT, bK), vT=(vT, bV), zT=(zT, bZ), oT=(oT, bH[0:8] + [bf('osample')]),
                 tA=(tA, [bTA]), tB=(tB, [bTB]), tC=(tCc, [bTC]), tU=(tU, [bTU]))
    for dn in dumps:
        t_, bb_ = names[dn]
        shp = list(t_.shape)
        do = dt("dbg_" + dn, shp, F32, kind="ExternalOutput").ap()
        P.dma("pool", (lambda t_, do: lambda e: e.dma_start(out=do, in_=t_[:]))(t_, do), reads=bb_)
    P.dma("sp", lambda e: e.dma_start(out=yout, in_=xT[:]), reads=bX)

    with nc.Block() as block:
        @block.tensor
        def _(e):
            P.emit("pe", e)

        @block.scalar
        def _(e):
            P.emit("act", e)

        @block.vector
        def _(e):
            P.emit("dve", e)

        @block.gpsimd
        def _(e):
            P.emit("pool", e)

        @block.sync
        def _(e):
            P.emit("sp", e)
    return nc


def _weight_stream(w_in, w_out, w_up, w_down):
    out = np.empty((L * PPL, 128, 2048), np.float32)
    i = 0

    def colpiece(W, c0):
        return W[:, c0:c0 + 128].reshape(16, 128, 128).transpose(1, 0, 2).reshape(128, 2048)

    for l in range(L):
        wi = w_in[l]
        order = [1024 + 128 * c for c in range(8)] + [2048 + 128 * c for c in range(8)]
        for c in range(8):
            order += [3072 + 128 * c, 5120 + 128 * c, 4096 + 128 * c]
        order += [128 * c for c in range(8)]
        for c0 in order:
            out[i] = colpiece(wi, c0); i += 1
        for m in range(16):
            out[i] = colpiece(w_out[l], 128 * m); i += 1

        def up(b):
            nonlocal i
            for j in range(8):
                out[i] = colpiece(w_up[l], (8 * b + j) * 128); i += 1

        def down(b):
            nonlocal i
            blk = w_down[l][b * 1024:(b + 1) * 1024]
            for g in range(8):
                out[i] = blk[:, g * 256:(g + 1) * 256].reshape(8, 128, 256).transpose(1, 0, 2).reshape(128, 2048); i += 1
        up(0)
        for b in range(8):
            if b + 1 < 8:
                up(b + 1)
            down(b)
    assert i == L * PPL
    return out


_NC_CACHE = {}
_PREP_ONLY = False


def kernel(x_prompt, x_sample, state_attn_k, state_attn_v, state_conv, rel_bias, norm_mix, w_in, q_norm, k_norm,
           conv_w, attn_out_norm, conv_out_norm, w_out, norm_mlp, w_up, w_down):
    f = lambda a: np.ascontiguousarray(np.asarray(a, dtype=np.float32))
    x_prompt, x_sample, state_attn_k, state_attn_v, state_conv = map(f, (x_prompt, x_sample, state_attn_k, state_attn_v, state_conv))
    rel_bias, norm_mix, q_norm, k_norm, conv_w, attn_out_norm, conv_out_norm, norm_mlp = map(
        f, (rel_bias, norm_mix, q_norm, k_norm, conv_w, attn_out_norm, conv_out_norm, norm_mlp))
    wst = _weight_stream(f(w_in), f(w_out), f(w_up), f(w_down))
    kk = np.arange(128)[:, None]
    cc = np.arange(TW)[None, :]
    dist = cc - kk
    valid = (dist >= 0) & (dist <= 2048)
    dc = np.clip(dist, 0, 2048)
    bidx = _bucket(dc)
    btoe = np.ascontiguousarray(rel_bias[bidx].transpose(2, 0, 1))
    mult = ((dc <= 128).astype(np.float32) + ((dc % 4 == 0) & (dc <= 512)) + ((dc % 16 == 0) & (dc <= 2048))) * valid
    mtoe = mult.astype(np.float32)
    sbias = np.zeros((128, 48), np.float32)
    for br, d in enumerate((1, 4, 16)):
        j = 128 - np.arange(128)
        sbias[:, br * 16:(br + 1) * 16] = rel_bias[_bucket(j * d)]
    sb0 = np.ascontiguousarray(rel_bias[0:1, :])
    cst = np.zeros((128, 4, 128), np.float32)
    cst[:, 0] = np.eye(128)
    cst[:, 1] = 1.0
    cst[0:64, 2, 0:64] = 1.0
    cst[64:128, 2, 64:128] = 1.0
    cst[:, 3] = np.eye(128)
    in_maps = []
    for core in range(8):
        b, hf = core // 2, core % 2
        xin = np.empty((128, 16, TC), np.float32)
        xin[:, :, 0:T] = x_prompt[b, hf * T:(hf + 1) * T].reshape(T, 16, 128).transpose(2, 1, 0)
        xin[:, :, T] = x_sample[core, 0].reshape(16, 128).T
        par = np.zeros((128, 304), np.float32)
        for l in range(L):
            pb = l * NPARL
            par[:, pb:pb + 16] = norm_mix[l].reshape(16, 128).T
            par[:, pb + 16:pb + 32] = norm_mlp[l].reshape(16, 128).T
            par[:, pb + 32] = np.tile(q_norm[l], 2)
            par[:, pb + 33] = np.tile(k_norm[l], 2)
            for i in range(3):
                par[:, pb + 34 + 8 * i:pb + 42 + 8 * i] = conv_w[l, i].reshape(8, 128).T
            par[:, pb + 58:pb + 66] = attn_out_norm[l].reshape(8, 128).T
            par[:, pb + 66:pb + 74] = conv_out_norm[l].reshape(8, 128).T
        par[:, 296] = float(hf)
        sconv = np.ascontiguousarray(state_conv[:, core].reshape(L, 2, 8, 128).transpose(3, 0, 2, 1))
        in_maps.append({
            "xin": xin, "wst": wst, "par": par, "sconv": sconv, "btoe": btoe, "mtoe": mtoe, "sbias": sbias, "sb0": sb0,
            "cst": cst, "kst": np.ascontiguousarray(state_attn_k[:, core].reshape(L, 2048, 1024)),
            "vst": np.ascontiguousarray(state_attn_v[:, core].reshape(L, 2048, 1024)),
        })
    if _PREP_ONLY:
        return in_maps
    if "nc" not in _NC_CACHE:
        _NC_CACHE["nc"] = build()
    res = run_bass_kernel_spmd(_NC_CACHE["nc"], in_maps, core_ids=list(range(8))).results
    y_p = np.empty((4, 2048, 2048), np.float32)
    y_s = np.empty((8, 1, 2048), np.float32)
    nk = np.empty((L, 4, 2048, 16, 64), np.float32)
    nv = np.empty((L, 4, 2048, 16, 64), np.float32)
    ncp = np.empty((L, 4, 2, 1024), np.float32)
    nks = np.empty((L, 8, 2048, 16, 64), np.float32)
    nvs = np.empty((L, 8, 2048, 16, 64), np.float32)
    ncs = np.empty((L, 8, 2, 1024), np.float32)
    for core in range(8):
        b, hf = core // 2, core % 2
        r = res[core]
        yo = r["yout"]
        y_p[b, hf * T:(hf + 1) * T] = yo[:, :, 0:T].transpose(2, 1, 0).reshape(T, 2048)
        y_s[core, 0] = yo[:, :, T].T.reshape(2048)
        nk[:, b, hf * T:(hf + 1) * T] = r["kTo"].transpose(0, 3, 2, 1).reshape(L, T, 16, 64)
        nv[:, b, hf * T:(hf + 1) * T] = r["vTo"].transpose(0, 3, 2, 1).reshape(L, T, 16, 64)
        if hf == 1:
            ncp[:, b] = r["cpo"].transpose(0, 3, 2, 1).reshape(L, 2, 1024)
        nks[:, core] = r["kso"].reshape(L, 2048, 16, 64)
        nvs[:, core] = r["vso"].reshape(L, 2048, 16, 64)
        ncs[:, core] = r["cso"].transpose(0, 3, 2, 1).reshape(L, 2, 1024)
    return (y_p, y_s, nk, nv, ncp, nks, nvs, ncs)
```

```python
import numpy as np
import concourse.bass as bass
import concourse.mybir as mybir
from concourse.bass_utils import run_bass_kernel_spmd

F32 = mybir.dt.float32
BF16 = mybir.dt.bfloat16
ALU = mybir.AluOpType
AF = mybir.ActivationFunctionType
AX = mybir.AxisListType

L = 4
T = 1024
TC = 1025
NSLOT = 5
PPL = 192
NPARL = 74
EPS = 1e-6
TW = 2176
TILES = ((0, 512), (512, 1024), (1024, 1025))


def _bucket(dist):
    n_exact = 16
    large = n_exact + (np.log(np.maximum(dist, 1) / n_exact) / np.log(2048 / n_exact) * (32 - n_exact)).astype(np.int32)
    large = np.minimum(large, 31)
    return np.where(dist < n_exact, dist, large).astype(np.int32)


ATTACH = True


class Buf:
    __slots__ = ("w", "r", "name")

    def __init__(self, name=""):
        self.w = None
        self.r = []
        self.name = name


class Plan:
    ENG = ("pe", "act", "dve", "pool", "sp")

    def __init__(self, nc, semfn):
        self.nc = nc
        self.semfn = semfn
        self.q = {e: [] for e in self.ENG}
        self.cnt = {e: 0 for e in self.ENG}
        self.sem = {e: semfn("c_" + e) for e in ("pe", "act", "dve", "pool")}
        self.waited = {e: {} for e in self.ENG}
        nds = 12
        self.dsems = {e: [semfn("d_%s%d" % (e, i)) for i in range(nds)] for e in ("pool", "sp")}
        self.dval = {e: [0] * nds for e in ("pool", "sp")}
        self.dslot = {e: 0 for e in ("pool", "sp")}

    def new_layer_sems(self, l):
        for e in ("pe", "act", "dve", "pool"):
            self.sem[e] = self.semfn("c_%s_%d" % (e, l))
            self.cnt[e] = 0

    def _waits(self, eng, reads, writes, extra):
        need = {}
        lst = list(extra)
        for b in reads:
            if b.w is not None:
                lst.append(b.w)
        for b in writes:
            if b.w is not None:
                lst.append(b.w)
            lst.extend(b.r)
        for t in lst:
            if t is None:
                continue
            k = id(t[0])
            if k not in need or need[k][1] < t[1]:
                need[k] = t
        final = []
        for k, (sem, val) in need.items():
            if self.waited[eng].get(k, 0) >= val:
                continue
            self.waited[eng][k] = val
            final.append((sem, val))
        return final

    def op(self, eng, fn, reads=(), writes=(), extra=(), lhs=None):
        final = self._waits(eng, reads, writes, extra)
        att = None
        if ATTACH and final and (eng != "pe" or lhs is not None):
            lhs_ids = set(id(b.w[0]) for b in (lhs or ()) if b.w is not None)
            cand = [w for w in final if id(w[0]) not in lhs_ids]
            if cand:
                att = cand[-1]
                final = [w for w in final if w is not att]
        self.cnt[eng] += 1
        tk = (self.sem[eng], self.cnt[eng])
        self.q[eng].append((final, fn, tk, 1, att))
        for b in reads:
            b.r.append(tk)
        for b in writes:
            b.w = tk
            b.r = []
        return tk

    def dma(self, eng, fn, reads=(), writes=(), extra=()):
        slot = self.dslot[eng]
        self.dslot[eng] = (slot + 1) % len(self.dsems[eng])
        sem = self.dsems[eng][slot]
        prev = self.dval[eng][slot]
        ex = list(extra)
        if prev > 0:
            ex.append((sem, prev))
        final = self._waits(eng, reads, writes, ex)
        self.dval[eng][slot] = prev + 16
        tk = (sem, prev + 16)
        self.q[eng].append((final, fn, tk, 16, None))
        for b in reads:
            b.r.append(tk)
        for b in writes:
            b.w = tk
            b.r = []
        return tk

    def coll(self, fn, reads=(), writes=()):
        sem = self.semfn("cc%d" % len(self.q["pool"]))
        final = self._waits("pool", reads, writes, ())
        tk = (sem, 1)
        self.q["pool"].append((final, fn, tk, 0, None))
        for b in reads:
            b.r.append(tk)
        for b in writes:
            b.w = tk
            b.r = []
        return tk

    def emit(self, eng, e):
        for final, fn, tk, inc, att in self.q[eng]:
            for sem, val in final:
                e.wait_ge(sem, val)
            ins = fn(e)
            if isinstance(ins, tuple):
                first, ins = ins
            else:
                first = ins
            if att is not None:
                first._wait_ge(att[0], att[1])
            if inc == 0:
                ins.then_inc(tk[0])
            else:
                ins.then_inc(tk[0], inc)
        if eng in ("pool", "sp"):
            for i, sem in enumerate(self.dsems[eng]):
                if self.dval[eng][i] > 0:
                    e.wait_ge(sem, self.dval[eng][i])


class _Stop(Exception):
    pass


def build(nl=L, stop=None, dumps=(), npieces=None):
    nc = bass.Bass("TRN2", target_bir_lowering=False)

    def mark(name):
        if stop is not None and name == stop:
            raise _Stop()
    dt = nc.dram_tensor
    xin = dt("xin", [128, 16, TC], F32, kind="ExternalInput").ap()
    wst = dt("wst", [npieces or nl * PPL, 128, 2048], F32, kind="ExternalInput").ap()
    par = dt("par", [128, 304], F32, kind="ExternalInput").ap()
    sconv = dt("sconv", [128, L, 8, 2], F32, kind="ExternalInput").ap()
    btoe = dt("btoe", [16, 128, TW], F32, kind="ExternalInput").ap()
    mtoe = dt("mtoe", [128, TW], F32, kind="ExternalInput").ap()
    sbias = dt("sbias", [128, 48], F32, kind="ExternalInput").ap()
    sb0 = dt("sb0", [1, 16], F32, kind="ExternalInput").ap()
    cst = dt("cst", [128, 4, 128], F32, kind="ExternalInput").ap()
    kst = dt("kst", [nl, 2048, 1024], F32, kind="ExternalInput").ap()
    vst = dt("vst", [nl, 2048, 1024], F32, kind="ExternalInput").ap()
    yout = dt("yout", [128, 16, TC], F32, kind="ExternalOutput").ap()
    kTo = dt("kTo", [L, 128, 8, T], F32, kind="ExternalOutput").ap()
    vTo = dt("vTo", [L, 128, 8, T], F32, kind="ExternalOutput").ap()
    cpo = dt("cpo", [L, 128, 8, 2], F32, kind="ExternalOutput").ap()
    kso = dt("kso", [nl, 2048, 1024], F32, kind="ExternalOutput").ap()
    vso = dt("vso", [nl, 2048, 1024], F32, kind="ExternalOutput").ap()
    cso = dt("cso", [L, 128, 8, 2], F32, kind="ExternalOutput").ap()
    xsp = dt("xsp", [128, 16, TC], F32).ap()
    bins = [[dt("bin%d_%d" % (l, c), [256, T], BF16) for c in range(8)] for l in range(L)]
    bouts = [[dt("bout%d_%d" % (l, c), [512, T], BF16) for c in range(8)] for l in range(L)]
    bin2s = [dt("binb%d" % l, [128, 16], F32) for l in range(L)]
    bout2s = [dt("boutb%d" % l, [256, 16], F32) for l in range(L)]

    off = [16512]

    def alloc(name, shape, dtp, at=None):
        esz = 2 if dtp == BF16 else 4
        nb = int(np.prod(shape[1:])) * esz
        nb = (nb + 31) // 32 * 32
        if at is None:
            o = off[0]
            off[0] += nb
        else:
            o = at
        return nc.alloc_sbuf_tensor_at(name, list(shape), dtp, offset=o), o, nb

    ring = []
    for i in range(NSLOT):
        t_, _, _ = alloc("ring%d" % i, [128, 2048], BF16)
        ring.append(t_)
    identb, _, _ = alloc("identb", [128, 128], BF16)
    onesb, _, _ = alloc("onesb", [128, 128], BF16)
    blkb, _, _ = alloc("blkb", [128, 128], BF16)
    cstf, _, _ = alloc("cstf", [128, 2, 128], F32)
    parS, _, _ = alloc("parS", [128, 304], F32)
    scS, _, _ = alloc("scS", [128, L, 8, 2], F32)
    sbS, _, _ = alloc("sbS", [128, 48], F32)
    sb0S, _, _ = alloc("sb0S", [1, 16], F32)
    mtS, _, _ = alloc("mtS", [128, TW], BF16)
    tail, _, _ = alloc("tail", [128, 8, 2], F32)
    y02, _, _ = alloc("y02", [128, 8, 2], F32)
    gb02, _, _ = alloc("gb02", [128, 8, 2], F32)
    csst, _, _ = alloc("csst", [128, 8, 2], F32)
    ptl, _, _ = alloc("ptl", [128, 8, 2], F32)
    sm1, _, _ = alloc("sm1", [128, 8, 8], F32)
    epsS, _, _ = alloc("epsS", [128, 1], F32)
    xT, XO, _ = alloc("xT", [128, 16, TC], F32)
    hT, HO, _ = alloc("hT", [128, 16, TC], BF16)
    off[0] += 32
    qT, QO, _ = alloc("qT", [128, 8, TC], BF16)
    kT, KO, _ = alloc("kT", [128, 8, TC], BF16)
    vT, _, _ = alloc("vT", [128, 8, TC], BF16)
    zT, _, _ = alloc("zT", [128, 8, TC], BF16)
    tA, _, _ = alloc("tA", [128, 1032], F32)
    tB, _, _ = alloc("tB", [128, 1032], F32)
    tCc, _, _ = alloc("tC", [128, 1032], F32)
    tU, _, _ = alloc("tU", [128, 1032], F32)
    tD, _, _ = alloc("tD", [128, 1032], BF16)
    assert off[0] <= 229376, off[0]
    oT, _, _ = alloc("oT", [128, 8, TC], BF16, at=HO)
    sqs, _, _ = alloc("sqs", [128, 8, TC], BF16, at=HO + 16416)
    hid = [alloc("hid0", [128, 8, TC], BF16, at=QO)[0], alloc("hid1", [128, 8, TC], BF16, at=KO)[0]]
    xo = [XO]

    def xalloc(name, shape, dtp):
        t_, o, nb = alloc(name, shape, dtp, at=xo[0])
        xo[0] += nb
        return t_

    Vown = [xalloc("Vown%d" % i, [128, 8, 192], BF16) for i in range(2)]
    Vprv = [xalloc("Vprv%d" % i, [128, 8, 192], BF16) for i in range(2)]
    kTp = [xalloc("kTp%d" % i, [128, T], BF16) for i in range(2)]
    vTp = [xalloc("vTp%d" % i, [128, T], BF16) for i in range(2)]
    Th = [xalloc("Th%d" % i, [128, TW], BF16) for i in range(4)]
    Eb = [xalloc("Eb%d" % i, [128, T], BF16) for i in range(2)]
    Pb = [xalloc("Pb%d" % i, [128, T], BF16) for i in range(2)]
    bst = xalloc("bst", [128, TW], F32)
    rden = xalloc("rden", [128, T], F32)
    vrow = xalloc("vrow", [1, T], F32)
    assert xo[0] <= XO + 65600, xo[0] - XO
    ho = [HO + 16416]

    def halloc(name, shape, dtp):
        t_, o, nb = alloc(name, shape, dtp, at=ho[0])
        ho[0] += nb
        return t_

    qb = halloc("qb", [128, T], F32)
    Kt = halloc("Kt", [128, T], F32)
    Vt2 = halloc("Vt2", [128, T], F32)
    assert ho[0] <= HO + 32832

    sems = []

    def semfn(name):
        s = nc.semaphore(name)
        h = s.__enter__()
        sems.append(s)
        return h

    ps_cm = nc.psum_tensor("ps", [128, 8, 512], F32)
    ps = ps_cm.__enter__()
    psf = ps[:].rearrange("p b n -> p (b n)")
    P = Plan(nc, semfn)

    B = {}

    def bf(name):
        if name not in B:
            B[name] = Buf(name)
        return B[name]

    def bl(prefix, n):
        return [bf("%s%d" % (prefix, i)) for i in range(n)]

    bX = bl("x", 16)
    bH = bl("h", 16)
    bQ = bl("q", 8)
    bK = bl("k", 8)
    bV = bl("v", 8)
    bZ = bl("z", 8)
    bRing = bl("ring", NSLOT)
    bBank = bl("bank", 8)
    bTA, bTB, bTC, bTU, bTD = bf("tA"), bf("tB"), bf("tC"), bf("tU"), bf("tD")
    bHid = [bl("hid0_", 8), bl("hid1_", 8)]
    accs = [(psf[:, 0:TC], [bBank[0], bBank[1], bBank[2]]), (psf[:, 1536:1536 + TC], [bBank[3], bBank[4], bBank[5]])]
    STAT = (psf[:, 3072:4096], [bBank[6], bBank[7]])

    def pv(i):
        return parS[:, i:i + 1]

    bC = bf("consts")
    P.dma("sp", lambda e: e.dma_start(out=xT[:], in_=xin), writes=bX)
    P.dma("sp", lambda e: e.dma_start(out=parS[:], in_=par), writes=[bC])
    P.dma("sp", lambda e: e.dma_start(out=scS[:], in_=sconv), writes=[bC])
    P.dma("sp", lambda e: e.dma_start(out=sbS[:], in_=sbias), writes=[bC])
    P.dma("sp", lambda e: e.dma_start(out=sb0S[:], in_=sb0), writes=[bC])
    P.dma("sp", lambda e: e.dma_start(out=cstf[:, 0, :], in_=cst[:, 0, :]), writes=[bC])
    P.dma("sp", lambda e: e.dma_start(out=cstf[:, 1, :], in_=cst[:, 1, :]), writes=[bC])
    P.dma("pool", lambda e: e.dma_start(out=identb[:], in_=cst[:, 0, :]), writes=[bC])
    P.dma("pool", lambda e: e.dma_start(out=onesb[:], in_=cst[:, 1, :]), writes=[bC])
    P.dma("pool", lambda e: e.dma_start(out=blkb[:], in_=cst[:, 2, :]), writes=[bC])
    P.dma("pool", lambda e: e.dma_start(out=mtS[:], in_=mtoe), writes=[bC])
    identf = cstf[:, 0, :]
    onesf = cstf[:, 1, :]
    P.op("dve", lambda e: e.memset(epsS[:], EPS), writes=[bC])
    P.op("dve", lambda e: e.memset(tU[:, 0:2], 0.0), writes=[bTU])
    FLAG = 296
    bVo = [bl("vo0_", 1), bl("vo1_", 1)]
    bVp = [bl("vp0_", 1), bl("vp1_", 1)]
    bKp = [bl("kTp0_", 1), bl("kTp1_", 1)]
    bVTp = [bl("vTp0_", 1), bl("vTp1_", 1)]
    bTh = [bf("Th%d" % i) for i in range(4)]
    bBst = bf("bst")
    bE = [bf("E0"), bf("E1")]
    bP = [bf("P0"), bf("P1")]
    bRd = bf("rden")
    bVr = bf("vrow")
    bSm = bf("sm1")
    XAL0 = [bVo[0][0], bVo[1][0], bVp[0][0], bVp[1][0], bKp[0][0], bKp[1][0], bVTp[0][0], bVTp[1][0]] + bTh + bE + bP + [bBst, bRd, bVr]
    XAL = XAL0 + [bf("E4_%d" % i) for i in range(4)] + [bf("P4_%d" % i) for i in range(4)]
    QROW = tCc[0:1, 0:T]
    KROW = tU[0:1, 2:2 + T]
    VROW = vrow[0:1, :]
    ROWS = [QROW, KROW, VROW]

    ws = {"next": 0, "cons": 0}

    def fetch_upto(n):
        while ws["next"] < min(n, npieces or nl * PPL):
            i = ws["next"]
            s = i % NSLOT
            P.dma("pool", (lambda i, s: lambda e: e.dma_start(out=ring[s][:], in_=wst[i]))(i, s), writes=[bRing[s]])
            ws["next"] += 1

    def next_piece():
        i = ws["cons"]
        ws["cons"] += 1
        fetch_upto(i + NSLOT)
        return i % NSLOT

    def mm_piece(slot, srcs, sbufs, acc, nk=16, wcol=lambda kc: (kc * 128, kc * 128 + 128)):
        accap, accb = acc

        def fn(e):
            ins = None
            first = None
            for kc in range(nk):
                a, b = wcol(kc)
                for lo, hi in TILES:
                    ins = e.matmul(accap[:, lo:hi], lhsT=ring[slot][:, a:b], rhs=srcs[kc][:, lo:hi],
                                   start=(kc == 0), stop=(kc == nk - 1))
                    if first is None:
                        first = ins
            return (first, ins)
        return P.op("pe", fn, reads=[bRing[slot]] + list(sbufs), writes=accb, lhs=[bRing[slot]])

    def stat_mm(lhs, srcs, sbufs, acc):
        accap, accb = acc
        n = len(srcs)

        def fn(e):
            ins = None
            for kc in range(n):
                for lo, hi in TILES:
                    ins = e.matmul(accap[:, lo:hi], lhsT=lhs[:], rhs=srcs[kc][:, lo:hi], start=(kc == 0), stop=(kc == n - 1))
            return ins
        return P.op("pe", fn, reads=[bC] + list(sbufs), writes=accb)

    def rstd_from(acc, out_t, out_b, scale):
        accap, accb = acc
        P.op("act", lambda e: e.activation(out=out_t[:, 0:TC], in_=accap, func=AF.Sqrt, bias=epsS[:], scale=scale),
             reads=accb + [bC], writes=[out_b])
        P.op("dve", lambda e: e.reciprocal(out=out_t[:, 0:TC], in_=out_t[:, 0:TC]), reads=[out_b], writes=[out_b])

    def rmsnorm_x(gbase):
        for c in range(16):
            P.op("act", (lambda c: lambda e: e.activation(out=hT[:, c, :], in_=xT[:, c, :], func=AF.Square))(c),
                 reads=[bX[c]], writes=[bH[c]])
        stat_mm(onesb, [hT[:, c, :] for c in range(16)], bH, accs[0])
        rstd_from(accs[0], tA, bTA, 1.0 / 2048)
        for c in range(16):
            eng = "dve"
            P.op(eng, (lambda c: lambda e: e.scalar_tensor_tensor(out=hT[:, c, :], in0=xT[:, c, :], scalar=pv(gbase + c),
                                                                  in1=tA[:, 0:TC], op0=ALU.mult, op1=ALU.mult))(c),
                 reads=[bX[c], bTA, bC], writes=[bH[c]])

    hsrc = [hT[:, c, :] for c in range(16)]
    accsel = [0]

    def nextacc():
        accsel[0] ^= 1
        return accs[accsel[0]]

    def layer(l):
        if l > 0:
            P.new_layer_sems(l)
        pb = l * NPARL
        G_MIX, G_MLP, G_Q, G_K, G_CW, G_GA, G_GC = pb, pb + 16, pb + 32, pb + 33, pb + 34, pb + 58, pb + 66
        for (dst_, src_) in ((kso, kst), (vso, vst)):
            P.dma("sp", (lambda l, dst_, src_: lambda e: e.dma_start(
                out=dst_[l, 0:2047, :].rearrange("(a b) c -> a (b c)", b=23),
                in_=src_[l, 1:2048, :].rearrange("(a b) c -> a (b c)", b=23)))(l, dst_, src_))
        rmsnorm_x(G_MIX)
        sp_tk = P.dma("sp", lambda e: e.dma_start(out=xsp, in_=xT[:]), reads=bX)
        for b_ in XAL:
            b_.w = sp_tk
            b_.r = []
        for i in range(2):
            P.op("dve", (lambda i: lambda e: e.memset(Vown[i][:, :, 64:128], 1.0))(i), writes=bVo[i])
            P.op("dve", (lambda i: lambda e: e.memset(Vprv[i][:, :, 64:128], 1.0))(i), writes=bVp[i])
            P.op("dve", (lambda i: lambda e: e.tensor_scalar(out=Vprv[i][:, :, 64:128], in0=Vprv[i][:, :, 64:128],
                                                              scalar1=pv(FLAG), scalar2=None, op0=ALU.mult))(i),
                 reads=[bC], writes=bVp[i])

        mark('p0')
        def qk_piece(dst, dbuf, c, gidx):
            slot = next_piece()
            mm_piece(slot, hsrc, bH, accs[0])
            P.op("dve", lambda e: e.tensor_copy(out=tA[:, 0:TC], in_=accs[0][0]), reads=accs[0][1], writes=[bTA])
            P.op("act", lambda e: e.activation(out=tD[:, 0:TC], in_=tA[:, 0:TC], func=AF.Square), reads=[bTA], writes=[bTD])
            stat_mm(blkb, [tD[:, 0:TC]], [bTD], accs[1])
            rstd_from(accs[1], tB, bTB, 1.0 / 64)
            P.op("dve", lambda e: e.scalar_tensor_tensor(out=dst[:, c, :], in0=tA[:, 0:TC], scalar=pv(gidx), in1=tB[:, 0:TC],
                                                         op0=ALU.mult, op1=ALU.mult),
                 reads=[bTA, bTB, bC], writes=[dbuf[c]])

        for c in range(8):
            qk_piece(kT, bK, c, G_K)
        bBin = [bf("bin%d_%d" % (l, c)) for c in range(8)]
        bBout = [bf("bout%d_%d" % (l, c)) for c in range(8)]
        for c in range(8):
            slot = next_piece()
            acc = nextacc()
            mm_piece(slot, hsrc, bH, acc)
            P.op("act", (lambda c, acc: lambda e: e.activation(out=vT[:, c, :], in_=acc[0], func=AF.Copy))(c, acc),
                 reads=acc[1], writes=[bV[c]])
            P.dma("sp", (lambda l, c: lambda e: e.dma_start(out=bins[l][c].ap()[0:128, :], in_=kT[:, c, 0:T]))(l, c),
                  reads=[bK[c]], writes=[bBin[c]])
            P.dma("sp", (lambda l, c: lambda e: e.dma_start(out=bins[l][c].ap()[128:256, :], in_=vT[:, c, 0:T]))(l, c),
                  reads=[bV[c]], writes=[bBin[c]])
            P.coll((lambda l, c: lambda e: e.collective_compute("AllGather", ALU.bypass,
                                                                replica_groups=[[0, 1], [2, 3], [4, 5], [6, 7]],
                                                                ins=[bins[l][c].ap()], outs=[bouts[l][c].ap()]))(l, c),
                   reads=[bBin[c]], writes=[bBout[c]])
        mark('x2')
        P.dma("pool", (lambda l: lambda e: e.dma_start(out=kTo[l], in_=kT[:, :, 0:T]))(l), reads=bK)
        P.dma("pool", (lambda l: lambda e: e.dma_start(out=vTo[l], in_=vT[:, :, 0:T]))(l), reads=bV)
        mark('xchg')
        bTail, bY02, bG02, bCs = bf("tail"), bf("y02"), bf("gb02"), bf("csst")
        for c in range(8):
            w0, w1, w2 = pv(G_CW + c), pv(G_CW + 8 + c), pv(G_CW + 16 + c)
            slot = next_piece(); acc = nextacc()
            mm_piece(slot, hsrc, bH, acc)
            P.op("act", (lambda acc: lambda e: e.activation(out=tA[:, 0:TC], in_=acc[0], func=AF.Copy))(acc),
                 reads=acc[1], writes=[bTA])
            mark('cv1')
            slot = next_piece(); acc = nextacc()
            mm_piece(slot, hsrc, bH, acc)
            P.op("dve", (lambda acc: lambda e: e.tensor_tensor(out=tU[:, 2:2 + TC], in0=acc[0], in1=tA[:, 0:TC], op=ALU.mult))(acc),
                 reads=acc[1] + [bTA], writes=[bTU])
            mark('cv2')
            P.op("act", (lambda c: lambda e: e.activation(out=tail[:, c, :], in_=tU[:, 1024:1026], func=AF.Copy))(c),
                 reads=[bTU], writes=[bTail])
            P.op("act", (lambda c: lambda e: e.activation(out=csst[:, c, 1:2], in_=tU[:, 1026:1027], func=AF.Copy))(c),
                 reads=[bTU], writes=[bCs])
            P.op("act", (lambda c, l: lambda e: e.activation(out=csst[:, c, 0:1], in_=scS[:, l, c, 1:2], func=AF.Copy))(c, l),
                 reads=[bC], writes=[bCs])
            mark('cv3')
            P.op("dve", (lambda w2: lambda e: e.tensor_scalar(out=tCc[:, 0:T], in0=tU[:, 2:2 + T], scalar1=w2, scalar2=None,
                                                              op0=ALU.mult))(w2), reads=[bTU, bC], writes=[bTC])
            P.op("dve", (lambda w1: lambda e: e.scalar_tensor_tensor(out=tCc[:, 0:T], in0=tU[:, 1:1 + T], scalar=w1, in1=tCc[:, 0:T],
                                                                     op0=ALU.mult, op1=ALU.add))(w1), reads=[bTU, bTC, bC], writes=[bTC])
            P.op("dve", (lambda w0: lambda e: e.scalar_tensor_tensor(out=tCc[:, 0:T], in0=tU[:, 0:T], scalar=w0, in1=tCc[:, 0:T],
                                                                     op0=ALU.mult, op1=ALU.add))(w0), reads=[bTU, bTC, bC], writes=[bTC])
            mark('cv4')
            P.op("act", (lambda c, l, w0: lambda e: e.activation(out=tCc[:, T:TC], in_=scS[:, l, c, 0:1], func=AF.Identity, scale=w0))(c, l, w0),
                 reads=[bC], writes=[bTC])
            P.op("act", (lambda c, l, w1: lambda e: e.activation(out=tCc[:, T:TC], in_=scS[:, l, c, 1:2], func=AF.Identity, scale=w1,
                                                                 bias=tCc[:, T:TC]))(c, l, w1), reads=[bC, bTC], writes=[bTC])
            P.op("act", (lambda w2: lambda e: e.activation(out=tCc[:, T:TC], in_=tU[:, 1026:1027], func=AF.Identity, scale=w2,
                                                           bias=tCc[:, T:TC]))(w2), reads=[bTU, bTC, bC], writes=[bTC])
            mark('cv5')
            P.op("act", (lambda c: lambda e: e.activation(out=y02[:, c, :], in_=tCc[:, 0:2], func=AF.Copy))(c),
                 reads=[bTC], writes=[bY02])
            slot = next_piece(); acc = nextacc()
            mm_piece(slot, hsrc, bH, acc)
            P.op("dve", (lambda c, acc: lambda e: e.tensor_tensor(out=zT[:, c, :], in0=acc[0], in1=tCc[:, 0:TC], op=ALU.mult))(c, acc),
                 reads=acc[1] + [bTC], writes=[bZ[c]])
            P.op("act", (lambda c, acc: lambda e: e.activation(out=gb02[:, c, :], in_=acc[0][:, 0:2], func=AF.Copy))(c, acc),
                 reads=[], writes=[bG02] + acc[1])
        mark('conv')
        bB2i, bB2o = bf("b2i%d" % l), bf("b2o%d" % l)
        P.dma("sp", (lambda l: lambda e: e.dma_start(out=bin2s[l].ap().rearrange("p (c r) -> p c r", r=2), in_=tail[:]))(l),
              reads=[bTail], writes=[bB2i])
        P.coll((lambda l: lambda e: e.collective_compute("AllGather", ALU.bypass,
                                                         replica_groups=[[0, 1], [2, 3], [4, 5], [6, 7]],
                                                         ins=[bin2s[l].ap()], outs=[bout2s[l].ap()]))(l),
               reads=[bB2i], writes=[bB2o])
        P.dma("sp", (lambda l: lambda e: e.dma_start(out=cpo[l], in_=tail[:]))(l), reads=[bTail])
        P.dma("sp", (lambda l: lambda e: e.dma_start(out=cso[l], in_=csst[:]))(l), reads=[bCs])
        for c in range(8):
            qk_piece(qT, bQ, c, G_Q)

        mark('p1')
        bO = bH[0:8]
        bS2 = bH[8:16]
        bank6, bank7 = psf[:, 3072:3584], psf[:, 3584:4096]
        for ri, (src, sb_) in enumerate(((qT, bQ), (kT, bK), (vT, bV))):
            def fn(e, src=src):
                ins = None
                for c in range(8):
                    ins = e.matmul(psf[0:1, 3072 + c * 128:3072 + (c + 1) * 128], lhsT=src[:, c, T:TC], rhs=identb[:],
                                   start=True, stop=True)
                return ins
            P.op("pe", fn, reads=list(sb_) + [bC], writes=[bBank[6], bBank[7]])
            P.op("act", (lambda ri: lambda e: e.activation(out=ROWS[ri], in_=psf[0:1, 3072:4096], func=AF.Copy))(ri),
                 reads=[bBank[6], bBank[7]], writes=bS2 + [bTC, bTU, bVr])
        P.dma("sp", (lambda l: lambda e: e.dma_start(out=kso[l, 2047:2048, :], in_=KROW))(l), reads=[bTU])
        P.dma("sp", (lambda l: lambda e: e.dma_start(out=vso[l, 2047:2048, :], in_=VROW))(l), reads=[bVr])

        def fn(e):
            e.matmul(bank6, lhsT=onesf[0:1, :], rhs=QROW[:, 0:512], start=True, stop=True)
            return e.matmul(bank7, lhsT=onesf[0:1, :], rhs=QROW[:, 512:1024], start=True, stop=True)
        P.op("pe", fn, reads=bS2 + [bC, bTC], writes=[bBank[6], bBank[7]])
        P.op("act", lambda e: e.activation(out=qb[:], in_=psf[:, 3072:4096], func=AF.Copy), reads=[bBank[6], bBank[7]], writes=bS2)
        lg = sm1[:, 0:6, :].rearrange("p a b -> p (a b)")
        pe_ = tB[:, 0:48]
        for br, d in enumerate((1, 4, 16)):
            P.dma("sp", (lambda l, d: lambda e: e.dma_start(out=Kt[:], in_=kst[l, 2048 - 128 * d:2048:d, :]))(l, d), writes=bS2)
            P.op("dve", lambda e: e.tensor_tensor(out=Kt[:], in0=Kt[:], in1=qb[:], op=ALU.mult), reads=bS2, writes=bS2)
            P.op("dve", (lambda br: lambda e: e.tensor_reduce(out=lg[:, br * 16:(br + 1) * 16],
                                                              in_=Kt[:].rearrange("p (h e) -> p h e", e=64), axis=AX.X, op=ALU.add))(br),
                 reads=bS2, writes=[bSm])
        P.op("dve", lambda e: e.scalar_tensor_tensor(out=lg, in0=lg, scalar=0.125, in1=sbS[:], op0=ALU.mult, op1=ALU.add),
             reads=[bC, bSm], writes=[bSm])
        P.op("act", lambda e: e.activation(out=pe_, in_=lg, func=AF.Exp), reads=[bSm], writes=[bTB])
        for br, d in enumerate((1, 4, 16)):
            P.dma("sp", (lambda l, d: lambda e: e.dma_start(out=Vt2[:], in_=vst[l, 2048 - 128 * d:2048:d, :]))(l, d), writes=bS2)
            P.op("dve", (lambda br: lambda e: e.tensor_tensor(
                out=Vt2[:].rearrange("p (h e) -> p h e", e=64), in0=Vt2[:].rearrange("p (h e) -> p h e", e=64),
                in1=pe_[:, br * 16:(br + 1) * 16].unsqueeze(2).to_broadcast([128, 16, 64]), op=ALU.mult))(br),
                reads=bS2 + [bTB], writes=bS2)

            def fn(e, br=br):
                e.matmul(psf[0:1, 3072:3584], lhsT=onesf[:, 0:1], rhs=Vt2[:, 0:512], start=(br == 0), stop=(br == 2))
                e.matmul(psf[0:1, 3584:4096], lhsT=onesf[:, 0:1], rhs=Vt2[:, 512:1024], start=(br == 0), stop=(br == 2))
                return e.matmul(psf[0:1, 2048:2064], lhsT=onesf[:, 0:1], rhs=pe_[:, br * 16:(br + 1) * 16], start=(br == 0), stop=(br == 2))
            P.op("pe", fn, reads=bS2 + [bTB, bC], writes=[bBank[6], bBank[7], bBank[4]])
        t1 = tA[0:1, 0:T]
        l0 = sm1[0:1, 6, 0:8]
        l0 = sm1[0:1, 6:8, :].rearrange("p a b -> p (a b)")
        P.op("dve", lambda e: e.tensor_tensor(out=t1, in0=QROW, in1=KROW, op=ALU.mult), reads=bS2 + [bTC, bTU], writes=[bTA])
        P.op("dve", lambda e: e.tensor_reduce(out=l0, in_=t1.rearrange("p (h e) -> p h e", e=64), axis=AX.X, op=ALU.add),
             reads=[bTA], writes=[bSm])
        P.op("dve", lambda e: e.scalar_tensor_tensor(out=l0, in0=l0, scalar=0.125, in1=sb0S[:], op0=ALU.mult, op1=ALU.add),
             reads=[bC, bSm], writes=[bSm])
        P.op("act", lambda e: e.activation(out=l0, in_=l0, func=AF.Exp), reads=[bSm], writes=[bSm])
        P.op("dve", lambda e: e.tensor_scalar(out=l0, in0=l0, scalar1=3.0, scalar2=None, op0=ALU.mult), reads=[bSm], writes=[bSm])
        P.op("dve", lambda e: e.tensor_tensor(out=t1.rearrange("p (h e) -> p h e", e=64),
                                              in0=VROW.rearrange("p (h e) -> p h e", e=64),
                                              in1=l0.unsqueeze(2).to_broadcast([1, 16, 64]), op=ALU.mult),
             reads=bS2 + [bVr, bSm], writes=[bTA])
        P.op("dve", lambda e: e.tensor_tensor(out=t1, in0=psf[0:1, 3072:4096], in1=t1, op=ALU.add),
             reads=[bBank[6], bBank[7], bTA], writes=[bTA])
        P.op("dve", lambda e: e.tensor_tensor(out=l0, in0=psf[0:1, 2048:2064], in1=l0, op=ALU.add), reads=[bBank[4], bSm], writes=[bSm])
        P.op("dve", lambda e: e.reciprocal(out=l0, in_=l0), reads=[bSm], writes=[bSm])
        P.op("dve", lambda e: e.tensor_tensor(out=QROW.rearrange("p (h e) -> p h e", e=64),
                                              in0=t1.rearrange("p (h e) -> p h e", e=64),
                                              in1=l0.unsqueeze(2).to_broadcast([1, 16, 64]), op=ALU.mult),
             reads=[bTA, bSm], writes=bS2 + [bTC])

        def fn(e):
            ins = None
            for c in range(8):
                ins = e.matmul(psf[:, 3072 + 2 * c:3074 + 2 * c], lhsT=QROW[:, c * 128:(c + 1) * 128], rhs=onesf[0:1, 0:2],
                               start=True, stop=True)
            return ins
        P.op("pe", fn, reads=bS2 + [bC, bTC], writes=[bBank[6], bBank[7]])
        bOs = bf("osample")
        P.op("act", lambda e: e.activation(out=oT[:, :, T:TC], in_=psf[:, 3072:3088].rearrange("p (c two) -> p c two", two=2)[:, :, 0:1], func=AF.Copy),
             reads=[bBank[6], bBank[7]], writes=[bOs])

        mark('samp')
        Sb = [(psf[:, k * 512:(k + 1) * 512], [bBank[k]]) for k in range(4)]
        Oaccs = [(psf[:, 2048:3072], [bBank[4], bBank[5]]), (psf[:, 3072:4096], [bBank[6], bBank[7]])]
        bE4 = [bf("E4_%d" % i) for i in range(4)]
        bP4 = [bf("P4_%d" % i) for i in range(4)]
        for b_ in bE4 + bP4:
            b_.w = sp_tk
            b_.r = []
        EbS = [Eb[k // 2][:, (k % 2) * 512:(k % 2) * 512 + 512] for k in range(4)]
        PbS = [Pb[k // 2][:, (k % 2) * 512:(k % 2) * 512 + 512] for k in range(4)]
        tbank = (psf[:, 0:512], psf[:, 512:1024])

        def build_v(c, pp):
            P.dma("sp", (lambda l, c, pp: lambda e: e.dma_start(out=kTp[pp][:], in_=bouts[l][c].ap()[0:128, :]))(l, c, pp),
                  reads=[bBout[c]], writes=bKp[pp])
            P.dma("sp", (lambda l, c, pp: lambda e: e.dma_start(out=vTp[pp][:], in_=bouts[l][c].ap()[128:256, :]))(l, c, pp),
                  reads=[bBout[c]], writes=bVTp[pp])
            for hh in range(2):
                ti = pp * 2 + hh
                P.dma("sp", (lambda c, hh: lambda e: e.dma_start(out=bst[:], in_=btoe[2 * c + hh]))(c, hh), writes=[bBst])
                P.op("act", (lambda ti: lambda e: e.activation(out=Th[ti][:], in_=bst[:], func=AF.Exp))(ti), reads=[bBst], writes=[bTh[ti]])
                P.op("dve", (lambda ti: lambda e: e.tensor_tensor(out=Th[ti][:], in0=Th[ti][:], in1=mtS[:], op=ALU.mult))(ti),
                     reads=[bTh[ti], bC], writes=[bTh[ti]])
            for (src, srcb, dstl, dstb, prev) in ((vT[:, c, :], [bV[c]], Vown[pp], bVo[pp], False),
                                                  (vTp[pp], bVTp[pp], Vprv[pp], bVp[pp], True)):
                for half in range(2):
                    bank = tbank[half]
                    bb = [bBank[half]]

                    def fn(e, src=src, half=half, bank=bank):
                        ins = None
                        for j in range(4):
                            jj = half * 4 + j
                            ins = e.matmul(bank[:, j * 128:(j + 1) * 128], lhsT=src[:, jj * 128:(jj + 1) * 128], rhs=identb[:],
                                           start=True, stop=True)
                        return ins
                    P.op("pe", fn, reads=list(srcb) + [bC], writes=bb)
                    bv = bank.rearrange("p (j f) -> p j f", f=128)
                    for (dc, sc_) in ((0, 0), (128, 64)):
                        if prev:
                            P.op("dve", (lambda dstl, half, dc, sc_, bv: lambda e: e.tensor_scalar(
                                out=dstl[:, half * 4:half * 4 + 4, dc:dc + 64], in0=bv[:, :, sc_:sc_ + 64], scalar1=pv(FLAG), scalar2=None,
                                op0=ALU.mult))(dstl, half, dc, sc_, bv), reads=bb + [bC], writes=dstb)
                        else:
                            P.op("act", (lambda dstl, half, dc, sc_, bv: lambda e: e.activation(
                                out=dstl[:, half * 4:half * 4 + 4, dc:dc + 64], in_=bv[:, :, sc_:sc_ + 64], func=AF.Copy))(dstl, half, dc, sc_, bv),
                                reads=bb, writes=dstb)

        def attn_head(c, pp, hh):
            po = 64 * hh
            ti = pp * 2 + hh
            Oacc = Oaccs[hh]
            steps = []
            for jp in range(8):
                for h in range(2):
                    steps.append((True, jp, 512 * h, 512 * h + 512, 1024 - 128 * jp + 512 * h))
            for j in range(8):
                for h in range(2):
                    a_, b_ = max(128 * j, 512 * h), 512 * h + 512
                    if a_ < b_:
                        steps.append((False, j, a_, b_, a_ - 128 * j))
            n = len(steps)
            firsts = {}
            lasts = {}
            for si, st in enumerate(steps):
                bk = st[2] // 512
                firsts.setdefault(bk, si)
                lasts[bk] = si

            def S_op(si):
                prev, j, a_, b_, ts = steps[si]
                sap, sbk = Sb[si % 4]
                if prev:
                    ksrc, kb = kTp[pp][po:po + 64, j * 128:(j + 1) * 128], bKp[pp]
                else:
                    ksrc, kb = kT[po:po + 64, c, j * 128:(j + 1) * 128], [bK[c]]
                P.op("pe", lambda e: e.matmul(sap[:, 0:b_ - a_], lhsT=ksrc, rhs=qT[po:po + 64, c, a_:b_], start=True, stop=True),
                     reads=list(kb) + [bQ[c]], writes=sbk, lhs=list(kb))

            def EP_op(si):
                prev, j, a_, b_, ts = steps[si]
                sap, sbk = Sb[si % 4]
                w_ = b_ - a_
                k = si % 4
                P.op("act", lambda e: e.activation(out=EbS[k][:, 0:w_], in_=sap[:, 0:w_], func=AF.Exp, scale=0.125),
                     reads=sbk, writes=[bE4[k]])
                P.op("dve", lambda e: e.tensor_tensor(out=PbS[k][:, 0:w_], in0=EbS[k][:, 0:w_], in1=Th[ti][:, ts:ts + w_], op=ALU.mult),
                     reads=[bE4[k], bTh[ti]], writes=[bP4[k]])

            def PV_op(si):
                prev, j, a_, b_, ts = steps[si]
                vt = (Vprv if prev else Vown)[pp]
                vb = (bVp if prev else bVo)[pp]
                lhs = vt[:, j, 0:128] if hh == 0 else vt[:, j, 64:192]
                bk = a_ // 512
                k = si % 4
                P.op("pe", lambda e: e.matmul(Oacc[0][:, a_:b_], lhsT=lhs, rhs=PbS[k][:, 0:b_ - a_],
                                              start=(firsts[bk] == si), stop=(lasts[bk] == si)),
                     reads=list(vb) + [bP4[k]], writes=Oacc[1], lhs=list(vb))

            for si in range(min(3, n)):
                S_op(si)
            for si in range(n):
                EP_op(si)
                PV_op(si)
                if si + 3 < n:
                    S_op(si + 3)
            dpo = 64 - po
            P.op("act", lambda e: e.activation(out=rden[po:po + 64, :], in_=Oacc[0][dpo:dpo + 64, :], func=AF.Ln), reads=Oacc[1], writes=[bRd])
            P.op("act", lambda e: e.activation(out=rden[po:po + 64, :], in_=rden[po:po + 64, :], func=AF.Exp, scale=-1.0), reads=[bRd], writes=[bRd])
            P.op("dve", lambda e: e.tensor_tensor(out=oT[po:po + 64, c, 0:T], in0=Oacc[0][po:po + 64, :], in1=rden[po:po + 64, :],
                                                  op=ALU.mult), reads=Oacc[1] + [bRd], writes=[bO[c]])

        build_v(0, 0)
        for c in range(8):
            if c + 1 < 8:
                build_v(c + 1, (c + 1) % 2)
            for hh in range(2):
                attn_head(c, c % 2, hh)

        mark('attn')
        bPt = bf("ptl")
        P.dma("sp", (lambda l: lambda e: e.dma_start(out=ptl[:], in_=bout2s[l].ap()[0:128, :].rearrange("p (c r) -> p c r", r=2)))(l),
              reads=[bB2o], writes=[bPt])
        P.op("dve", lambda e: e.tensor_scalar(out=ptl[:], in0=ptl[:], scalar1=pv(FLAG), scalar2=None, op0=ALU.mult),
             reads=[bPt, bC], writes=[bPt])
        W0, W1 = parS[:, G_CW:G_CW + 8], parS[:, G_CW + 8:G_CW + 16]
        f0, f1, f2 = sm1[:, 0, :], sm1[:, 1, :], sm1[:, 2, :]
        bF = bSm
        P.op("dve", lambda e: e.tensor_tensor(out=f0, in0=ptl[:, :, 0], in1=W0, op=ALU.mult), reads=[bPt, bC], writes=[bF])
        P.op("dve", lambda e: e.tensor_tensor(out=f1, in0=ptl[:, :, 1], in1=W1, op=ALU.mult), reads=[bPt, bC, bF], writes=[bF])
        P.op("dve", lambda e: e.tensor_tensor(out=f0, in0=f0, in1=f1, op=ALU.add), reads=[bF], writes=[bF])
        P.op("dve", lambda e: e.tensor_tensor(out=f0, in0=f0, in1=y02[:, :, 0], op=ALU.add), reads=[bF, bY02], writes=[bF])
        P.op("dve", lambda e: e.tensor_tensor(out=zT[:, :, 0], in0=f0, in1=gb02[:, :, 0], op=ALU.mult), reads=[bF, bG02], writes=bZ)
        P.op("dve", lambda e: e.tensor_tensor(out=f2, in0=ptl[:, :, 1], in1=W0, op=ALU.mult), reads=[bPt, bC, bF], writes=[bF])
        P.op("dve", lambda e: e.tensor_tensor(out=f2, in0=f2, in1=y02[:, :, 1], op=ALU.add), reads=[bF, bY02], writes=[bF])
        P.op("dve", lambda e: e.tensor_tensor(out=zT[:, :, 1], in0=f2, in1=gb02[:, :, 1], op=ALU.mult), reads=[bF, bG02], writes=bZ)
        for (src, sbuf_, acc, rt, rb, gbase) in ((oT, bO, accs[0], tA, bTA, G_GA), (zT, bZ, accs[1], tB, bTB, G_GC)):
            for c in range(8):
                P.op("act", (lambda c, src: lambda e: e.activation(out=sqs[:, c, :], in_=src[:, c, :], func=AF.Square))(c, src),
                     reads=[sbuf_[c], bOs], writes=[bS2[c]])
            stat_mm(onesb, [sqs[:, c, :] for c in range(8)], bS2, acc)
            rstd_from(acc, rt, rb, 1.0 / 1024)
            for c in range(8):
                P.op("dve", (lambda c, src, rt, gbase: lambda e: e.scalar_tensor_tensor(
                    out=src[:, c, :], in0=src[:, c, :], scalar=pv(gbase + c), in1=rt[:, 0:TC], op0=ALU.mult, op1=ALU.mult))(c, src, rt, gbase),
                    reads=[sbuf_[c], rb, bC, bOs], writes=[sbuf_[c]])
        allx = XAL
        P.dma("sp", lambda e: e.dma_start(out=xT[:], in_=xsp), reads=[], writes=bX + allx)

        mark('p25')
        osrc = [oT[:, c, :] for c in range(8)] + [zT[:, c, :] for c in range(8)]
        for m in range(16):
            slot = next_piece(); acc = nextacc()
            mm_piece(slot, osrc, list(bO) + list(bZ), acc)
            P.op("dve", (lambda m, acc: lambda e: e.tensor_tensor(out=xT[:, m, :], in0=acc[0], in1=xT[:, m, :], op=ALU.add))(m, acc),
                 reads=acc[1] + [bX[m]], writes=[bX[m]])

        mark('p3')
        rmsnorm_x(G_MLP)

        def up_block(b):
            hb = b % 2
            for j in range(8):
                slot = next_piece(); acc = nextacc()
                mm_piece(slot, hsrc, bH, acc)
                P.op("act", (lambda acc: lambda e: e.activation(out=tCc[:, 0:TC], in_=acc[0], func=AF.Relu))(acc),
                     reads=acc[1], writes=[bTC])
                P.op("dve", (lambda hb, j: lambda e: e.tensor_tensor(out=hid[hb][:, j, :], in0=tCc[:, 0:TC], in1=tCc[:, 0:TC], op=ALU.mult))(hb, j),
                     reads=[bTC], writes=[bHid[hb][j], bQ[j] if hb == 0 else bK[j]])

        def down_block(b):
            hb = b % 2
            hs = [hid[hb][:, j, :] for j in range(8)]
            for g in range(8):
                slot = next_piece()
                for mm_ in range(2):
                    acc = nextacc()
                    m = 2 * g + mm_
                    mm_piece(slot, hs, bHid[hb] + (bQ if hb == 0 else bK), acc, nk=8, wcol=(lambda mm_: lambda kc: (kc * 256 + mm_ * 128, kc * 256 + mm_ * 128 + 128))(mm_))
                    P.op("dve", (lambda m, acc: lambda e: e.tensor_tensor(out=xT[:, m, :], in0=acc[0], in1=xT[:, m, :], op=ALU.add))(m, acc),
                         reads=acc[1] + [bX[m]], writes=[bX[m]])

        up_block(0)
        for b in range(8):
            if b + 1 < 8:
                up_block(b + 1)
            down_block(b)

    try:
        for l in range(nl):
            layer(l)
    except _Stop:
        pass
    names = dict(xT=(xT, bX), hT=(hT, bH), qT=(qT, bQ), kT=(kT, bK), vT=(vT, bV), zT=(zT, bZ), oT=(oT, bH[0:8] + [bf('osample')]),
                 tA=(tA, [bTA]), tB=(tB, [bTB]), tC=(tCc, [bTC]), tU=(tU, [bTU]))
    for dn in dumps:
        t_, bb_ = names[dn]
        shp = list(t_.shape)
        do = dt("dbg_" + dn, shp, F32, kind="ExternalOutput").ap()
        P.dma("pool", (lambda t_, do: lambda e: e.dma_start(out=do, in_=t_[:]))(t_, do), reads=bb_)
    P.dma("sp", lambda e: e.dma_start(out=yout, in_=xT[:]), reads=bX)

    with nc.Block() as block:
        @block.tensor
        def _(e):
            P.emit("pe", e)

        @block.scalar
        def _(e):
            P.emit("act", e)

        @block.vector
        def _(e):
            P.emit("dve", e)

        @block.gpsimd
        def _(e):
            P.emit("pool", e)

        @block.sync
        def _(e):
            P.emit("sp", e)
    return nc


def _weight_stream(w_in, w_out, w_up, w_down):
    out = np.empty((L * PPL, 128, 2048), np.float32)
    i = 0

    def colpiece(W, c0):
        return W[:, c0:c0 + 128].reshape(16, 128, 128).transpose(1, 0, 2).reshape(128, 2048)

    for l in range(L):
        wi = w_in[l]
        order = [1024 + 128 * c for c in range(8)] + [2048 + 128 * c for c in range(8)]
        for c in range(8):
            order += [3072 + 128 * c, 5120 + 128 * c, 4096 + 128 * c]
        order += [128 * c for c in range(8)]
        for c0 in order:
            out[i] = colpiece(wi, c0); i += 1
        for m in range(16):
            out[i] = colpiece(w_out[l], 128 * m); i += 1

        def up(b):
            nonlocal i
            for j in range(8):
                out[i] = colpiece(w_up[l], (8 * b + j) * 128); i += 1

        def down(b):
            nonlocal i
            blk = w_down[l][b * 1024:(b + 1) * 1024]
            for g in range(8):
                out[i] = blk[:, g * 256:(g + 1) * 256].reshape(8, 128, 256).transpose(1, 0, 2).reshape(128, 2048); i += 1
        up(0)
        for b in range(8):
            if b + 1 < 8:
                up(b + 1)
            down(b)
    assert i == L * PPL
    return out


_NC_CACHE = {}
_PREP_ONLY = False


def kernel(x_prompt, x_sample, state_attn_k, state_attn_v, state_conv, rel_bias, norm_mix, w_in, q_norm, k_norm,
           conv_w, attn_out_norm, conv_out_norm, w_out, norm_mlp, w_up, w_down):
    f = lambda a: np.ascontiguousarray(np.asarray(a, dtype=np.float32))
    x_prompt, x_sample, state_attn_k, state_attn_v, state_conv = map(f, (x_prompt, x_sample, state_attn_k, state_attn_v, state_conv))
    rel_bias, norm_mix, q_norm, k_norm, conv_w, attn_out_norm, conv_out_norm, norm_mlp = map(
        f, (rel_bias, norm_mix, q_norm, k_norm, conv_w, attn_out_norm, conv_out_norm, norm_mlp))
    wst = _weight_stream(f(w_in), f(w_out), f(w_up), f(w_down))
    kk = np.arange(128)[:, None]
    cc = np.arange(TW)[None, :]
    dist = cc - kk
    valid = (dist >= 0) & (dist <= 2048)
    dc = np.clip(dist, 0, 2048)
    bidx = _bucket(dc)
    btoe = np.ascontiguousarray(rel_bias[bidx].transpose(2, 0, 1))
    mult = ((dc <= 128).astype(np.float32) + ((dc % 4 == 0) & (dc <= 512)) + ((dc % 16 == 0) & (dc <= 2048))) * valid
    mtoe = mult.astype(np.float32)
    sbias = np.zeros((128, 48), np.float32)
    for br, d in enumerate((1, 4, 16)):
        j = 128 - np.arange(128)
        sbias[:, br * 16:(br + 1) * 16] = rel_bias[_bucket(j * d)]
    sb0 = np.ascontiguousarray(rel_bias[0:1, :])
    cst = np.zeros((128, 4, 128), np.float32)
    cst[:, 0] = np.eye(128)
    cst[:, 1] = 1.0
    cst[0:64, 2, 0:64] = 1.0
    cst[64:128, 2, 64:128] = 1.0
    cst[:, 3] = np.eye(128)
    in_maps = []
    for core in range(8):
        b, hf = core // 2, core % 2
        xin = np.empty((128, 16, TC), np.float32)
        xin[:, :, 0:T] = x_prompt[b, hf * T:(hf + 1) * T].reshape(T, 16, 128).transpose(2, 1, 0)
        xin[:, :, T] = x_sample[core, 0].reshape(16, 128).T
        par = np.zeros((128, 304), np.float32)
        for l in range(L):
            pb = l * NPARL
            par[:, pb:pb + 16] = norm_mix[l].reshape(16, 128).T
            par[:, pb + 16:pb + 32] = norm_mlp[l].reshape(16, 128).T
            par[:, pb + 32] = np.tile(q_norm[l], 2)
            par[:, pb + 33] = np.tile(k_norm[l], 2)
            for i in range(3):
                par[:, pb + 34 + 8 * i:pb + 42 + 8 * i] = conv_w[l, i].reshape(8, 128).T
            par[:, pb + 58:pb + 66] = attn_out_norm[l].reshape(8, 128).T
            par[:, pb + 66:pb + 74] = conv_out_norm[l].reshape(8, 128).T
        par[:, 296] = float(hf)
        sconv = np.ascontiguousarray(state_conv[:, core].reshape(L, 2, 8, 128).transpose(3, 0, 2, 1))
        in_maps.append({
            "xin": xin, "wst": wst, "par": par, "sconv": sconv, "btoe": btoe, "mtoe": mtoe, "sbias": sbias, "sb0": sb0,
            "cst": cst, "kst": np.ascontiguousarray(state_attn_k[:, core].reshape(L, 2048, 1024)),
            "vst": np.ascontiguousarray(state_attn_v[:, core].reshape(L, 2048, 1024)),
        })
    if _PREP_ONLY:
        return in_maps
    if "nc" not in _NC_CACHE:
        _NC_CACHE["nc"] = build()
    res = run_bass_kernel_spmd(_NC_CACHE["nc"], in_maps, core_ids=list(range(8))).results
    y_p = np.empty((4, 2048, 2048), np.float32)
    y_s = np.empty((8, 1, 2048), np.float32)
    nk = np.empty((L, 4, 2048, 16, 64), np.float32)
    nv = np.empty((L, 4, 2048, 16, 64), np.float32)
    ncp = np.empty((L, 4, 2, 1024), np.float32)
    nks = np.empty((L, 8, 2048, 16, 64), np.float32)
    nvs = np.empty((L, 8, 2048, 16, 64), np.float32)
    ncs = np.empty((L, 8, 2, 1024), np.float32)
    for core in range(8):
        b, hf = core // 2, core % 2
        r = res[core]
        yo = r["yout"]
        y_p[b, hf * T:(hf + 1) * T] = yo[:, :, 0:T].transpose(2, 1, 0).reshape(T, 2048)
        y_s[core, 0] = yo[:, :, T].T.reshape(2048)
        nk[:, b, hf * T:(hf + 1) * T] = r["kTo"].transpose(0, 3, 2, 1).reshape(L, T, 16, 64)
        nv[:, b, hf * T:(hf + 1) * T] = r["vTo"].transpose(0, 3, 2, 1).reshape(L, T, 16, 64)
        if hf == 1:
            ncp[:, b] = r["cpo"].transpose(0, 3, 2, 1).reshape(L, 2, 1024)
        nks[:, core] = r["kso"].reshape(L, 2048, 16, 64)
        nvs[:, core] = r["vso"].reshape(L, 2048, 16, 64)
        ncs[:, core] = r["cso"].transpose(0, 3, 2, 1).reshape(L, 2, 1024)
    return (y_p, y_s, nk, nv, ncp, nks, nvs, ncs)
```

```python
import numpy as np
import concourse.bass as bass
import concourse.mybir as mybir
from concourse.bass_utils import run_bass_kernel_spmd

F32 = mybir.dt.float32
BF16 = mybir.dt.bfloat16
ALU = mybir.AluOpType
AF = mybir.ActivationFunctionType
AX = mybir.AxisListType

L = 4
T = 1024
TC = 1025
NSLOT = 5
PPL = 192
NPARL = 74
EPS = 1e-6
TW = 2176
TILES = ((0, 512), (512, 1024), (1024, 1025))


def _bucket(dist):
    n_exact = 16
    large = n_exact + (np.log(np.maximum(dist, 1) / n_exact) / np.log(2048 / n_exact) * (32 - n_exact)).astype(np.int32)
    large = np.minimum(large, 31)
    return np.where(dist < n_exact, dist, large).astype(np.int32)


ATTACH = True


class Buf:
    __slots__ = ("w", "r", "name")

    def __init__(self, name=""):
        self.w = None
        self.r = []
        self.name = name


class Plan:
    ENG = ("pe", "act", "dve", "pool", "sp")

    def __init__(self, nc, semfn):
        self.nc = nc
        self.semfn = semfn
        self.q = {e: [] for e in self.ENG}
        self.cnt = {e: 0 for e in self.ENG}
        self.sem = {e: semfn("c_" + e) for e in ("pe", "act", "dve", "pool")}
        self.waited = {e: {} for e in self.ENG}
        nds = 12
        self.dsems = {e: [semfn("d_%s%d" % (e, i)) for i in range(nds)] for e in ("pool", "sp")}
        self.dval = {e: [0] * nds for e in ("pool", "sp")}
        self.dslot = {e: 0 for e in ("pool", "sp")}

    def new_layer_sems(self, l):
        for e in ("pe", "act", "dve", "pool"):
            self.sem[e] = self.semfn("c_%s_%d" % (e, l))
            self.cnt[e] = 0

    def _waits(self, eng, reads, writes, extra):
        need = {}
        lst = list(extra)
        for b in reads:
            if b.w is not None:
                lst.append(b.w)
        for b in writes:
            if b.w is not None:
                lst.append(b.w)
            lst.extend(b.r)
        for t in lst:
            if t is None:
                continue
            k = id(t[0])
            if k not in need or need[k][1] < t[1]:
                need[k] = t
        final = []
        for k, (sem, val) in need.items():
            if self.waited[eng].get(k, 0) >= val:
                continue
            self.waited[eng][k] = val
            final.append((sem, val))
        return final

    def op(self, eng, fn, reads=(), writes=(), extra=(), lhs=None):
        final = self._waits(eng, reads, writes, extra)
        att = None
        if ATTACH and final and (eng != "pe" or lhs is not None):
            lhs_ids = set(id(b.w[0]) for b in (lhs or ()) if b.w is not None)
            cand = [w for w in final if id(w[0]) not in lhs_ids]
            if cand:
                att = cand[-1]
                final = [w for w in final if w is not att]
        self.cnt[eng] += 1
        tk = (self.sem[eng], self.cnt[eng])
        self.q[eng].append((final, fn, tk, 1, att))
        for b in reads:
            b.r.append(tk)
        for b in writes:
            b.w = tk
            b.r = []
        return tk

    def dma(self, eng, fn, reads=(), writes=(), extra=()):
        slot = self.dslot[eng]
        self.dslot[eng] = (slot + 1) % len(self.dsems[eng])
        sem = self.dsems[eng][slot]
        prev = self.dval[eng][slot]
        ex = list(extra)
        if prev > 0:
            ex.append((sem, prev))
        final = self._waits(eng, reads, writes, ex)
        self.dval[eng][slot] = prev + 16
        tk = (sem, prev + 16)
        self.q[eng].append((final, fn, tk, 16, None))
        for b in reads:
            b.r.append(tk)
        for b in writes:
            b.w = tk
            b.r = []
        return tk

    def coll(self, fn, reads=(), writes=()):
        sem = self.semfn("cc%d" % len(self.q["pool"]))
        final = self._waits("pool", reads, writes, ())
        tk = (sem, 1)
        self.q["pool"].append((final, fn, tk, 0, None))
        for b in reads:
            b.r.append(tk)
        for b in writes:
            b.w = tk
            b.r = []
        return tk

    def emit(self, eng, e):
        for final, fn, tk, inc, att in self.q[eng]:
            for sem, val in final:
                e.wait_ge(sem, val)
            ins = fn(e)
            if isinstance(ins, tuple):
                first, ins = ins
            else:
                first = ins
            if att is not None:
                first._wait_ge(att[0], att[1])
            if inc == 0:
                ins.then_inc(tk[0])
            else:
                ins.then_inc(tk[0], inc)
        if eng in ("pool", "sp"):
            for i, sem in enumerate(self.dsems[eng]):
                if self.dval[eng][i] > 0:
                    e.wait_ge(sem, self.dval[eng][i])


class _Stop(Exception):
    pass


def build(nl=L, stop=None, dumps=(), npieces=None):
    nc = bass.Bass("TRN2", target_bir_lowering=False)

    def mark(name):
        if stop is not None and name == stop:
            raise _Stop()
    dt = nc.dram_tensor
    xin = dt("xin", [128, 16, TC], F32, kind="ExternalInput").ap()
    wst = dt("wst", [npieces or nl * PPL, 128, 2048], F32, kind="ExternalInput").ap()
    par = dt("par", [128, 304], F32, kind="ExternalInput").ap()
    sconv = dt("sconv", [128, L, 8, 2], F32, kind="ExternalInput").ap()
    btoe = dt("btoe", [16, 128, TW], F32, kind="ExternalInput").ap()
    mtoe = dt("mtoe", [128, TW], F32, kind="ExternalInput").ap()
    sbias = dt("sbias", [128, 48], F32, kind="ExternalInput").ap()
    sb0 = dt("sb0", [1, 16], F32, kind="ExternalInput").ap()
    cst = dt("cst", [128, 4, 128], F32, kind="ExternalInput").ap()
    kst = dt("kst", [nl, 2048, 1024], F32, kind="ExternalInput").ap()
    vst = dt("vst", [nl, 2048, 1024], F32, kind="ExternalInput").ap()
    yout = dt("yout", [128, 16, TC], F32, kind="ExternalOutput").ap()
    kTo = dt("kTo", [L, 128, 8, T], F32, kind="ExternalOutput").ap()
    vTo = dt("vTo", [L, 128, 8, T], F32, kind="ExternalOutput").ap()
    cpo = dt("cpo", [L, 128, 8, 2], F32, kind="ExternalOutput").ap()
    kso = dt("kso", [nl, 2048, 1024], F32, kind="ExternalOutput").ap()
    vso = dt("vso", [nl, 2048, 1024], F32, kind="ExternalOutput").ap()
    cso = dt("cso", [L, 128, 8, 2], F32, kind="ExternalOutput").ap()
    xsp = dt("xsp", [128, 16, TC], F32).ap()
    bins = [[dt("bin%d_%d" % (l, c), [256, T], BF16) for c in range(8)] for l in range(L)]
    bouts = [[dt("bout%d_%d" % (l, c), [512, T], BF16) for c in range(8)] for l in range(L)]
    bin2s = [dt("binb%d" % l, [128, 16], F32) for l in range(L)]
    bout2s = [dt("boutb%d" % l, [256, 16], F32) for l in range(L)]

    off = [16512]

    def alloc(name, shape, dtp, at=None):
        esz = 2 if dtp == BF16 else 4
        nb = int(np.prod(shape[1:])) * esz
        nb = (nb + 31) // 32 * 32
        if at is None:
            o = off[0]
            off[0] += nb
        else:
            o = at
        return nc.alloc_sbuf_tensor_at(name, list(shape), dtp, offset=o), o, nb

    ring = []
    for i in range(NSLOT):
        t_, _, _ = alloc("ring%d" % i, [128, 2048], BF16)
        ring.append(t_)
    identb, _, _ = alloc("identb", [128, 128], BF16)
    onesb, _, _ = alloc("onesb", [128, 128], BF16)
    blkb, _, _ = alloc("blkb", [128, 128], BF16)
    cstf, _, _ = alloc("cstf", [128, 2, 128], F32)
    parS, _, _ = alloc("parS", [128, 304], F32)
    scS, _, _ = alloc("scS", [128, L, 8, 2], F32)
    sbS, _, _ = alloc("sbS", [128, 48], F32)
    sb0S, _, _ = alloc("sb0S", [1, 16], F32)
    mtS, _, _ = alloc("mtS", [128, TW], BF16)
    tail, _, _ = alloc("tail", [128, 8, 2], F32)
    y02, _, _ = alloc("y02", [128, 8, 2], F32)
    gb02, _, _ = alloc("gb02", [128, 8, 2], F32)
    csst, _, _ = alloc("csst", [128, 8, 2], F32)
    ptl, _, _ = alloc("ptl", [128, 8, 2], F32)
    sm1, _, _ = alloc("sm1", [128, 8, 8], F32)
    epsS, _, _ = alloc("epsS", [128, 1], F32)
    xT, XO, _ = alloc("xT", [128, 16, TC], F32)
    hT, HO, _ = alloc("hT", [128, 16, TC], BF16)
    off[0] += 32
    qT, QO, _ = alloc("qT", [128, 8, TC], BF16)
    kT, KO, _ = alloc("kT", [128, 8, TC], BF16)
    vT, _, _ = alloc("vT", [128, 8, TC], BF16)
    zT, _, _ = alloc("zT", [128, 8, TC], BF16)
    tA, _, _ = alloc("tA", [128, 1032], F32)
    tB, _, _ = alloc("tB", [128, 1032], F32)
    tCc, _, _ = alloc("tC", [128, 1032], F32)
    tU, _, _ = alloc("tU", [128, 1032], F32)
    tD, _, _ = alloc("tD", [128, 1032], BF16)
    assert off[0] <= 229376, off[0]
    oT, _, _ = alloc("oT", [128, 8, TC], BF16, at=HO)
    sqs, _, _ = alloc("sqs", [128, 8, TC], BF16, at=HO + 16416)
    hid = [alloc("hid0", [128, 8, TC], BF16, at=QO)[0], alloc("hid1", [128, 8, TC], BF16, at=KO)[0]]
    xo = [XO]

    def xalloc(name, shape, dtp):
        t_, o, nb = alloc(name, shape, dtp, at=xo[0])
        xo[0] += nb
        return t_

    Vown = [xalloc("Vown%d" % i, [128, 8, 192], BF16) for i in range(2)]
    Vprv = [xalloc("Vprv%d" % i, [128, 8, 192], BF16) for i in range(2)]
    kTp = [xalloc("kTp%d" % i, [128, T], BF16) for i in range(2)]
    vTp = [xalloc("vTp%d" % i, [128, T], BF16) for i in range(2)]
    Th = [xalloc("Th%d" % i, [128, TW], BF16) for i in range(4)]
    Eb = [xalloc("Eb%d" % i, [128, T], BF16) for i in range(2)]
    Pb = [xalloc("Pb%d" % i, [128, T], BF16) for i in range(2)]
    bst = xalloc("bst", [128, TW], F32)
    rden = xalloc("rden", [128, T], F32)
    vrow = xalloc("vrow", [1, T], F32)
    assert xo[0] <= XO + 65600, xo[0] - XO
    ho = [HO + 16416]

    def halloc(name, shape, dtp):
        t_, o, nb = alloc(name, shape, dtp, at=ho[0])
        ho[0] += nb
        return t_

    qb = halloc("qb", [128, T], F32)
    Kt = halloc("Kt", [128, T], F32)
    Vt2 = halloc("Vt2", [128, T], F32)
    assert ho[0] <= HO + 32832

    sems = []

    def semfn(name):
        s = nc.semaphore(name)
        h = s.__enter__()
        sems.append(s)
        return h

    ps_cm = nc.psum_tensor("ps", [128, 8, 512], F32)
    ps = ps_cm.__enter__()
    psf = ps[:].rearrange("p b n -> p (b n)")
    P = Plan(nc, semfn)

    B = {}

    def bf(name):
        if name not in B:
            B[name] = Buf(name)
        return B[name]

    def bl(prefix, n):
        return [bf("%s%d" % (prefix, i)) for i in range(n)]

    bX = bl("x", 16)
    bH = bl("h", 16)
    bQ = bl("q", 8)
    bK = bl("k", 8)
    bV = bl("v", 8)
    bZ = bl("z", 8)
    bRing = bl("ring", NSLOT)
    bBank = bl("bank", 8)
    bTA, bTB, bTC, bTU, bTD = bf("tA"), bf("tB"), bf("tC"), bf("tU"), bf("tD")
    bHid = [bl("hid0_", 8), bl("hid1_", 8)]
    accs = [(psf[:, 0:TC], [bBank[0], bBank[1], bBank[2]]), (psf[:, 1536:1536 + TC], [bBank[3], bBank[4], bBank[5]])]
    STAT = (psf[:, 3072:4096], [bBank[6], bBank[7]])

    def pv(i):
        return parS[:, i:i + 1]

    bC = bf("consts")
    P.dma("sp", lambda e: e.dma_start(out=xT[:], in_=xin), writes=bX)
    P.dma("sp", lambda e: e.dma_start(out=parS[:], in_=par), writes=[bC])
    P.dma("sp", lambda e: e.dma_start(out=scS[:], in_=sconv), writes=[bC])
    P.dma("sp", lambda e: e.dma_start(out=sbS[:], in_=sbias), writes=[bC])
    P.dma("sp", lambda e: e.dma_start(out=sb0S[:], in_=sb0), writes=[bC])
    P.dma("sp", lambda e: e.dma_start(out=cstf[:, 0, :], in_=cst[:, 0, :]), writes=[bC])
    P.dma("sp", lambda e: e.dma_start(out=cstf[:, 1, :], in_=cst[:, 1, :]), writes=[bC])
    P.dma("pool", lambda e: e.dma_start(out=identb[:], in_=cst[:, 0, :]), writes=[bC])
    P.dma("pool", lambda e: e.dma_start(out=onesb[:], in_=cst[:, 1, :]), writes=[bC])
    P.dma("pool", lambda e: e.dma_start(out=blkb[:], in_=cst[:, 2, :]), writes=[bC])
    P.dma("pool", lambda e: e.dma_start(out=mtS[:], in_=mtoe), writes=[bC])
    identf = cstf[:, 0, :]
    onesf = cstf[:, 1, :]
    P.op("dve", lambda e: e.memset(epsS[:], EPS), writes=[bC])
    P.op("dve", lambda e: e.memset(tU[:, 0:2], 0.0), writes=[bTU])
    FLAG = 296
    bVo = [bl("vo0_", 1), bl("vo1_", 1)]
    bVp = [bl("vp0_", 1), bl("vp1_", 1)]
    bKp = [bl("kTp0_", 1), bl("kTp1_", 1)]
    bVTp = [bl("vTp0_", 1), bl("vTp1_", 1)]
    bTh = [bf("Th%d" % i) for i in range(4)]
    bBst = bf("bst")
    bE = [bf("E0"), bf("E1")]
    bP = [bf("P0"), bf("P1")]
    bRd = bf("rden")
    bVr = bf("vrow")
    bSm = bf("sm1")
    XAL0 = [bVo[0][0], bVo[1][0], bVp[0][0], bVp[1][0], bKp[0][0], bKp[1][0], bVTp[0][0], bVTp[1][0]] + bTh + bE + bP + [bBst, bRd, bVr]
    XAL = XAL0 + [bf("E4_%d" % i) for i in range(4)] + [bf("P4_%d" % i) for i in range(4)]
    QROW = tCc[0:1, 0:T]
    KROW = tU[0:1, 2:2 + T]
    VROW = vrow[0:1, :]
    ROWS = [QROW, KROW, VROW]

    ws = {"next": 0, "cons": 0}

    def fetch_upto(n):
        while ws["next"] < min(n, npieces or nl * PPL):
            i = ws["next"]
            s = i % NSLOT
            P.dma("pool", (lambda i, s: lambda e: e.dma_start(out=ring[s][:], in_=wst[i]))(i, s), writes=[bRing[s]])
            ws["next"] += 1

    def next_piece():
        i = ws["cons"]
        ws["cons"] += 1
        fetch_upto(i + NSLOT)
        return i % NSLOT

    def mm_piece(slot, srcs, sbufs, acc, nk=16, wcol=lambda kc: (kc * 128, kc * 128 + 128)):
        accap, accb = acc

        def fn(e):
            ins = None
            first = None
            for kc in range(nk):
                a, b = wcol(kc)
                for lo, hi in TILES:
                    ins = e.matmul(accap[:, lo:hi], lhsT=ring[slot][:, a:b], rhs=srcs[kc][:, lo:hi],
                                   start=(kc == 0), stop=(kc == nk - 1))
                    if first is None:
                        first = ins
            return (first, ins)
        return P.op("pe", fn, reads=[bRing[slot]] + list(sbufs), writes=accb, lhs=[bRing[slot]])

    def stat_mm(lhs, srcs, sbufs, acc):
        accap, accb = acc
        n = len(srcs)

        def fn(e):
            ins = None
            for kc in range(n):
                for lo, hi in TILES:
                    ins = e.matmul(accap[:, lo:hi], lhsT=lhs[:], rhs=srcs[kc][:, lo:hi], start=(kc == 0), stop=(kc == n - 1))
            return ins
        return P.op("pe", fn, reads=[bC] + list(sbufs), writes=accb)

    def rstd_from(acc, out_t, out_b, scale):
        accap, accb = acc
        P.op("act", lambda e: e.activation(out=out_t[:, 0:TC], in_=accap, func=AF.Ln, bias=epsS[:], scale=scale),
             reads=accb + [bC], writes=[out_b])
        P.op("act", lambda e: e.activation(out=out_t[:, 0:TC], in_=out_t[:, 0:TC], func=AF.Exp, scale=-0.5), reads=[out_b], writes=[out_b])

    def rmsnorm_x(gbase):
        for c in range(16):
            P.op("act", (lambda c: lambda e: e.activation(out=hT[:, c, :], in_=xT[:, c, :], func=AF.Square))(c),
                 reads=[bX[c]], writes=[bH[c]])
        stat_mm(onesb, [hT[:, c, :] for c in range(16)], bH, accs[0])
        rstd_from(accs[0], tA, bTA, 1.0 / 2048)
        for c in range(16):
            eng = "dve"
            P.op(eng, (lambda c: lambda e: e.scalar_tensor_tensor(out=hT[:, c, :], in0=xT[:, c, :], scalar=pv(gbase + c),
                                                                  in1=tA[:, 0:TC], op0=ALU.mult, op1=ALU.mult))(c),
                 reads=[bX[c], bTA, bC], writes=[bH[c]])

    hsrc = [hT[:, c, :] for c in range(16)]
    accsel = [0]

    def nextacc():
        accsel[0] ^= 1
        return accs[accsel[0]]

    def layer(l):
        if l > 0:
            P.new_layer_sems(l)
        pb = l * NPARL
        G_MIX, G_MLP, G_Q, G_K, G_CW, G_GA, G_GC = pb, pb + 16, pb + 32, pb + 33, pb + 34, pb + 58, pb + 66
        for (dst_, src_) in ((kso, kst), (vso, vst)):
            P.dma("sp", (lambda l, dst_, src_: lambda e: e.dma_start(
                out=dst_[l, 0:2047, :].rearrange("(a b) c -> a (b c)", b=23),
                in_=src_[l, 1:2048, :].rearrange("(a b) c -> a (b c)", b=23)))(l, dst_, src_))
        rmsnorm_x(G_MIX)
        sp_tk = P.dma("sp", lambda e: e.dma_start(out=xsp, in_=xT[:]), reads=bX)
        for b_ in XAL:
            b_.w = sp_tk
            b_.r = []
        for i in range(2):
            P.op("dve", (lambda i: lambda e: e.memset(Vown[i][:, :, 64:128], 1.0))(i), writes=bVo[i])
            P.op("dve", (lambda i: lambda e: e.memset(Vprv[i][:, :, 64:128], 1.0))(i), writes=bVp[i])
            P.op("dve", (lambda i: lambda e: e.tensor_scalar(out=Vprv[i][:, :, 64:128], in0=Vprv[i][:, :, 64:128],
                                                              scalar1=pv(FLAG), scalar2=None, op0=ALU.mult))(i),
                 reads=[bC], writes=bVp[i])

        mark('p0')
        def qk_piece(dst, dbuf, c, gidx):
            slot = next_piece()
            mm_piece(slot, hsrc, bH, accs[0])
            P.op("dve", lambda e: e.tensor_copy(out=tA[:, 0:TC], in_=accs[0][0]), reads=accs[0][1], writes=[bTA])
            P.op("act", lambda e: e.activation(out=tD[:, 0:TC], in_=tA[:, 0:TC], func=AF.Square), reads=[bTA], writes=[bTD])
            stat_mm(blkb, [tD[:, 0:TC]], [bTD], accs[1])
            rstd_from(accs[1], tB, bTB, 1.0 / 64)
            P.op("dve", lambda e: e.scalar_tensor_tensor(out=dst[:, c, :], in0=tA[:, 0:TC], scalar=pv(gidx), in1=tB[:, 0:TC],
                                                         op0=ALU.mult, op1=ALU.mult),
                 reads=[bTA, bTB, bC], writes=[dbuf[c]])

        for c in range(8):
            qk_piece(kT, bK, c, G_K)
        bBin = [bf("bin%d_%d" % (l, c)) for c in range(8)]
        bBout = [bf("bout%d_%d" % (l, c)) for c in range(8)]
        for c in range(8):
            slot = next_piece()
            acc = nextacc()
            mm_piece(slot, hsrc, bH, acc)
            P.op("act", (lambda c, acc: lambda e: e.activation(out=vT[:, c, :], in_=acc[0], func=AF.Copy))(c, acc),
                 reads=acc[1], writes=[bV[c]])
            P.dma("sp", (lambda l, c: lambda e: e.dma_start(out=bins[l][c].ap()[0:128, :], in_=kT[:, c, 0:T]))(l, c),
                  reads=[bK[c]], writes=[bBin[c]])
            P.dma("sp", (lambda l, c: lambda e: e.dma_start(out=bins[l][c].ap()[128:256, :], in_=vT[:, c, 0:T]))(l, c),
                  reads=[bV[c]], writes=[bBin[c]])
            P.coll((lambda l, c: lambda e: e.collective_compute("AllGather", ALU.bypass,
                                                                replica_groups=[[0, 1], [2, 3], [4, 5], [6, 7]],
                                                                ins=[bins[l][c].ap()], outs=[bouts[l][c].ap()]))(l, c),
                   reads=[bBin[c]], writes=[bBout[c]])
        mark('x2')
        P.dma("pool", (lambda l: lambda e: e.dma_start(out=kTo[l], in_=kT[:, :, 0:T]))(l), reads=bK)
        P.dma("pool", (lambda l: lambda e: e.dma_start(out=vTo[l], in_=vT[:, :, 0:T]))(l), reads=bV)
        mark('xchg')
        bTail, bY02, bG02, bCs = bf("tail"), bf("y02"), bf("gb02"), bf("csst")
        for c in range(8):
            w0, w1, w2 = pv(G_CW + c), pv(G_CW + 8 + c), pv(G_CW + 16 + c)
            slot = next_piece(); acc = nextacc()
            mm_piece(slot, hsrc, bH, acc)
            P.op("act", (lambda acc: lambda e: e.activation(out=tA[:, 0:TC], in_=acc[0], func=AF.Copy))(acc),
                 reads=acc[1], writes=[bTA])
            mark('cv1')
            slot = next_piece(); acc = nextacc()
            mm_piece(slot, hsrc, bH, acc)
            P.op("dve", (lambda acc: lambda e: e.tensor_tensor(out=tU[:, 2:2 + TC], in0=acc[0], in1=tA[:, 0:TC], op=ALU.mult))(acc),
                 reads=acc[1] + [bTA], writes=[bTU])
            mark('cv2')
            P.op("act", (lambda c: lambda e: e.activation(out=tail[:, c, :], in_=tU[:, 1024:1026], func=AF.Copy))(c),
                 reads=[bTU], writes=[bTail])
            P.op("act", (lambda c: lambda e: e.activation(out=csst[:, c, 1:2], in_=tU[:, 1026:1027], func=AF.Copy))(c),
                 reads=[bTU], writes=[bCs])
            P.op("act", (lambda c, l: lambda e: e.activation(out=csst[:, c, 0:1], in_=scS[:, l, c, 1:2], func=AF.Copy))(c, l),
                 reads=[bC], writes=[bCs])
            mark('cv3')
            P.op("dve", (lambda w2: lambda e: e.tensor_scalar(out=tCc[:, 0:T], in0=tU[:, 2:2 + T], scalar1=w2, scalar2=None,
                                                              op0=ALU.mult))(w2), reads=[bTU, bC], writes=[bTC])
            P.op("dve", (lambda w1: lambda e: e.scalar_tensor_tensor(out=tCc[:, 0:T], in0=tU[:, 1:1 + T], scalar=w1, in1=tCc[:, 0:T],
                                                                     op0=ALU.mult, op1=ALU.add))(w1), reads=[bTU, bTC, bC], writes=[bTC])
            P.op("dve", (lambda w0: lambda e: e.scalar_tensor_tensor(out=tCc[:, 0:T], in0=tU[:, 0:T], scalar=w0, in1=tCc[:, 0:T],
                                                                     op0=ALU.mult, op1=ALU.add))(w0), reads=[bTU, bTC, bC], writes=[bTC])
            mark('cv4')
            P.op("act", (lambda c, l, w0: lambda e: e.activation(out=tCc[:, T:TC], in_=scS[:, l, c, 0:1], func=AF.Identity, scale=w0))(c, l, w0),
                 reads=[bC], writes=[bTC])
            P.op("act", (lambda c, l, w1: lambda e: e.activation(out=tCc[:, T:TC], in_=scS[:, l, c, 1:2], func=AF.Identity, scale=w1,
                                                                 bias=tCc[:, T:TC]))(c, l, w1), reads=[bC, bTC], writes=[bTC])
            P.op("act", (lambda w2: lambda e: e.activation(out=tCc[:, T:TC], in_=tU[:, 1026:1027], func=AF.Identity, scale=w2,
                                                           bias=tCc[:, T:TC]))(w2), reads=[bTU, bTC, bC], writes=[bTC])
            mark('cv5')
            P.op("act", (lambda c: lambda e: e.activation(out=y02[:, c, :], in_=tCc[:, 0:2], func=AF.Copy))(c),
                 reads=[bTC], writes=[bY02])
            slot = next_piece(); acc = nextacc()
            mm_piece(slot, hsrc, bH, acc)
            P.op("dve", (lambda c, acc: lambda e: e.tensor_tensor(out=zT[:, c, :], in0=acc[0], in1=tCc[:, 0:TC], op=ALU.mult))(c, acc),
                 reads=acc[1] + [bTC], writes=[bZ[c]])
            P.op("act", (lambda c, acc: lambda e: e.activation(out=gb02[:, c, :], in_=acc[0][:, 0:2], func=AF.Copy))(c, acc),
                 reads=[], writes=[bG02] + acc[1])
        mark('conv')
        bB2i, bB2o = bf("b2i%d" % l), bf("b2o%d" % l)
        P.dma("sp", (lambda l: lambda e: e.dma_start(out=bin2s[l].ap().rearrange("p (c r) -> p c r", r=2), in_=tail[:]))(l),
              reads=[bTail], writes=[bB2i])
        P.coll((lambda l: lambda e: e.collective_compute("AllGather", ALU.bypass,
                                                         replica_groups=[[0, 1], [2, 3], [4, 5], [6, 7]],
                                                         ins=[bin2s[l].ap()], outs=[bout2s[l].ap()]))(l),
               reads=[bB2i], writes=[bB2o])
        P.dma("sp", (lambda l: lambda e: e.dma_start(out=cpo[l], in_=tail[:]))(l), reads=[bTail])
        P.dma("sp", (lambda l: lambda e: e.dma_start(out=cso[l], in_=csst[:]))(l), reads=[bCs])
        for c in range(8):
            qk_piece(qT, bQ, c, G_Q)

        mark('p1')
        bO = bH[0:8]
        bS2 = bH[8:16]
        bank6, bank7 = psf[:, 3072:3584], psf[:, 3584:4096]
        for ri, (src, sb_) in enumerate(((qT, bQ), (kT, bK), (vT, bV))):
            def fn(e, src=src):
                ins = None
                for c in range(8):
                    ins = e.matmul(psf[0:1, 3072 + c * 128:3072 + (c + 1) * 128], lhsT=src[:, c, T:TC], rhs=identb[:],
                                   start=True, stop=True)
                return ins
            P.op("pe", fn, reads=list(sb_) + [bC], writes=[bBank[6], bBank[7]])
            P.op("act", (lambda ri: lambda e: e.activation(out=ROWS[ri], in_=psf[0:1, 3072:4096], func=AF.Copy))(ri),
                 reads=[bBank[6], bBank[7]], writes=bS2 + [bTC, bTU, bVr])
        P.dma("sp", (lambda l: lambda e: e.dma_start(out=kso[l, 2047:2048, :], in_=KROW))(l), reads=[bTU])
        P.dma("sp", (lambda l: lambda e: e.dma_start(out=vso[l, 2047:2048, :], in_=VROW))(l), reads=[bVr])

        def fn(e):
            e.matmul(bank6, lhsT=onesf[0:1, :], rhs=QROW[:, 0:512], start=True, stop=True)
            return e.matmul(bank7, lhsT=onesf[0:1, :], rhs=QROW[:, 512:1024], start=True, stop=True)
        P.op("pe", fn, reads=bS2 + [bC, bTC], writes=[bBank[6], bBank[7]])
        P.op("act", lambda e: e.activation(out=qb[:], in_=psf[:, 3072:4096], func=AF.Copy), reads=[bBank[6], bBank[7]], writes=bS2)
        lg = sm1[:, 0:6, :].rearrange("p a b -> p (a b)")
        pe_ = tB[:, 0:48]
        for br, d in enumerate((1, 4, 16)):
            P.dma("sp", (lambda l, d: lambda e: e.dma_start(out=Kt[:], in_=kst[l, 2048 - 128 * d:2048:d, :]))(l, d), writes=bS2)
            P.op("dve", lambda e: e.tensor_tensor(out=Kt[:], in0=Kt[:], in1=qb[:], op=ALU.mult), reads=bS2, writes=bS2)
            P.op("dve", (lambda br: lambda e: e.tensor_reduce(out=lg[:, br * 16:(br + 1) * 16],
                                                              in_=Kt[:].rearrange("p (h e) -> p h e", e=64), axis=AX.X, op=ALU.add))(br),
                 reads=bS2, writes=[bSm])
        P.op("dve", lambda e: e.scalar_tensor_tensor(out=lg, in0=lg, scalar=0.125, in1=sbS[:], op0=ALU.mult, op1=ALU.add),
             reads=[bC, bSm], writes=[bSm])
        P.op("act", lambda e: e.activation(out=pe_, in_=lg, func=AF.Exp), reads=[bSm], writes=[bTB])
        for br, d in enumerate((1, 4, 16)):
            P.dma("sp", (lambda l, d: lambda e: e.dma_start(out=Vt2[:], in_=vst[l, 2048 - 128 * d:2048:d, :]))(l, d), writes=bS2)
            P.op("dve", (lambda br: lambda e: e.tensor_tensor(
                out=Vt2[:].rearrange("p (h e) -> p h e", e=64), in0=Vt2[:].rearrange("p (h e) -> p h e", e=64),
                in1=pe_[:, br * 16:(br + 1) * 16].unsqueeze(2).to_broadcast([128, 16, 64]), op=ALU.mult))(br),
                reads=bS2 + [bTB], writes=bS2)

            def fn(e, br=br):
                e.matmul(psf[0:1, 3072:3584], lhsT=onesf[:, 0:1], rhs=Vt2[:, 0:512], start=(br == 0), stop=(br == 2))
                e.matmul(psf[0:1, 3584:4096], lhsT=onesf[:, 0:1], rhs=Vt2[:, 512:1024], start=(br == 0), stop=(br == 2))
                return e.matmul(psf[0:1, 2048:2064], lhsT=onesf[:, 0:1], rhs=pe_[:, br * 16:(br + 1) * 16], start=(br == 0), stop=(br == 2))
            P.op("pe", fn, reads=bS2 + [bTB, bC], writes=[bBank[6], bBank[7], bBank[4]])
        t1 = tA[0:1, 0:T]
        l0 = sm1[0:1, 6, 0:8]
        l0 = sm1[0:1, 6:8, :].rearrange("p a b -> p (a b)")
        P.op("dve", lambda e: e.tensor_tensor(out=t1, in0=QROW, in1=KROW, op=ALU.mult), reads=bS2 + [bTC, bTU], writes=[bTA])
        P.op("dve", lambda e: e.tensor_reduce(out=l0, in_=t1.rearrange("p (h e) -> p h e", e=64), axis=AX.X, op=ALU.add),
             reads=[bTA], writes=[bSm])
        P.op("dve", lambda e: e.scalar_tensor_tensor(out=l0, in0=l0, scalar=0.125, in1=sb0S[:], op0=ALU.mult, op1=ALU.add),
             reads=[bC, bSm], writes=[bSm])
        P.op("act", lambda e: e.activation(out=l0, in_=l0, func=AF.Exp), reads=[bSm], writes=[bSm])
        P.op("dve", lambda e: e.tensor_scalar(out=l0, in0=l0, scalar1=3.0, scalar2=None, op0=ALU.mult), reads=[bSm], writes=[bSm])
        P.op("dve", lambda e: e.tensor_tensor(out=t1.rearrange("p (h e) -> p h e", e=64),
                                              in0=VROW.rearrange("p (h e) -> p h e", e=64),
                                              in1=l0.unsqueeze(2).to_broadcast([1, 16, 64]), op=ALU.mult),
             reads=bS2 + [bVr, bSm], writes=[bTA])
        P.op("dve", lambda e: e.tensor_tensor(out=t1, in0=psf[0:1, 3072:4096], in1=t1, op=ALU.add),
             reads=[bBank[6], bBank[7], bTA], writes=[bTA])
        P.op("dve", lambda e: e.tensor_tensor(out=l0, in0=psf[0:1, 2048:2064], in1=l0, op=ALU.add), reads=[bBank[4], bSm], writes=[bSm])
        P.op("dve", lambda e: e.reciprocal(out=l0, in_=l0), reads=[bSm], writes=[bSm])
        P.op("dve", lambda e: e.tensor_tensor(out=QROW.rearrange("p (h e) -> p h e", e=64),
                                              in0=t1.rearrange("p (h e) -> p h e", e=64),
                                              in1=l0.unsqueeze(2).to_broadcast([1, 16, 64]), op=ALU.mult),
             reads=[bTA, bSm], writes=bS2 + [bTC])

        def fn(e):
            ins = None
            for c in range(8):
                ins = e.matmul(psf[:, 3072 + 2 * c:3074 + 2 * c], lhsT=QROW[:, c * 128:(c + 1) * 128], rhs=onesf[0:1, 0:2],
                               start=True, stop=True)
            return ins
        P.op("pe", fn, reads=bS2 + [bC, bTC], writes=[bBank[6], bBank[7]])
        bOs = bf("osample")
        P.op("act", lambda e: e.activation(out=oT[:, :, T:TC], in_=psf[:, 3072:3088].rearrange("p (c two) -> p c two", two=2)[:, :, 0:1], func=AF.Copy),
             reads=[bBank[6], bBank[7]], writes=[bOs])

        mark('samp')
        Sb = [(psf[:, k * 512:(k + 1) * 512], [bBank[k]]) for k in range(4)]
        Oaccs = [(psf[:, 2048:3072], [bBank[4], bBank[5]]), (psf[:, 3072:4096], [bBank[6], bBank[7]])]
        bE4 = [bf("E4_%d" % i) for i in range(4)]
        bP4 = [bf("P4_%d" % i) for i in range(4)]
        for b_ in bE4 + bP4:
            b_.w = sp_tk
            b_.r = []
        EbS = [Eb[k // 2][:, (k % 2) * 512:(k % 2) * 512 + 512] for k in range(4)]
        PbS = [Pb[k // 2][:, (k % 2) * 512:(k % 2) * 512 + 512] for k in range(4)]
        tbank = (psf[:, 0:512], psf[:, 512:1024])

        def build_v(c, pp):
            P.dma("sp", (lambda l, c, pp: lambda e: e.dma_start(out=kTp[pp][:], in_=bouts[l][c].ap()[0:128, :]))(l, c, pp),
                  reads=[bBout[c]], writes=bKp[pp])
            P.dma("sp", (lambda l, c, pp: lambda e: e.dma_start(out=vTp[pp][:], in_=bouts[l][c].ap()[128:256, :]))(l, c, pp),
                  reads=[bBout[c]], writes=bVTp[pp])
            for hh in range(2):
                ti = pp * 2 + hh
                P.dma("sp", (lambda c, hh: lambda e: e.dma_start(out=bst[:], in_=btoe[2 * c + hh]))(c, hh), writes=[bBst])
                P.op("act", (lambda ti: lambda e: e.activation(out=Th[ti][:], in_=bst[:], func=AF.Exp))(ti), reads=[bBst], writes=[bTh[ti]])
                P.op("dve", (lambda ti: lambda e: e.tensor_tensor(out=Th[ti][:], in0=Th[ti][:], in1=mtS[:], op=ALU.mult))(ti),
                     reads=[bTh[ti], bC], writes=[bTh[ti]])
            for (src, srcb, dstl, dstb, prev) in ((vT[:, c, :], [bV[c]], Vown[pp], bVo[pp], False),
                                                  (vTp[pp], bVTp[pp], Vprv[pp], bVp[pp], True)):
                for half in range(2):
                    bank = tbank[half]
                    bb = [bBank[half]]

                    def fn(e, src=src, half=half, bank=bank):
                        ins = None
                        for j in range(4):
                            jj = half * 4 + j
                            ins = e.matmul(bank[:, j * 128:(j + 1) * 128], lhsT=src[:, jj * 128:(jj + 1) * 128], rhs=identb[:],
                                           start=True, stop=True)
                        return ins
                    P.op("pe", fn, reads=list(srcb) + [bC], writes=bb)
                    bv = bank.rearrange("p (j f) -> p j f", f=128)
                    for (dc, sc_) in ((0, 0), (128, 64)):
                        if prev:
                            P.op("dve", (lambda dstl, half, dc, sc_, bv: lambda e: e.tensor_scalar(
                                out=dstl[:, half * 4:half * 4 + 4, dc:dc + 64], in0=bv[:, :, sc_:sc_ + 64], scalar1=pv(FLAG), scalar2=None,
                                op0=ALU.mult))(dstl, half, dc, sc_, bv), reads=bb + [bC], writes=dstb)
                        else:
                            P.op("act", (lambda dstl, half, dc, sc_, bv: lambda e: e.activation(
                                out=dstl[:, half * 4:half * 4 + 4, dc:dc + 64], in_=bv[:, :, sc_:sc_ + 64], func=AF.Copy))(dstl, half, dc, sc_, bv),
                                reads=bb, writes=dstb)

        def attn_head(c, pp, hh):
            po = 64 * hh
            ti = pp * 2 + hh
            Oacc = Oaccs[hh]
            steps = []
            for jp in range(8):
                for h in range(2):
                    steps.append((True, jp, 512 * h, 512 * h + 512, 1024 - 128 * jp + 512 * h))
            for j in range(8):
                for h in range(2):
                    a_, b_ = max(128 * j, 512 * h), 512 * h + 512
                    if a_ < b_:
                        steps.append((False, j, a_, b_, a_ - 128 * j))
            n = len(steps)
            firsts = {}
            lasts = {}
            for si, st in enumerate(steps):
                bk = st[2] // 512
                firsts.setdefault(bk, si)
                lasts[bk] = si

            def S_op(si):
                prev, j, a_, b_, ts = steps[si]
                sap, sbk = Sb[si % 4]
                if prev:
                    ksrc, kb = kTp[pp][po:po + 64, j * 128:(j + 1) * 128], bKp[pp]
                else:
                    ksrc, kb = kT[po:po + 64, c, j * 128:(j + 1) * 128], [bK[c]]
                P.op("pe", lambda e: e.matmul(sap[:, 0:b_ - a_], lhsT=ksrc, rhs=qT[po:po + 64, c, a_:b_], start=True, stop=True),
                     reads=list(kb) + [bQ[c]], writes=sbk, lhs=list(kb))

            def EP_op(si):
                prev, j, a_, b_, ts = steps[si]
                sap, sbk = Sb[si % 4]
                w_ = b_ - a_
                k = si % 4
                P.op("act", lambda e: e.activation(out=EbS[k][:, 0:w_], in_=sap[:, 0:w_], func=AF.Exp, scale=0.125),
                     reads=sbk, writes=[bE4[k]])
                P.op("dve", lambda e: e.tensor_tensor(out=PbS[k][:, 0:w_], in0=EbS[k][:, 0:w_], in1=Th[ti][:, ts:ts + w_], op=ALU.mult),
                     reads=[bE4[k], bTh[ti]], writes=[bP4[k]])

            def PV_op(si):
                prev, j, a_, b_, ts = steps[si]
                vt = (Vprv if prev else Vown)[pp]
                vb = (bVp if prev else bVo)[pp]
                lhs = vt[:, j, 0:128] if hh == 0 else vt[:, j, 64:192]
                bk = a_ // 512
                k = si % 4
                P.op("pe", lambda e: e.matmul(Oacc[0][:, a_:b_], lhsT=lhs, rhs=PbS[k][:, 0:b_ - a_],
                                              start=(firsts[bk] == si), stop=(lasts[bk] == si)),
                     reads=list(vb) + [bP4[k]], writes=Oacc[1], lhs=list(vb))

            for si in range(min(3, n)):
                S_op(si)
            for si in range(n):
                EP_op(si)
                PV_op(si)
                if si + 3 < n:
                    S_op(si + 3)
            dpo = 64 - po
            P.op("act", lambda e: e.activation(out=rden[po:po + 64, :], in_=Oacc[0][dpo:dpo + 64, :], func=AF.Ln), reads=Oacc[1], writes=[bRd])
            P.op("act", lambda e: e.activation(out=rden[po:po + 64, :], in_=rden[po:po + 64, :], func=AF.Exp, scale=-1.0), reads=[bRd], writes=[bRd])
            P.op("dve", lambda e: e.tensor_tensor(out=oT[po:po + 64, c, 0:T], in0=Oacc[0][po:po + 64, :], in1=rden[po:po + 64, :],
                                                  op=ALU.mult), reads=Oacc[1] + [bRd], writes=[bO[c]])

        build_v(0, 0)
        for c in range(8):
            if c + 1 < 8:
                build_v(c + 1, (c + 1) % 2)
            for hh in range(2):
                attn_head(c, c % 2, hh)

        mark('attn')
        bPt = bf("ptl")
        P.dma("sp", (lambda l: lambda e: e.dma_start(out=ptl[:], in_=bout2s[l].ap()[0:128, :].rearrange("p (c r) -> p c r", r=2)))(l),
              reads=[bB2o], writes=[bPt])
        P.op("dve", lambda e: e.tensor_scalar(out=ptl[:], in0=ptl[:], scalar1=pv(FLAG), scalar2=None, op0=ALU.mult),
             reads=[bPt, bC], writes=[bPt])
        W0, W1 = parS[:, G_CW:G_CW + 8], parS[:, G_CW + 8:G_CW + 16]
        f0, f1, f2 = sm1[:, 0, :], sm1[:, 1, :], sm1[:, 2, :]
        bF = bSm
        P.op("dve", lambda e: e.tensor_tensor(out=f0, in0=ptl[:, :, 0], in1=W0, op=ALU.mult), reads=[bPt, bC], writes=[bF])
        P.op("dve", lambda e: e.tensor_tensor(out=f1, in0=ptl[:, :, 1], in1=W1, op=ALU.mult), reads=[bPt, bC, bF], writes=[bF])
        P.op("dve", lambda e: e.tensor_tensor(out=f0, in0=f0, in1=f1, op=ALU.add), reads=[bF], writes=[bF])
        P.op("dve", lambda e: e.tensor_tensor(out=f0, in0=f0, in1=y02[:, :, 0], op=ALU.add), reads=[bF, bY02], writes=[bF])
        P.op("dve", lambda e: e.tensor_tensor(out=zT[:, :, 0], in0=f0, in1=gb02[:, :, 0], op=ALU.mult), reads=[bF, bG02], writes=bZ)
        P.op("dve", lambda e: e.tensor_tensor(out=f2, in0=ptl[:, :, 1], in1=W0, op=ALU.mult), reads=[bPt, bC, bF], writes=[bF])
        P.op("dve", lambda e: e.tensor_tensor(out=f2, in0=f2, in1=y02[:, :, 1], op=ALU.add), reads=[bF, bY02], writes=[bF])
        P.op("dve", lambda e: e.tensor_tensor(out=zT[:, :, 1], in0=f2, in1=gb02[:, :, 1], op=ALU.mult), reads=[bF, bG02], writes=bZ)
        for (src, sbuf_, acc, rt, rb, gbase) in ((oT, bO, accs[0], tA, bTA, G_GA), (zT, bZ, accs[1], tB, bTB, G_GC)):
            for c in range(8):
                P.op("act", (lambda c, src: lambda e: e.activation(out=sqs[:, c, :], in_=src[:, c, :], func=AF.Square))(c, src),
                     reads=[sbuf_[c], bOs], writes=[bS2[c]])
            stat_mm(onesb, [sqs[:, c, :] for c in range(8)], bS2, acc)
            rstd_from(acc, rt, rb, 1.0 / 1024)
            for c in range(8):
                P.op("dve", (lambda c, src, rt, gbase: lambda e: e.scalar_tensor_tensor(
                    out=src[:, c, :], in0=src[:, c, :], scalar=pv(gbase + c), in1=rt[:, 0:TC], op0=ALU.mult, op1=ALU.mult))(c, src, rt, gbase),
                    reads=[sbuf_[c], rb, bC, bOs], writes=[sbuf_[c]])
        allx = XAL
        P.dma("sp", lambda e: e.dma_start(out=xT[:], in_=xsp), reads=[], writes=bX + allx)

        mark('p25')
        osrc = [oT[:, c, :] for c in range(8)] + [zT[:, c, :] for c in range(8)]
        for m in range(16):
            slot = next_piece(); acc = nextacc()
            mm_piece(slot, osrc, list(bO) + list(bZ), acc)
            P.op("dve", (lambda m, acc: lambda e: e.tensor_tensor(out=xT[:, m, :], in0=acc[0], in1=xT[:, m, :], op=ALU.add))(m, acc),
                 reads=acc[1] + [bX[m]], writes=[bX[m]])

        mark('p3')
        rmsnorm_x(G_MLP)

        def up_block(b):
            hb = b % 2
            for j in range(8):
                slot = next_piece(); acc = nextacc()
                mm_piece(slot, hsrc, bH, acc)
                P.op("act", (lambda acc: lambda e: e.activation(out=tCc[:, 0:TC], in_=acc[0], func=AF.Relu))(acc),
                     reads=acc[1], writes=[bTC])
                P.op("dve", (lambda hb, j: lambda e: e.tensor_tensor(out=hid[hb][:, j, :], in0=tCc[:, 0:TC], in1=tCc[:, 0:TC], op=ALU.mult))(hb, j),
                     reads=[bTC], writes=[bHid[hb][j], bQ[j] if hb == 0 else bK[j]])

        def down_block(b):
            hb = b % 2
            hs = [hid[hb][:, j, :] for j in range(8)]
            for g in range(8):
                slot = next_piece()
                for mm_ in range(2):
                    acc = nextacc()
                    m = 2 * g + mm_
                    mm_piece(slot, hs, bHid[hb] + (bQ if hb == 0 else bK), acc, nk=8, wcol=(lambda mm_: lambda kc: (kc * 256 + mm_ * 128, kc * 256 + mm_ * 128 + 128))(mm_))
                    P.op("dve", (lambda m, acc: lambda e: e.tensor_tensor(out=xT[:, m, :], in0=acc[0], in1=xT[:, m, :], op=ALU.add))(m, acc),
                         reads=acc[1] + [bX[m]], writes=[bX[m]])

        up_block(0)
        for b in range(8):
            if b + 1 < 8:
                up_block(b + 1)
            down_block(b)

    try:
        for l in range(nl):
            layer(l)
    except _Stop:
        pass
    names = dict(xT=(xT, bX), hT=(hT, bH), qT=(qT, bQ), kT=(kT, bK), vT=(vT, bV), zT=(zT, bZ), oT=(oT, bH[0:8] + [bf('osample')]),
                 tA=(tA, [bTA]), tB=(tB, [bTB]), tC=(tCc, [bTC]), tU=(tU, [bTU]))
    for dn in dumps:
        t_, bb_ = names[dn]
        shp = list(t_.shape)
        do = dt("dbg_" + dn, shp, F32, kind="ExternalOutput").ap()
        P.dma("pool", (lambda t_, do: lambda e: e.dma_start(out=do, in_=t_[:]))(t_, do), reads=bb_)
    P.dma("sp", lambda e: e.dma_start(out=yout, in_=xT[:]), reads=bX)

    with nc.Block() as block:
        @block.tensor
        def _(e):
            P.emit("pe", e)

        @block.scalar
        def _(e):
            P.emit("act", e)

        @block.vector
        def _(e):
            P.emit("dve", e)

        @block.gpsimd
        def _(e):
            P.emit("pool", e)

        @block.sync
        def _(e):
            P.emit("sp", e)
    return nc


def _weight_stream(w_in, w_out, w_up, w_down):
    out = np.empty((L * PPL, 128, 2048), np.float32)
    i = 0

    def colpiece(W, c0):
        return W[:, c0:c0 + 128].reshape(16, 128, 128).transpose(1, 0, 2).reshape(128, 2048)

    for l in range(L):
        wi = w_in[l]
        order = [1024 + 128 * c for c in range(8)] + [2048 + 128 * c for c in range(8)]
        for c in range(8):
            order += [3072 + 128 * c, 5120 + 128 * c, 4096 + 128 * c]
        order += [128 * c for c in range(8)]
        for c0 in order:
            out[i] = colpiece(wi, c0); i += 1
        for m in range(16):
            out[i] = colpiece(w_out[l], 128 * m); i += 1

        def up(b):
            nonlocal i
            for j in range(8):
                out[i] = colpiece(w_up[l], (8 * b + j) * 128); i += 1

        def down(b):
            nonlocal i
            blk = w_down[l][b * 1024:(b + 1) * 1024]
            for g in range(8):
                out[i] = blk[:, g * 256:(g + 1) * 256].reshape(8, 128, 256).transpose(1, 0, 2).reshape(128, 2048); i += 1
        up(0)
        for b in range(8):
            if b + 1 < 8:
                up(b + 1)
            down(b)
    assert i == L * PPL
    return out


_NC_CACHE = {}
_PREP_ONLY = False


def kernel(x_prompt, x_sample, state_attn_k, state_attn_v, state_conv, rel_bias, norm_mix, w_in, q_norm, k_norm,
           conv_w, attn_out_norm, conv_out_norm, w_out, norm_mlp, w_up, w_down):
    f = lambda a: np.ascontiguousarray(np.asarray(a, dtype=np.float32))
    x_prompt, x_sample, state_attn_k, state_attn_v, state_conv = map(f, (x_prompt, x_sample, state_attn_k, state_attn_v, state_conv))
    rel_bias, norm_mix, q_norm, k_norm, conv_w, attn_out_norm, conv_out_norm, norm_mlp = map(
        f, (rel_bias, norm_mix, q_norm, k_norm, conv_w, attn_out_norm, conv_out_norm, norm_mlp))
    wst = _weight_stream(f(w_in), f(w_out), f(w_up), f(w_down))
    kk = np.arange(128)[:, None]
    cc = np.arange(TW)[None, :]
    dist = cc - kk
    valid = (dist >= 0) & (dist <= 2048)
    dc = np.clip(dist, 0, 2048)
    bidx = _bucket(dc)
    btoe = np.ascontiguousarray(rel_bias[bidx].transpose(2, 0, 1))
    mult = ((dc <= 128).astype(np.float32) + ((dc % 4 == 0) & (dc <= 512)) + ((dc % 16 == 0) & (dc <= 2048))) * valid
    mtoe = mult.astype(np.float32)
    sbias = np.zeros((128, 48), np.float32)
    for br, d in enumerate((1, 4, 16)):
        j = 128 - np.arange(128)
        sbias[:, br * 16:(br + 1) * 16] = rel_bias[_bucket(j * d)]
    sb0 = np.ascontiguousarray(rel_bias[0:1, :])
    cst = np.zeros((128, 4, 128), np.float32)
    cst[:, 0] = np.eye(128)
    cst[:, 1] = 1.0
    cst[0:64, 2, 0:64] = 1.0
    cst[64:128, 2, 64:128] = 1.0
    cst[:, 3] = np.eye(128)
    in_maps = []
    for core in range(8):
        b, hf = core // 2, core % 2
        xin = np.empty((128, 16, TC), np.float32)
        xin[:, :, 0:T] = x_prompt[b, hf * T:(hf + 1) * T].reshape(T, 16, 128).transpose(2, 1, 0)
        xin[:, :, T] = x_sample[core, 0].reshape(16, 128).T
        par = np.zeros((128, 304), np.float32)
        for l in range(L):
            pb = l * NPARL
            par[:, pb:pb + 16] = norm_mix[l].reshape(16, 128).T
            par[:, pb + 16:pb + 32] = norm_mlp[l].reshape(16, 128).T
            par[:, pb + 32] = np.tile(q_norm[l], 2)
            par[:, pb + 33] = np.tile(k_norm[l], 2)
            for i in range(3):
                par[:, pb + 34 + 8 * i:pb + 42 + 8 * i] = conv_w[l, i].reshape(8, 128).T
            par[:, pb + 58:pb + 66] = attn_out_norm[l].reshape(8, 128).T
            par[:, pb + 66:pb + 74] = conv_out_norm[l].reshape(8, 128).T
        par[:, 296] = float(hf)
        sconv = np.ascontiguousarray(state_conv[:, core].reshape(L, 2, 8, 128).transpose(3, 0, 2, 1))
        in_maps.append({
            "xin": xin, "wst": wst, "par": par, "sconv": sconv, "btoe": btoe, "mtoe": mtoe, "sbias": sbias, "sb0": sb0,
            "cst": cst, "kst": np.ascontiguousarray(state_attn_k[:, core].reshape(L, 2048, 1024)),
            "vst": np.ascontiguousarray(state_attn_v[:, core].reshape(L, 2048, 1024)),
        })
    if _PREP_ONLY:
        return in_maps
    if "nc" not in _NC_CACHE:
        _NC_CACHE["nc"] = build()
    res = run_bass_kernel_spmd(_NC_CACHE["nc"], in_maps, core_ids=list(range(8))).results
    y_p = np.empty((4, 2048, 2048), np.float32)
    y_s = np.empty((8, 1, 2048), np.float32)
    nk = np.empty((L, 4, 2048, 16, 64), np.float32)
    nv = np.empty((L, 4, 2048, 16, 64), np.float32)
    ncp = np.empty((L, 4, 2, 1024), np.float32)
    nks = np.empty((L, 8, 2048, 16, 64), np.float32)
    nvs = np.empty((L, 8, 2048, 16, 64), np.float32)
    ncs = np.empty((L, 8, 2, 1024), np.float32)
    for core in range(8):
        b, hf = core // 2, core % 2
        r = res[core]
        yo = r["yout"]
        y_p[b, hf * T:(hf + 1) * T] = yo[:, :, 0:T].transpose(2, 1, 0).reshape(T, 2048)
        y_s[core, 0] = yo[:, :, T].T.reshape(2048)
        nk[:, b, hf * T:(hf + 1) * T] = r["kTo"].transpose(0, 3, 2, 1).reshape(L, T, 16, 64)
        nv[:, b, hf * T:(hf + 1) * T] = r["vTo"].transpose(0, 3, 2, 1).reshape(L, T, 16, 64)
        if hf == 1:
            ncp[:, b] = r["cpo"].transpose(0, 3, 2, 1).reshape(L, 2, 1024)
        nks[:, core] = r["kso"].reshape(L, 2048, 16, 64)
        nvs[:, core] = r["vso"].reshape(L, 2048, 16, 64)
        ncs[:, core] = r["cso"].transpose(0, 3, 2, 1).reshape(L, 2, 1024)
    return (y_p, y_s, nk, nv, ncp, nks, nvs, ncs)
```

```python
import numpy as np
import concourse.bass as bass
import concourse.mybir as mybir
from concourse.bass_utils import run_bass_kernel_spmd

F32 = mybir.dt.float32
BF16 = mybir.dt.bfloat16
ALU = mybir.AluOpType
AF = mybir.ActivationFunctionType
AX = mybir.AxisListType

L = 4
T = 1024
TC = 1025
NSLOT = 5
PPL = 192
NPARL = 74
EPS = 1e-6
TW = 2176
TILES = ((0, 512), (512, 1024), (1024, 1025))


def _bucket(dist):
    n_exact = 16
    large = n_exact + (np.log(np.maximum(dist, 1) / n_exact) / np.log(2048 / n_exact) * (32 - n_exact)).astype(np.int32)
    large = np.minimum(large, 31)
    return np.where(dist < n_exact, dist, large).astype(np.int32)


ATTACH = True


class Buf:
    __slots__ = ("w", "r", "name")

    def __init__(self, name=""):
        self.w = None
        self.r = []
        self.name = name


class Plan:
    ENG = ("pe", "act", "dve", "pool", "sp")

    def __init__(self, nc, semfn):
        self.nc = nc
        self.semfn = semfn
        self.q = {e: [] for e in self.ENG}
        self.cnt = {e: 0 for e in self.ENG}
        self.sem = {e: semfn("c_" + e) for e in ("pe", "act", "dve", "pool")}
        self.waited = {e: {} for e in self.ENG}
        nds = 12
        self.dsems = {e: [semfn("d_%s%d" % (e, i)) for i in range(nds)] for e in ("pool", "sp")}
        self.dval = {e: [0] * nds for e in ("pool", "sp")}
        self.dslot = {e: 0 for e in ("pool", "sp")}

    def new_layer_sems(self, l):
        for e in ("pe", "act", "dve", "pool"):
            self.sem[e] = self.semfn("c_%s_%d" % (e, l))
            self.cnt[e] = 0

    def _waits(self, eng, reads, writes, extra):
        need = {}
        lst = list(extra)
        for b in reads:
            if b.w is not None:
                lst.append(b.w)
        for b in writes:
            if b.w is not None and not b.r:
                lst.append(b.w)
            lst.extend(b.r)
        for t in lst:
            if t is None:
                continue
            k = id(t[0])
            if k not in need or need[k][1] < t[1]:
                need[k] = t
        final = []
        for k, (sem, val) in need.items():
            if self.waited[eng].get(k, 0) >= val:
                continue
            self.waited[eng][k] = val
            final.append((sem, val))
        return final

    def op(self, eng, fn, reads=(), writes=(), extra=(), lhs=None):
        final = self._waits(eng, reads, writes, extra)
        att = None
        if ATTACH and final and (eng != "pe" or lhs is not None):
            lhs_ids = set(id(b.w[0]) for b in (lhs or ()) if b.w is not None)
            cand = [w for w in final if id(w[0]) not in lhs_ids]
            if cand:
                att = cand[-1]
                final = [w for w in final if w is not att]
        self.cnt[eng] += 1
        tk = (self.sem[eng], self.cnt[eng])
        self.q[eng].append((final, fn, tk, 1, att))
        for b in reads:
            b.r.append(tk)
        for b in writes:
            b.w = tk
            b.r = []
        return tk

    def dma(self, eng, fn, reads=(), writes=(), extra=()):
        slot = self.dslot[eng]
        self.dslot[eng] = (slot + 1) % len(self.dsems[eng])
        sem = self.dsems[eng][slot]
        prev = self.dval[eng][slot]
        ex = list(extra)
        if prev > 0:
            ex.append((sem, prev))
        final = self._waits(eng, reads, writes, ex)
        self.dval[eng][slot] = prev + 16
        tk = (sem, prev + 16)
        self.q[eng].append((final, fn, tk, 16, None))
        for b in reads:
            b.r.append(tk)
        for b in writes:
            b.w = tk
            b.r = []
        return tk

    def coll(self, fn, reads=(), writes=()):
        sem = self.semfn("cc%d" % len(self.q["pool"]))
        final = self._waits("pool", reads, writes, ())
        tk = (sem, 1)
        self.q["pool"].append((final, fn, tk, 0, None))
        for b in reads:
            b.r.append(tk)
        for b in writes:
            b.w = tk
            b.r = []
        return tk

    def emit(self, eng, e):
        for final, fn, tk, inc, att in self.q[eng]:
            for sem, val in final:
                e.wait_ge(sem, val)
            ins = fn(e)
            if isinstance(ins, tuple):
                first, ins = ins
            else:
                first = ins
            if att is not None:
                first._wait_ge(att[0], att[1])
            if inc == 0:
                ins.then_inc(tk[0])
            else:
                ins.then_inc(tk[0], inc)
        if eng in ("pool", "sp"):
            for i, sem in enumerate(self.dsems[eng]):
                if self.dval[eng][i] > 0:
                    e.wait_ge(sem, self.dval[eng][i])


class _Stop(Exception):
    pass


def build(nl=L, stop=None, dumps=(), npieces=None):
    nc = bass.Bass("TRN2", target_bir_lowering=False)

    def mark(name):
        if stop is not None and name == stop:
            raise _Stop()
    dt = nc.dram_tensor
    xin = dt("xin", [128, 16, TC], F32, kind="ExternalInput").ap()
    wst = dt("wst", [npieces or nl * PPL, 128, 2048], F32, kind="ExternalInput").ap()
    par = dt("par", [128, 304], F32, kind="ExternalInput").ap()
    sconv = dt("sconv", [128, L, 8, 2], F32, kind="ExternalInput").ap()
    btoe = dt("btoe", [16, 128, TW], F32, kind="ExternalInput").ap()
    mtoe = dt("mtoe", [128, TW], F32, kind="ExternalInput").ap()
    sbias = dt("sbias", [128, 48], F32, kind="ExternalInput").ap()
    sb0 = dt("sb0", [1, 16], F32, kind="ExternalInput").ap()
    cst = dt("cst", [128, 4, 128], F32, kind="ExternalInput").ap()
    kst = dt("kst", [nl, 2048, 1024], F32, kind="ExternalInput").ap()
    vst = dt("vst", [nl, 2048, 1024], F32, kind="ExternalInput").ap()
    yout = dt("yout", [128, 16, TC], F32, kind="ExternalOutput").ap()
    kTo = dt("kTo", [L, 128, 8, T], F32, kind="ExternalOutput").ap()
    vTo = dt("vTo", [L, 128, 8, T], F32, kind="ExternalOutput").ap()
    cpo = dt("cpo", [L, 128, 8, 2], F32, kind="ExternalOutput").ap()
    kso = dt("kso", [nl, 2048, 1024], F32, kind="ExternalOutput").ap()
    vso = dt("vso", [nl, 2048, 1024], F32, kind="ExternalOutput").ap()
    cso = dt("cso", [L, 128, 8, 2], F32, kind="ExternalOutput").ap()
    xsp = dt("xsp", [128, 16, TC], F32).ap()
    bins = [[dt("bin%d_%d" % (l, c), [256, T], BF16) for c in range(8)] for l in range(L)]
    bouts = [[dt("bout%d_%d" % (l, c), [512, T], BF16) for c in range(8)] for l in range(L)]
    bin2s = [dt("binb%d" % l, [128, 16], F32) for l in range(L)]
    bout2s = [dt("boutb%d" % l, [256, 16], F32) for l in range(L)]

    off = [16512]

    def alloc(name, shape, dtp, at=None):
        esz = 2 if dtp == BF16 else 4
        nb = int(np.prod(shape[1:])) * esz
        nb = (nb + 31) // 32 * 32
        if at is None:
            o = off[0]
            off[0] += nb
        else:
            o = at
        return nc.alloc_sbuf_tensor_at(name, list(shape), dtp, offset=o), o, nb

    ring = []
    for i in range(NSLOT):
        t_, _, _ = alloc("ring%d" % i, [128, 2048], BF16)
        ring.append(t_)
    identb, _, _ = alloc("identb", [128, 128], BF16)
    onesb, _, _ = alloc("onesb", [128, 128], BF16)
    blkb, _, _ = alloc("blkb", [128, 128], BF16)
    cstf, _, _ = alloc("cstf", [128, 2, 128], F32)
    parS, _, _ = alloc("parS", [128, 304], F32)
    scS, _, _ = alloc("scS", [128, L, 8, 2], F32)
    sbS, _, _ = alloc("sbS", [128, 48], F32)
    sb0S, _, _ = alloc("sb0S", [1, 16], F32)
    mtS, _, _ = alloc("mtS", [128, TW], BF16)
    tail, _, _ = alloc("tail", [128, 8, 2], F32)
    y02, _, _ = alloc("y02", [128, 8, 2], F32)
    gb02, _, _ = alloc("gb02", [128, 8, 2], F32)
    csst, _, _ = alloc("csst", [128, 8, 2], F32)
    ptl, _, _ = alloc("ptl", [128, 8, 2], F32)
    sm1, _, _ = alloc("sm1", [128, 8, 8], F32)
    epsS, _, _ = alloc("epsS", [128, 1], F32)
    xT, XO, _ = alloc("xT", [128, 16, TC], F32)
    hT, HO, _ = alloc("hT", [128, 16, TC], BF16)
    off[0] += 32
    qT, QO, _ = alloc("qT", [128, 8, TC], BF16)
    kT, KO, _ = alloc("kT", [128, 8, TC], BF16)
    vT, _, _ = alloc("vT", [128, 8, TC], BF16)
    zT, _, _ = alloc("zT", [128, 8, TC], BF16)
    tA, _, _ = alloc("tA", [128, 1032], F32)
    tB, _, _ = alloc("tB", [128, 1032], F32)
    tCc, _, _ = alloc("tC", [128, 1032], F32)
    tU, _, _ = alloc("tU", [128, 1032], F32)
    tD, _, _ = alloc("tD", [128, 1032], BF16)
    assert off[0] <= 229376, off[0]
    oT, _, _ = alloc("oT", [128, 8, TC], BF16, at=HO)
    sqs, _, _ = alloc("sqs", [128, 8, TC], BF16, at=HO + 16416)
    hid = [alloc("hid0", [128, 8, TC], BF16, at=QO)[0], alloc("hid1", [128, 8, TC], BF16, at=KO)[0]]
    xo = [XO]

    def xalloc(name, shape, dtp):
        t_, o, nb = alloc(name, shape, dtp, at=xo[0])
        xo[0] += nb
        return t_

    Vown = [xalloc("Vown%d" % i, [128, 8, 192], BF16) for i in range(2)]
    Vprv = [xalloc("Vprv%d" % i, [128, 8, 192], BF16) for i in range(2)]
    kTp = [xalloc("kTp%d" % i, [128, T], BF16) for i in range(2)]
    vTp = [xalloc("vTp%d" % i, [128, T], BF16) for i in range(2)]
    Th = [xalloc("Th%d" % i, [128, TW], BF16) for i in range(4)]
    Eb = [xalloc("Eb%d" % i, [128, T], BF16) for i in range(2)]
    Pb = [xalloc("Pb%d" % i, [128, T], BF16) for i in range(2)]
    bst = xalloc("bst", [128, TW], F32)
    rden = xalloc("rden", [128, T], F32)
    vrow = xalloc("vrow", [1, T], F32)
    assert xo[0] <= XO + 65600, xo[0] - XO
    ho = [HO + 16416]

    def halloc(name, shape, dtp):
        t_, o, nb = alloc(name, shape, dtp, at=ho[0])
        ho[0] += nb
        return t_

    qb = halloc("qb", [128, T], F32)
    Kt = halloc("Kt", [128, T], F32)
    Vt2 = halloc("Vt2", [128, T], F32)
    assert ho[0] <= HO + 32832

    sems = []

    def semfn(name):
        s = nc.semaphore(name)
        h = s.__enter__()
        sems.append(s)
        return h

    ps_cm = nc.psum_tensor("ps", [128, 8, 512], F32)
    ps = ps_cm.__enter__()
    psf = ps[:].rearrange("p b n -> p (b n)")
    P = Plan(nc, semfn)

    B = {}

    def bf(name):
        if name not in B:
            B[name] = Buf(name)
        return B[name]

    def bl(prefix, n):
        return [bf("%s%d" % (prefix, i)) for i in range(n)]

    bX = bl("x", 16)
    bH = bl("h", 16)
    bQ = bl("q", 8)
    bK = bl("k", 8)
    bV = bl("v", 8)
    bZ = bl("z", 8)
    bRing = bl("ring", NSLOT)
    bBank = bl("bank", 8)
    bTA, bTB, bTC, bTU, bTD = bf("tA"), bf("tB"), bf("tC"), bf("tU"), bf("tD")
    bHid = [bl("hid0_", 8), bl("hid1_", 8)]
    accs = [(psf[:, 0:TC], [bBank[0], bBank[1], bBank[2]]), (psf[:, 1536:1536 + TC], [bBank[3], bBank[4], bBank[5]])]
    STAT = (psf[:, 3072:4096], [bBank[6], bBank[7]])

    def pv(i):
        return parS[:, i:i + 1]

    bC = bf("consts")
    P.dma("sp", lambda e: e.dma_start(out=xT[:], in_=xin), writes=bX)
    P.dma("sp", lambda e: e.dma_start(out=parS[:], in_=par), writes=[bC])
    P.dma("sp", lambda e: e.dma_start(out=scS[:], in_=sconv), writes=[bC])
    P.dma("sp", lambda e: e.dma_start(out=sbS[:], in_=sbias), writes=[bC])
    P.dma("sp", lambda e: e.dma_start(out=sb0S[:], in_=sb0), writes=[bC])
    P.dma("sp", lambda e: e.dma_start(out=cstf[:, 0, :], in_=cst[:, 0, :]), writes=[bC])
    P.dma("sp", lambda e: e.dma_start(out=cstf[:, 1, :], in_=cst[:, 1, :]), writes=[bC])
    P.dma("pool", lambda e: e.dma_start(out=identb[:], in_=cst[:, 0, :]), writes=[bC])
    P.dma("pool", lambda e: e.dma_start(out=onesb[:], in_=cst[:, 1, :]), writes=[bC])
    P.dma("pool", lambda e: e.dma_start(out=blkb[:], in_=cst[:, 2, :]), writes=[bC])
    P.dma("pool", lambda e: e.dma_start(out=mtS[:], in_=mtoe), writes=[bC])
    identf = cstf[:, 0, :]
    onesf = cstf[:, 1, :]
    P.op("dve", lambda e: e.memset(epsS[:], EPS), writes=[bC])
    P.op("dve", lambda e: e.memset(tU[:, 0:2], 0.0), writes=[bTU])
    FLAG = 296
    bVo = [bl("vo0_", 1), bl("vo1_", 1)]
    bVp = [bl("vp0_", 1), bl("vp1_", 1)]
    bKp = [bl("kTp0_", 1), bl("kTp1_", 1)]
    bVTp = [bl("vTp0_", 1), bl("vTp1_", 1)]
    bTh = [bf("Th%d" % i) for i in range(4)]
    bBst = bf("bst")
    bE = [bf("E0"), bf("E1")]
    bP = [bf("P0"), bf("P1")]
    bRd = bf("rden")
    bVr = bf("vrow")
    bSm = bf("sm1")
    XAL0 = [bVo[0][0], bVo[1][0], bVp[0][0], bVp[1][0], bKp[0][0], bKp[1][0], bVTp[0][0], bVTp[1][0]] + bTh + bE + bP + [bBst, bRd, bVr]
    XAL = XAL0 + [bf("E4_%d" % i) for i in range(4)] + [bf("P4_%d" % i) for i in range(4)]
    QROW = tCc[0:1, 0:T]
    KROW = tU[0:1, 2:2 + T]
    VROW = vrow[0:1, :]
    ROWS = [QROW, KROW, VROW]

    ws = {"next": 0, "cons": 0}

    def fetch_upto(n):
        while ws["next"] < min(n, npieces or nl * PPL):
            i = ws["next"]
            s = i % NSLOT
            P.dma("pool", (lambda i, s: lambda e: e.dma_start(out=ring[s][:], in_=wst[i]))(i, s), writes=[bRing[s]])
            ws["next"] += 1

    def next_piece():
        i = ws["cons"]
        ws["cons"] += 1
        fetch_upto(i + NSLOT)
        return i % NSLOT

    def mm_piece(slot, srcs, sbufs, acc, nk=16, wcol=lambda kc: (kc * 128, kc * 128 + 128)):
        accap, accb = acc

        def fn(e):
            ins = None
            first = None
            for kc in range(nk):
                a, b = wcol(kc)
                for lo, hi in TILES:
                    ins = e.matmul(accap[:, lo:hi], lhsT=ring[slot][:, a:b], rhs=srcs[kc][:, lo:hi],
                                   start=(kc == 0), stop=(kc == nk - 1))
                    if first is None:
                        first = ins
            return (first, ins)
        return P.op("pe", fn, reads=[bRing[slot]] + list(sbufs), writes=accb, lhs=[bRing[slot]])

    def stat_mm(lhs, srcs, sbufs, acc):
        accap, accb = acc
        n = len(srcs)

        def fn(e):
            ins = None
            for kc in range(n):
                for lo, hi in TILES:
                    ins = e.matmul(accap[:, lo:hi], lhsT=lhs[:], rhs=srcs[kc][:, lo:hi], start=(kc == 0), stop=(kc == n - 1))
            return ins
        return P.op("pe", fn, reads=[bC] + list(sbufs), writes=accb)

    def rstd_from(acc, out_t, out_b, scale):
        accap, accb = acc
        P.op("act", lambda e: e.activation(out=out_t[:, 0:TC], in_=accap, func=AF.Ln, bias=epsS[:], scale=scale),
             reads=accb + [bC], writes=[out_b])
        P.op("act", lambda e: e.activation(out=out_t[:, 0:TC], in_=out_t[:, 0:TC], func=AF.Exp, scale=-0.5), reads=[out_b], writes=[out_b])

    def rmsnorm_x(gbase):
        for c in range(16):
            P.op("act", (lambda c: lambda e: e.activation(out=hT[:, c, :], in_=xT[:, c, :], func=AF.Square))(c),
                 reads=[bX[c]], writes=[bH[c]])
        stat_mm(onesb, [hT[:, c, :] for c in range(16)], bH, accs[0])
        rstd_from(accs[0], tA, bTA, 1.0 / 2048)
        for c in range(16):
            eng = "dve"
            P.op(eng, (lambda c: lambda e: e.scalar_tensor_tensor(out=hT[:, c, :], in0=xT[:, c, :], scalar=pv(gbase + c),
                                                                  in1=tA[:, 0:TC], op0=ALU.mult, op1=ALU.mult))(c),
                 reads=[bX[c], bTA, bC], writes=[bH[c]])

    hsrc = [hT[:, c, :] for c in range(16)]
    accsel = [0]

    def nextacc():
        accsel[0] ^= 1
        return accs[accsel[0]]

    def layer(l):
        if l > 0:
            P.new_layer_sems(l)
        pb = l * NPARL
        G_MIX, G_MLP, G_Q, G_K, G_CW, G_GA, G_GC = pb, pb + 16, pb + 32, pb + 33, pb + 34, pb + 58, pb + 66
        for (dst_, src_) in ((kso, kst), (vso, vst)):
            P.dma("sp", (lambda l, dst_, src_: lambda e: e.dma_start(
                out=dst_[l, 0:2047, :].rearrange("(a b) c -> a (b c)", b=23),
                in_=src_[l, 1:2048, :].rearrange("(a b) c -> a (b c)", b=23)))(l, dst_, src_))
        rmsnorm_x(G_MIX)
        sp_tk = P.dma("sp", lambda e: e.dma_start(out=xsp, in_=xT[:]), reads=bX)
        for b_ in XAL:
            b_.w = sp_tk
            b_.r = []
        for i in range(2):
            P.op("dve", (lambda i: lambda e: e.memset(Vown[i][:, :, 64:128], 1.0))(i), writes=bVo[i])
            P.op("dve", (lambda i: lambda e: e.memset(Vprv[i][:, :, 64:128], 1.0))(i), writes=bVp[i])
            P.op("dve", (lambda i: lambda e: e.tensor_scalar(out=Vprv[i][:, :, 64:128], in0=Vprv[i][:, :, 64:128],
                                                              scalar1=pv(FLAG), scalar2=None, op0=ALU.mult))(i),
                 reads=[bC], writes=bVp[i])

        mark('p0')
        def qk_piece(dst, dbuf, c, gidx):
            slot = next_piece()
            mm_piece(slot, hsrc, bH, accs[0])
            P.op("dve", lambda e: e.tensor_copy(out=tA[:, 0:TC], in_=accs[0][0]), reads=accs[0][1], writes=[bTA])
            P.op("act", lambda e: e.activation(out=tD[:, 0:TC], in_=tA[:, 0:TC], func=AF.Square), reads=[bTA], writes=[bTD])
            stat_mm(blkb, [tD[:, 0:TC]], [bTD], accs[1])
            rstd_from(accs[1], tB, bTB, 1.0 / 64)
            P.op("dve", lambda e: e.scalar_tensor_tensor(out=dst[:, c, :], in0=tA[:, 0:TC], scalar=pv(gidx), in1=tB[:, 0:TC],
                                                         op0=ALU.mult, op1=ALU.mult),
                 reads=[bTA, bTB, bC], writes=[dbuf[c]])

        for c in range(8):
            qk_piece(kT, bK, c, G_K)
        bBin = [bf("bin%d_%d" % (l, c)) for c in range(8)]
        bBout = [bf("bout%d_%d" % (l, c)) for c in range(8)]
        for c in range(8):
            slot = next_piece()
            acc = nextacc()
            mm_piece(slot, hsrc, bH, acc)
            P.op("act", (lambda c, acc: lambda e: e.activation(out=vT[:, c, :], in_=acc[0], func=AF.Copy))(c, acc),
                 reads=acc[1], writes=[bV[c]])
            P.dma("sp", (lambda l, c: lambda e: e.dma_start(out=bins[l][c].ap()[0:128, :], in_=kT[:, c, 0:T]))(l, c),
                  reads=[bK[c]], writes=[bBin[c]])
            P.dma("sp", (lambda l, c: lambda e: e.dma_start(out=bins[l][c].ap()[128:256, :], in_=vT[:, c, 0:T]))(l, c),
                  reads=[bV[c]], writes=[bBin[c]])
            P.coll((lambda l, c: lambda e: e.collective_compute("AllGather", ALU.bypass,
                                                                replica_groups=[[0, 1], [2, 3], [4, 5], [6, 7]],
                                                                ins=[bins[l][c].ap()], outs=[bouts[l][c].ap()]))(l, c),
                   reads=[bBin[c]], writes=[bBout[c]])
        mark('x2')
        P.dma("pool", (lambda l: lambda e: e.dma_start(out=kTo[l], in_=kT[:, :, 0:T]))(l), reads=bK)
        P.dma("pool", (lambda l: lambda e: e.dma_start(out=vTo[l], in_=vT[:, :, 0:T]))(l), reads=bV)
        mark('xchg')
        bTail, bY02, bG02, bCs = bf("tail"), bf("y02"), bf("gb02"), bf("csst")
        for c in range(8):
            w0, w1, w2 = pv(G_CW + c), pv(G_CW + 8 + c), pv(G_CW + 16 + c)
            slot = next_piece(); acc = nextacc()
            mm_piece(slot, hsrc, bH, acc)
            P.op("act", (lambda acc: lambda e: e.activation(out=tA[:, 0:TC], in_=acc[0], func=AF.Copy))(acc),
                 reads=acc[1], writes=[bTA])
            mark('cv1')
            slot = next_piece(); acc = nextacc()
            mm_piece(slot, hsrc, bH, acc)
            P.op("dve", (lambda acc: lambda e: e.tensor_tensor(out=tU[:, 2:2 + TC], in0=acc[0], in1=tA[:, 0:TC], op=ALU.mult))(acc),
                 reads=acc[1] + [bTA], writes=[bTU])
            mark('cv2')
            P.op("act", (lambda c: lambda e: e.activation(out=tail[:, c, :], in_=tU[:, 1024:1026], func=AF.Copy))(c),
                 reads=[bTU], writes=[bTail])
            P.op("act", (lambda c: lambda e: e.activation(out=csst[:, c, 1:2], in_=tU[:, 1026:1027], func=AF.Copy))(c),
                 reads=[bTU], writes=[bCs])
            P.op("act", (lambda c, l: lambda e: e.activation(out=csst[:, c, 0:1], in_=scS[:, l, c, 1:2], func=AF.Copy))(c, l),
                 reads=[bC], writes=[bCs])
            mark('cv3')
            P.op("dve", (lambda w2: lambda e: e.tensor_scalar(out=tCc[:, 0:T], in0=tU[:, 2:2 + T], scalar1=w2, scalar2=None,
                                                              op0=ALU.mult))(w2), reads=[bTU, bC], writes=[bTC])
            P.op("dve", (lambda w1: lambda e: e.scalar_tensor_tensor(out=tCc[:, 0:T], in0=tU[:, 1:1 + T], scalar=w1, in1=tCc[:, 0:T],
                                                                     op0=ALU.mult, op1=ALU.add))(w1), reads=[bTU, bTC, bC], writes=[bTC])
            P.op("dve", (lambda w0: lambda e: e.scalar_tensor_tensor(out=tCc[:, 0:T], in0=tU[:, 0:T], scalar=w0, in1=tCc[:, 0:T],
                                                                     op0=ALU.mult, op1=ALU.add))(w0), reads=[bTU, bTC, bC], writes=[bTC])
            mark('cv4')
            P.op("act", (lambda c, l, w0: lambda e: e.activation(out=tCc[:, T:TC], in_=scS[:, l, c, 0:1], func=AF.Identity, scale=w0))(c, l, w0),
                 reads=[bC], writes=[bTC])
            P.op("act", (lambda c, l, w1: lambda e: e.activation(out=tCc[:, T:TC], in_=scS[:, l, c, 1:2], func=AF.Identity, scale=w1,
                                                                 bias=tCc[:, T:TC]))(c, l, w1), reads=[bC, bTC], writes=[bTC])
            P.op("act", (lambda w2: lambda e: e.activation(out=tCc[:, T:TC], in_=tU[:, 1026:1027], func=AF.Identity, scale=w2,
                                                           bias=tCc[:, T:TC]))(w2), reads=[bTU, bTC, bC], writes=[bTC])
            mark('cv5')
            P.op("act", (lambda c: lambda e: e.activation(out=y02[:, c, :], in_=tCc[:, 0:2], func=AF.Copy))(c),
                 reads=[bTC], writes=[bY02])
            slot = next_piece(); acc = nextacc()
            mm_piece(slot, hsrc, bH, acc)
            P.op("dve", (lambda c, acc: lambda e: e.tensor_tensor(out=zT[:, c, :], in0=acc[0], in1=tCc[:, 0:TC], op=ALU.mult))(c, acc),
                 reads=acc[1] + [bTC], writes=[bZ[c]])
            P.op("act", (lambda c, acc: lambda e: e.activation(out=gb02[:, c, :], in_=acc[0][:, 0:2], func=AF.Copy))(c, acc),
                 reads=[], writes=[bG02] + acc[1])
        mark('conv')
        bB2i, bB2o = bf("b2i%d" % l), bf("b2o%d" % l)
        P.dma("sp", (lambda l: lambda e: e.dma_start(out=bin2s[l].ap().rearrange("p (c r) -> p c r", r=2), in_=tail[:]))(l),
              reads=[bTail], writes=[bB2i])
        P.coll((lambda l: lambda e: e.collective_compute("AllGather", ALU.bypass,
                                                         replica_groups=[[0, 1], [2, 3], [4, 5], [6, 7]],
                                                         ins=[bin2s[l].ap()], outs=[bout2s[l].ap()]))(l),
               reads=[bB2i], writes=[bB2o])
        P.dma("sp", (lambda l: lambda e: e.dma_start(out=cpo[l], in_=tail[:]))(l), reads=[bTail])
        P.dma("sp", (lambda l: lambda e: e.dma_start(out=cso[l], in_=csst[:]))(l), reads=[bCs])
        for c in range(8):
            qk_piece(qT, bQ, c, G_Q)

        mark('p1')
        bO = bH[0:8]
        bS2 = bH[8:16]
        bank6, bank7 = psf[:, 3072:3584], psf[:, 3584:4096]
        for ri, (src, sb_) in enumerate(((qT, bQ), (kT, bK), (vT, bV))):
            def fn(e, src=src):
                ins = None
                for c in range(8):
                    ins = e.matmul(psf[0:1, 3072 + c * 128:3072 + (c + 1) * 128], lhsT=src[:, c, T:TC], rhs=identb[:],
                                   start=True, stop=True)
                return ins
            P.op("pe", fn, reads=list(sb_) + [bC], writes=[bBank[6], bBank[7]])
            P.op("act", (lambda ri: lambda e: e.activation(out=ROWS[ri], in_=psf[0:1, 3072:4096], func=AF.Copy))(ri),
                 reads=[bBank[6], bBank[7]], writes=bS2 + [bTC, bTU, bVr])
        P.dma("sp", (lambda l: lambda e: e.dma_start(out=kso[l, 2047:2048, :], in_=KROW))(l), reads=[bTU])
        P.dma("sp", (lambda l: lambda e: e.dma_start(out=vso[l, 2047:2048, :], in_=VROW))(l), reads=[bVr])

        def fn(e):
            e.matmul(bank6, lhsT=onesf[0:1, :], rhs=QROW[:, 0:512], start=True, stop=True)
            return e.matmul(bank7, lhsT=onesf[0:1, :], rhs=QROW[:, 512:1024], start=True, stop=True)
        P.op("pe", fn, reads=bS2 + [bC, bTC], writes=[bBank[6], bBank[7]])
        P.op("act", lambda e: e.activation(out=qb[:], in_=psf[:, 3072:4096], func=AF.Copy), reads=[bBank[6], bBank[7]], writes=bS2)
        lg = sm1[:, 0:6, :].rearrange("p a b -> p (a b)")
        pe_ = tB[:, 0:48]
        for br, d in enumerate((1, 4, 16)):
            P.dma("sp", (lambda l, d: lambda e: e.dma_start(out=Kt[:], in_=kst[l, 2048 - 128 * d:2048:d, :]))(l, d), writes=bS2)
            P.op("dve", lambda e: e.tensor_tensor(out=Kt[:], in0=Kt[:], in1=qb[:], op=ALU.mult), reads=bS2, writes=bS2)
            P.op("dve", (lambda br: lambda e: e.tensor_reduce(out=lg[:, br * 16:(br + 1) * 16],
                                                              in_=Kt[:].rearrange("p (h e) -> p h e", e=64), axis=AX.X, op=ALU.add))(br),
                 reads=bS2, writes=[bSm])
        P.op("dve", lambda e: e.scalar_tensor_tensor(out=lg, in0=lg, scalar=0.125, in1=sbS[:], op0=ALU.mult, op1=ALU.add),
             reads=[bC, bSm], writes=[bSm])
        P.op("act", lambda e: e.activation(out=pe_, in_=lg, func=AF.Exp), reads=[bSm], writes=[bTB])
        for br, d in enumerate((1, 4, 16)):
            P.dma("sp", (lambda l, d: lambda e: e.dma_start(out=Vt2[:], in_=vst[l, 2048 - 128 * d:2048:d, :]))(l, d), writes=bS2)
            P.op("dve", (lambda br: lambda e: e.tensor_tensor(
                out=Vt2[:].rearrange("p (h e) -> p h e", e=64), in0=Vt2[:].rearrange("p (h e) -> p h e", e=64),
                in1=pe_[:, br * 16:(br + 1) * 16].unsqueeze(2).to_broadcast([128, 16, 64]), op=ALU.mult))(br),
                reads=bS2 + [bTB], writes=bS2)

            def fn(e, br=br):
                e.matmul(psf[0:1, 3072:3584], lhsT=onesf[:, 0:1], rhs=Vt2[:, 0:512], start=(br == 0), stop=(br == 2))
                e.matmul(psf[0:1, 3584:4096], lhsT=onesf[:, 0:1], rhs=Vt2[:, 512:1024], start=(br == 0), stop=(br == 2))
                return e.matmul(psf[0:1, 2048:2064], lhsT=onesf[:, 0:1], rhs=pe_[:, br * 16:(br + 1) * 16], start=(br == 0), stop=(br == 2))
            P.op("pe", fn, reads=bS2 + [bTB, bC], writes=[bBank[6], bBank[7], bBank[4]])
        t1 = tA[0:1, 0:T]
        l0 = sm1[0:1, 6, 0:8]
        l0 = sm1[0:1, 6:8, :].rearrange("p a b -> p (a b)")
        P.op("dve", lambda e: e.tensor_tensor(out=t1, in0=QROW, in1=KROW, op=ALU.mult), reads=bS2 + [bTC, bTU], writes=[bTA])
        P.op("dve", lambda e: e.tensor_reduce(out=l0, in_=t1.rearrange("p (h e) -> p h e", e=64), axis=AX.X, op=ALU.add),
             reads=[bTA], writes=[bSm])
        P.op("dve", lambda e: e.scalar_tensor_tensor(out=l0, in0=l0, scalar=0.125, in1=sb0S[:], op0=ALU.mult, op1=ALU.add),
             reads=[bC, bSm], writes=[bSm])
        P.op("act", lambda e: e.activation(out=l0, in_=l0, func=AF.Exp), reads=[bSm], writes=[bSm])
        P.op("dve", lambda e: e.tensor_scalar(out=l0, in0=l0, scalar1=3.0, scalar2=None, op0=ALU.mult), reads=[bSm], writes=[bSm])
        P.op("dve", lambda e: e.tensor_tensor(out=t1.rearrange("p (h e) -> p h e", e=64),
                                              in0=VROW.rearrange("p (h e) -> p h e", e=64),
                                              in1=l0.unsqueeze(2).to_broadcast([1, 16, 64]), op=ALU.mult),
             reads=bS2 + [bVr, bSm], writes=[bTA])
        P.op("dve", lambda e: e.tensor_tensor(out=t1, in0=psf[0:1, 3072:4096], in1=t1, op=ALU.add),
             reads=[bBank[6], bBank[7], bTA], writes=[bTA])
        P.op("dve", lambda e: e.tensor_tensor(out=l0, in0=psf[0:1, 2048:2064], in1=l0, op=ALU.add), reads=[bBank[4], bSm], writes=[bSm])
        P.op("dve", lambda e: e.reciprocal(out=l0, in_=l0), reads=[bSm], writes=[bSm])
        P.op("dve", lambda e: e.tensor_tensor(out=QROW.rearrange("p (h e) -> p h e", e=64),
                                              in0=t1.rearrange("p (h e) -> p h e", e=64),
                                              in1=l0.unsqueeze(2).to_broadcast([1, 16, 64]), op=ALU.mult),
             reads=[bTA, bSm], writes=bS2 + [bTC])

        def fn(e):
            ins = None
            for c in range(8):
                ins = e.matmul(psf[:, 3072 + 2 * c:3074 + 2 * c], lhsT=QROW[:, c * 128:(c + 1) * 128], rhs=onesf[0:1, 0:2],
                               start=True, stop=True)
            return ins
        P.op("pe", fn, reads=bS2 + [bC, bTC], writes=[bBank[6], bBank[7]])
        bOs = bf("osample")
        P.op("act", lambda e: e.activation(out=oT[:, :, T:TC], in_=psf[:, 3072:3088].rearrange("p (c two) -> p c two", two=2)[:, :, 0:1], func=AF.Copy),
             reads=[bBank[6], bBank[7]], writes=[bOs])

        mark('samp')
        Sb = [(psf[:, k * 512:(k + 1) * 512], [bBank[k]]) for k in range(4)]
        Oaccs = [(psf[:, 2048:3072], [bBank[4], bBank[5]]), (psf[:, 3072:4096], [bBank[6], bBank[7]])]
        bE4 = [bf("E4_%d" % i) for i in range(4)]
        bP4 = [bf("P4_%d" % i) for i in range(4)]
        for b_ in bE4 + bP4:
            b_.w = sp_tk
            b_.r = []
        EbS = [Eb[k // 2][:, (k % 2) * 512:(k % 2) * 512 + 512] for k in range(4)]
        PbS = [Pb[k // 2][:, (k % 2) * 512:(k % 2) * 512 + 512] for k in range(4)]
        tbank = (psf[:, 0:512], psf[:, 512:1024])

        def build_v(c, pp):
            P.dma("sp", (lambda l, c, pp: lambda e: e.dma_start(out=kTp[pp][:], in_=bouts[l][c].ap()[0:128, :]))(l, c, pp),
                  reads=[bBout[c]], writes=bKp[pp])
            P.dma("sp", (lambda l, c, pp: lambda e: e.dma_start(out=vTp[pp][:], in_=bouts[l][c].ap()[128:256, :]))(l, c, pp),
                  reads=[bBout[c]], writes=bVTp[pp])
            for hh in range(2):
                ti = pp * 2 + hh
                P.dma("sp", (lambda c, hh: lambda e: e.dma_start(out=bst[:], in_=btoe[2 * c + hh]))(c, hh), writes=[bBst])
                P.op("act", (lambda ti: lambda e: e.activation(out=Th[ti][:], in_=bst[:], func=AF.Exp))(ti), reads=[bBst], writes=[bTh[ti]])
                P.op("dve", (lambda ti: lambda e: e.tensor_tensor(out=Th[ti][:], in0=Th[ti][:], in1=mtS[:], op=ALU.mult))(ti),
                     reads=[bTh[ti], bC], writes=[bTh[ti]])
            for (src, srcb, dstl, dstb, prev) in ((vT[:, c, :], [bV[c]], Vown[pp], bVo[pp], False),
                                                  (vTp[pp], bVTp[pp], Vprv[pp], bVp[pp], True)):
                for half in range(2):
                    bank = tbank[half]
                    bb = [bBank[half]]

                    def fn(e, src=src, half=half, bank=bank):
                        ins = None
                        for j in range(4):
                            jj = half * 4 + j
                            ins = e.matmul(bank[:, j * 128:(j + 1) * 128], lhsT=src[:, jj * 128:(jj + 1) * 128], rhs=identb[:],
                                           start=True, stop=True)
                        return ins
                    P.op("pe", fn, reads=list(srcb) + [bC], writes=bb)
                    bv = bank.rearrange("p (j f) -> p j f", f=128)
                    for (dc, sc_) in ((0, 0), (128, 64)):
                        if prev:
                            P.op("dve", (lambda dstl, half, dc, sc_, bv: lambda e: e.tensor_scalar(
                                out=dstl[:, half * 4:half * 4 + 4, dc:dc + 64], in0=bv[:, :, sc_:sc_ + 64], scalar1=pv(FLAG), scalar2=None,
                                op0=ALU.mult))(dstl, half, dc, sc_, bv), reads=bb + [bC], writes=dstb)
                        else:
                            P.op("act", (lambda dstl, half, dc, sc_, bv: lambda e: e.activation(
                                out=dstl[:, half * 4:half * 4 + 4, dc:dc + 64], in_=bv[:, :, sc_:sc_ + 64], func=AF.Copy))(dstl, half, dc, sc_, bv),
                                reads=bb, writes=dstb)

        def attn_head(c, pp, hh):
            po = 64 * hh
            ti = pp * 2 + hh
            Oacc = Oaccs[hh]
            steps = []
            for jp in range(8):
                for h in range(2):
                    steps.append((True, jp, 512 * h, 512 * h + 512, 1024 - 128 * jp + 512 * h))
            for j in range(8):
                for h in range(2):
                    a_, b_ = max(128 * j, 512 * h), 512 * h + 512
                    if a_ < b_:
                        steps.append((False, j, a_, b_, a_ - 128 * j))
            n = len(steps)
            firsts = {}
            lasts = {}
            for si, st in enumerate(steps):
                bk = st[2] // 512
                firsts.setdefault(bk, si)
                lasts[bk] = si

            def S_op(si):
                prev, j, a_, b_, ts = steps[si]
                sap, sbk = Sb[si % 4]
                if prev:
                    ksrc, kb = kTp[pp][po:po + 64, j * 128:(j + 1) * 128], bKp[pp]
                else:
                    ksrc, kb = kT[po:po + 64, c, j * 128:(j + 1) * 128], [bK[c]]
                P.op("pe", lambda e: e.matmul(sap[:, 0:b_ - a_], lhsT=ksrc, rhs=qT[po:po + 64, c, a_:b_], start=True, stop=True),
                     reads=list(kb) + [bQ[c]], writes=sbk, lhs=list(kb))

            def EP_op(si):
                prev, j, a_, b_, ts = steps[si]
                sap, sbk = Sb[si % 4]
                w_ = b_ - a_
                k = si % 4
                P.op("act", lambda e: e.activation(out=EbS[k][:, 0:w_], in_=sap[:, 0:w_], func=AF.Exp, scale=0.125),
                     reads=sbk, writes=[bE4[k]])
                P.op("dve", lambda e: e.tensor_tensor(out=PbS[k][:, 0:w_], in0=EbS[k][:, 0:w_], in1=Th[ti][:, ts:ts + w_], op=ALU.mult),
                     reads=[bE4[k], bTh[ti]], writes=[bP4[k]])

            def PV_op(si):
                prev, j, a_, b_, ts = steps[si]
                vt = (Vprv if prev else Vown)[pp]
                vb = (bVp if prev else bVo)[pp]
                lhs = vt[:, j, 0:128] if hh == 0 else vt[:, j, 64:192]
                bk = a_ // 512
                k = si % 4
                P.op("pe", lambda e: e.matmul(Oacc[0][:, a_:b_], lhsT=lhs, rhs=PbS[k][:, 0:b_ - a_],
                                              start=(firsts[bk] == si), stop=(lasts[bk] == si)),
                     reads=list(vb) + [bP4[k]], writes=Oacc[1], lhs=list(vb))

            for si in range(min(3, n)):
                S_op(si)
            for si in range(n):
                EP_op(si)
                PV_op(si)
                if si + 3 < n:
                    S_op(si + 3)
            dpo = 64 - po
            P.op("act", lambda e: e.activation(out=rden[po:po + 64, :], in_=Oacc[0][dpo:dpo + 64, :], func=AF.Ln), reads=Oacc[1], writes=[bRd])
            P.op("act", lambda e: e.activation(out=rden[po:po + 64, :], in_=rden[po:po + 64, :], func=AF.Exp, scale=-1.0), reads=[bRd], writes=[bRd])
            P.op("dve", lambda e: e.tensor_tensor(out=oT[po:po + 64, c, 0:T], in0=Oacc[0][po:po + 64, :], in1=rden[po:po + 64, :],
                                                  op=ALU.mult), reads=Oacc[1] + [bRd], writes=[bO[c]])

        build_v(0, 0)
        for c in range(8):
            if c + 1 < 8:
                build_v(c + 1, (c + 1) % 2)
            for hh in range(2):
                attn_head(c, c % 2, hh)

        mark('attn')
        bPt = bf("ptl")
        P.dma("sp", (lambda l: lambda e: e.dma_start(out=ptl[:], in_=bout2s[l].ap()[0:128, :].rearrange("p (c r) -> p c r", r=2)))(l),
              reads=[bB2o], writes=[bPt])
        P.op("dve", lambda e: e.tensor_scalar(out=ptl[:], in0=ptl[:], scalar1=pv(FLAG), scalar2=None, op0=ALU.mult),
             reads=[bPt, bC], writes=[bPt])
        W0, W1 = parS[:, G_CW:G_CW + 8], parS[:, G_CW + 8:G_CW + 16]
        f0, f1, f2 = sm1[:, 0, :], sm1[:, 1, :], sm1[:, 2, :]
        bF = bSm
        P.op("dve", lambda e: e.tensor_tensor(out=f0, in0=ptl[:, :, 0], in1=W0, op=ALU.mult), reads=[bPt, bC], writes=[bF])
        P.op("dve", lambda e: e.tensor_tensor(out=f1, in0=ptl[:, :, 1], in1=W1, op=ALU.mult), reads=[bPt, bC, bF], writes=[bF])
        P.op("dve", lambda e: e.tensor_tensor(out=f0, in0=f0, in1=f1, op=ALU.add), reads=[bF], writes=[bF])
        P.op("dve", lambda e: e.tensor_tensor(out=f0, in0=f0, in1=y02[:, :, 0], op=ALU.add), reads=[bF, bY02], writes=[bF])
        P.op("dve", lambda e: e.tensor_tensor(out=zT[:, :, 0], in0=f0, in1=gb02[:, :, 0], op=ALU.mult), reads=[bF, bG02], writes=bZ)
        P.op("dve", lambda e: e.tensor_tensor(out=f2, in0=ptl[:, :, 1], in1=W0, op=ALU.mult), reads=[bPt, bC, bF], writes=[bF])
        P.op("dve", lambda e: e.tensor_tensor(out=f2, in0=f2, in1=y02[:, :, 1], op=ALU.add), reads=[bF, bY02], writes=[bF])
        P.op("dve", lambda e: e.tensor_tensor(out=zT[:, :, 1], in0=f2, in1=gb02[:, :, 1], op=ALU.mult), reads=[bF, bG02], writes=bZ)
        for (src, sbuf_, acc, rt, rb, gbase) in ((oT, bO, accs[0], tA, bTA, G_GA), (zT, bZ, accs[1], tB, bTB, G_GC)):
            for c in range(8):
                P.op("act", (lambda c, src: lambda e: e.activation(out=sqs[:, c, :], in_=src[:, c, :], func=AF.Square))(c, src),
                     reads=[sbuf_[c], bOs], writes=[bS2[c]])
            stat_mm(onesb, [sqs[:, c, :] for c in range(8)], bS2, acc)
            rstd_from(acc, rt, rb, 1.0 / 1024)
            for c in range(8):
                P.op("dve", (lambda c, src, rt, gbase: lambda e: e.scalar_tensor_tensor(
                    out=src[:, c, :], in0=src[:, c, :], scalar=pv(gbase + c), in1=rt[:, 0:TC], op0=ALU.mult, op1=ALU.mult))(c, src, rt, gbase),
                    reads=[sbuf_[c], rb, bC, bOs], writes=[sbuf_[c]])
        allx = XAL
        P.dma("sp", lambda e: e.dma_start(out=xT[:], in_=xsp), reads=[], writes=bX + allx)

        mark('p25')
        osrc = [oT[:, c, :] for c in range(8)] + [zT[:, c, :] for c in range(8)]
        for m in range(16):
            slot = next_piece(); acc = nextacc()
            mm_piece(slot, osrc, list(bO) + list(bZ), acc)
            P.op("dve", (lambda m, acc: lambda e: e.tensor_tensor(out=xT[:, m, :], in0=acc[0], in1=xT[:, m, :], op=ALU.add))(m, acc),
                 reads=acc[1] + [bX[m]], writes=[bX[m]])

        mark('p3')
        rmsnorm_x(G_MLP)

        def up_block(b):
            hb = b % 2
            for j in range(8):
                slot = next_piece(); acc = nextacc()
                mm_piece(slot, hsrc, bH, acc)
                P.op("act", (lambda acc: lambda e: e.activation(out=tCc[:, 0:TC], in_=acc[0], func=AF.Relu))(acc),
                     reads=acc[1], writes=[bTC])
                P.op("dve", (lambda hb, j: lambda e: e.tensor_tensor(out=hid[hb][:, j, :], in0=tCc[:, 0:TC], in1=tCc[:, 0:TC], op=ALU.mult))(hb, j),
                     reads=[bTC], writes=[bHid[hb][j], bQ[j] if hb == 0 else bK[j]])

        def down_block(b):
            hb = b % 2
            hs = [hid[hb][:, j, :] for j in range(8)]
            for g in range(8):
                slot = next_piece()
                for mm_ in range(2):
                    acc = nextacc()
                    m = 2 * g + mm_
                    mm_piece(slot, hs, bHid[hb] + (bQ if hb == 0 else bK), acc, nk=8, wcol=(lambda mm_: lambda kc: (kc * 256 + mm_ * 128, kc * 256 + mm_ * 128 + 128))(mm_))
                    P.op("dve", (lambda m, acc: lambda e: e.tensor_tensor(out=xT[:, m, :], in0=acc[0], in1=xT[:, m, :], op=ALU.add))(m, acc),
                         reads=acc[1] + [bX[m]], writes=[bX[m]])

        up_block(0)
        for b in range(8):
            if b + 1 < 8:
                up_block(b + 1)
            down_block(b)

    try:
        for l in range(nl):
            layer(l)
    except _Stop:
        pass
    names = dict(xT=(xT, bX), hT=(hT, bH), qT=(qT, bQ), kT=(kT, bK), vT=(vT, bV), zT=(zT, bZ), oT=(oT, bH[0:8] + [bf('osample')]),
                 tA=(tA, [bTA]), tB=(tB, [bTB]), tC=(tCc, [bTC]), tU=(tU, [bTU]))
    for dn in dumps:
        t_, bb_ = names[dn]
        shp = list(t_.shape)
        do = dt("dbg_" + dn, shp, F32, kind="ExternalOutput").ap()
        P.dma("pool", (lambda t_, do: lambda e: e.dma_start(out=do, in_=t_[:]))(t_, do), reads=bb_)
    P.dma("sp", lambda e: e.dma_start(out=yout, in_=xT[:]), reads=bX)

    with nc.Block() as block:
        @block.tensor
        def _(e):
            P.emit("pe", e)

        @block.scalar
        def _(e):
            P.emit("act", e)

        @block.vector
        def _(e):
            P.emit("dve", e)

        @block.gpsimd
        def _(e):
            P.emit("pool", e)

        @block.sync
        def _(e):
            P.emit("sp", e)
    return nc


def _weight_stream(w_in, w_out, w_up, w_down):
    out = np.empty((L * PPL, 128, 2048), np.float32)
    i = 0

    def colpiece(W, c0):
        return W[:, c0:c0 + 128].reshape(16, 128, 128).transpose(1, 0, 2).reshape(128, 2048)

    for l in range(L):
        wi = w_in[l]
        order = [1024 + 128 * c for c in range(8)] + [2048 + 128 * c for c in range(8)]
        for c in range(8):
            order += [3072 + 128 * c, 5120 + 128 * c, 4096 + 128 * c]
        order += [128 * c for c in range(8)]
        for c0 in order:
            out[i] = colpiece(wi, c0); i += 1
        for m in range(16):
            out[i] = colpiece(w_out[l], 128 * m); i += 1

        def up(b):
            nonlocal i
            for j in range(8):
                out[i] = colpiece(w_up[l], (8 * b + j) * 128); i += 1

        def down(b):
            nonlocal i
            blk = w_down[l][b * 1024:(b + 1) * 1024]
            for g in range(8):
                out[i] = blk[:, g * 256:(g + 1) * 256].reshape(8, 128, 256).transpose(1, 0, 2).reshape(128, 2048); i += 1
        up(0)
        for b in range(8):
            if b + 1 < 8:
                up(b + 1)
            down(b)
    assert i == L * PPL
    return out


_NC_CACHE = {}
_PREP_ONLY = False


def kernel(x_prompt, x_sample, state_attn_k, state_attn_v, state_conv, rel_bias, norm_mix, w_in, q_norm, k_norm,
           conv_w, attn_out_norm, conv_out_norm, w_out, norm_mlp, w_up, w_down):
    f = lambda a: np.ascontiguousarray(np.asarray(a, dtype=np.float32))
    x_prompt, x_sample, state_attn_k, state_attn_v, state_conv = map(f, (x_prompt, x_sample, state_attn_k, state_attn_v, state_conv))
    rel_bias, norm_mix, q_norm, k_norm, conv_w, attn_out_norm, conv_out_norm, norm_mlp = map(
        f, (rel_bias, norm_mix, q_norm, k_norm, conv_w, attn_out_norm, conv_out_norm, norm_mlp))
    wst = _weight_stream(f(w_in), f(w_out), f(w_up), f(w_down))
    kk = np.arange(128)[:, None]
    cc = np.arange(TW)[None, :]
    dist = cc - kk
    valid = (dist >= 0) & (dist <= 2048)
    dc = np.clip(dist, 0, 2048)
    bidx = _bucket(dc)
    btoe = np.ascontiguousarray(rel_bias[bidx].transpose(2, 0, 1))
    mult = ((dc <= 128).astype(np.float32) + ((dc % 4 == 0) & (dc <= 512)) + ((dc % 16 == 0) & (dc <= 2048))) * valid
    mtoe = mult.astype(np.float32)
    sbias = np.zeros((128, 48), np.float32)
    for br, d in enumerate((1, 4, 16)):
        j = 128 - np.arange(128)
        sbias[:, br * 16:(br + 1) * 16] = rel_bias[_bucket(j * d)]
    sb0 = np.ascontiguousarray(rel_bias[0:1, :])
    cst = np.zeros((128, 4, 128), np.float32)
    cst[:, 0] = np.eye(128)
    cst[:, 1] = 1.0
    cst[0:64, 2, 0:64] = 1.0
    cst[64:128, 2, 64:128] = 1.0
    cst[:, 3] = np.eye(128)
    in_maps = []
    for core in range(8):
        b, hf = core // 2, core % 2
        xin = np.empty((128, 16, TC), np.float32)
        xin[:, :, 0:T] = x_prompt[b, hf * T:(hf + 1) * T].reshape(T, 16, 128).transpose(2, 1, 0)
        xin[:, :, T] = x_sample[core, 0].reshape(16, 128).T
        par = np.zeros((128, 304), np.float32)
        for l in range(L):
            pb = l * NPARL
            par[:, pb:pb + 16] = norm_mix[l].reshape(16, 128).T
            par[:, pb + 16:pb + 32] = norm_mlp[l].reshape(16, 128).T
            par[:, pb + 32] = np.tile(q_norm[l], 2)
            par[:, pb + 33] = np.tile(k_norm[l], 2)
            for i in range(3):
                par[:, pb + 34 + 8 * i:pb + 42 + 8 * i] = conv_w[l, i].reshape(8, 128).T
            par[:, pb + 58:pb + 66] = attn_out_norm[l].reshape(8, 128).T
            par[:, pb + 66:pb + 74] = conv_out_norm[l].reshape(8, 128).T
        par[:, 296] = float(hf)
        sconv = np.ascontiguousarray(state_conv[:, core].reshape(L, 2, 8, 128).transpose(3, 0, 2, 1))
        in_maps.append({
            "xin": xin, "wst": wst, "par": par, "sconv": sconv, "btoe": btoe, "mtoe": mtoe, "sbias": sbias, "sb0": sb0,
            "cst": cst, "kst": np.ascontiguousarray(state_attn_k[:, core].reshape(L, 2048, 1024)),
            "vst": np.ascontiguousarray(state_attn_v[:, core].reshape(L, 2048, 1024)),
        })
    if _PREP_ONLY:
        return in_maps
    if "nc" not in _NC_CACHE:
        _NC_CACHE["nc"] = build()
    res = run_bass_kernel_spmd(_NC_CACHE["nc"], in_maps, core_ids=list(range(8))).results
    y_p = np.empty((4, 2048, 2048), np.float32)
    y_s = np.empty((8, 1, 2048), np.float32)
    nk = np.empty((L, 4, 2048, 16, 64), np.float32)
    nv = np.empty((L, 4, 2048, 16, 64), np.float32)
    ncp = np.empty((L, 4, 2, 1024), np.float32)
    nks = np.empty((L, 8, 2048, 16, 64), np.float32)
    nvs = np.empty((L, 8, 2048, 16, 64), np.float32)
    ncs = np.empty((L, 8, 2, 1024), np.float32)
    for core in range(8):
        b, hf = core // 2, core % 2
        r = res[core]
        yo = r["yout"]
        y_p[b, hf * T:(hf + 1) * T] = yo[:, :, 0:T].transpose(2, 1, 0).reshape(T, 2048)
        y_s[core, 0] = yo[:, :, T].T.reshape(2048)
        nk[:, b, hf * T:(hf + 1) * T] = r["kTo"].transpose(0, 3, 2, 1).reshape(L, T, 16, 64)
        nv[:, b, hf * T:(hf + 1) * T] = r["vTo"].transpose(0, 3, 2, 1).reshape(L, T, 16, 64)
        if hf == 1:
            ncp[:, b] = r["cpo"].transpose(0, 3, 2, 1).reshape(L, 2, 1024)
        nks[:, core] = r["kso"].reshape(L, 2048, 16, 64)
        nvs[:, core] = r["vso"].reshape(L, 2048, 16, 64)
        ncs[:, core] = r["cso"].transpose(0, 3, 2, 1).reshape(L, 2, 1024)
    return (y_p, y_s, nk, nv, ncp, nks, nvs, ncs)
```
